# Optimizing a Trainium2 kernel written in Bass

```python
import math
import jax, jax.numpy as jnp
from jax import lax
import numpy as np

D_MODEL = 1024
BATCH = 2
SEQ = 8192
DEPTH = 2

N_META = 16
N_A_LAYERS = DEPTH // 2
N_B_LAYERS = DEPTH - N_A_LAYERS
POOL_WINDOWS = (2, 4, 8, 16)
N_POOL_GROUPS = len(POOL_WINDOWS)
POOL_GROUP_DIM = D_MODEL // N_POOL_GROUPS
HEAD_DIM = 64
N_Q_HEADS = D_MODEL // HEAD_DIM
N_KV_HEADS = max(1, N_Q_HEADS // 8)
GROUP = N_Q_HEADS // N_KV_HEADS
ATTN_DIM = N_Q_HEADS * HEAD_DIM
KV_DIM = N_KV_HEADS * HEAD_DIM
WINDOW = 128
BLOCK = 128
N_BUCKETS = 32
MAX_DISTANCE = 128
D_FF = (((8 * D_MODEL + 2) // 3 + 255) // 256) * 256
EPS = 1e-6
PAD_FRONT = (-N_META) % BLOCK

kernel_name = "yoco_pool_swa_sink_hybrid"


def rms_norm(x, g):
    xf = x.astype(jnp.float32)
    y = xf * lax.rsqrt(jnp.mean(xf * xf, axis=-1, keepdims=True) + EPS)
    return (y * g.astype(jnp.float32)).astype(x.dtype)


def swiglu(h, w_gate_up, w_down):
    gu = h @ w_gate_up
    gate, up = jnp.split(gu, 2, axis=-1)
    return (jax.nn.silu(gate) * up) @ w_down


def pool_mixer(h, w_groups, scale):
    B, L, _ = h.shape
    hg = h.astype(jnp.float32).reshape(B, L, N_POOL_GROUPS, POOL_GROUP_DIM)
    cs = jnp.cumsum(hg, axis=1)
    pos = jnp.arange(1, L + 1, dtype=jnp.float32)
    outs = []
    for gi, w in enumerate(POOL_WINDOWS):
        c = cs[:, :, gi]
        lag = jnp.pad(c, ((0, 0), (w, 0), (0, 0)))[:, :L]
        cnt = jnp.minimum(pos, float(w))[None, :, None]
        outs.append((c - lag) / cnt - hg[:, :, gi])
    p = jnp.stack(outs, axis=2).astype(h.dtype)
    y = jnp.einsum('blgc,gcd->blgd', p, w_groups).reshape(B, L, D_MODEL)
    return y * scale


def t5_bucket(d):
    max_exact = N_BUCKETS // 2
    df = jnp.maximum(d, 1).astype(jnp.float32)
    large = max_exact + (jnp.log(df / max_exact) / math.log(MAX_DISTANCE / max_exact)
                         * (N_BUCKETS - max_exact)).astype(jnp.int32)
    large = jnp.minimum(large, N_BUCKETS - 1)
    return jnp.where(d < max_exact, d, large)


def band_bias_and_mask(rel_bias, n_blocks):
    q = jnp.arange(BLOCK)[:, None]
    s = jnp.arange(2 * BLOCK)[None, :]
    d = q + BLOCK - s
    in_window = (d >= 0) & (d < WINDOW)
    bias = rel_bias[t5_bucket(jnp.maximum(d, 0))]
    bias = jnp.transpose(bias, (2, 0, 1)).astype(jnp.float32)
    key_pos = jnp.arange(n_blocks)[:, None] * BLOCK + s - BLOCK
    valid = in_window[None] & (key_pos >= PAD_FRONT)[:, None, :]
    return bias, valid


def to_blocks_with_prev(t):
    B, L = t.shape[:2]
    nb = (L + PAD_FRONT) // BLOCK
    t = jnp.pad(t, ((0, 0), (PAD_FRONT + BLOCK, 0), (0, 0), (0, 0)))
    t = t.reshape(B, nb + 1, BLOCK, t.shape[2], t.shape[3])
    return jnp.concatenate([t[:, :-1], t[:, 1:]], axis=2)


def sliding_window_attention(h, k_blk, v_blk, w_q, b_q, w_o, b_o, sinks, bias, valid):
    B, L, _ = h.shape
    q = (h @ w_q + b_q).reshape(B, L, N_KV_HEADS, GROUP, HEAD_DIM)
    q = jnp.pad(q, ((0, 0), (PAD_FRONT, 0), (0, 0), (0, 0), (0, 0)))
    nb = q.shape[1] // BLOCK
    q = q.reshape(B, nb, BLOCK, N_KV_HEADS, GROUP, HEAD_DIM)
    logits = jnp.einsum('bnqhgd,bnshd->bhgnqs', q, k_blk,
                        preferred_element_type=jnp.float32) * (HEAD_DIM ** -0.5)
    logits = logits + bias.reshape(N_KV_HEADS, GROUP, 1, BLOCK, 2 * BLOCK)
    logits = jnp.where(valid, logits, -jnp.inf)
    sink = sinks.astype(jnp.float32).reshape(N_KV_HEADS, GROUP, 1, 1, 1)
    m = jnp.maximum(logits.max(axis=-1, keepdims=True), sink)
    p = jnp.exp(logits - m)
    denom = p.sum(axis=-1, keepdims=True) + jnp.exp(sink - m)
    probs = (p / denom).astype(v_blk.dtype)
    o = jnp.einsum('bhgnqs,bnshd->bnqhgd', probs, v_blk)
    o = o.reshape(B, nb * BLOCK, ATTN_DIM)[:, PAD_FRONT:]
    return o @ w_o + b_o


def setup_inputs(seed: int = 0) -> dict:
    key = jax.random.key(seed)
    ks = jax.random.split(key, 24)
    f32 = jnp.float32

    def nrm(k, shape, scale):
        return jax.random.normal(k, shape, f32) * scale

    def gain(k, shape):
        return jnp.ones(shape, f32) + 0.05 * jax.random.normal(k, shape, f32)

    return {
        "x": nrm(ks[0], (BATCH, SEQ, D_MODEL), 1.0),
        "meta_tokens": nrm(ks[1], (N_META, D_MODEL), 1.0),
        "norm_mix_pre": gain(ks[2], (DEPTH, D_MODEL)),
        "norm_mix_post": gain(ks[3], (DEPTH, D_MODEL)),
        "norm_ffn_pre": gain(ks[4], (DEPTH, D_MODEL)),
        "norm_ffn_post": gain(ks[5], (DEPTH, D_MODEL)),
        "pool_w": nrm(ks[6], (N_A_LAYERS, N_POOL_GROUPS, POOL_GROUP_DIM, POOL_GROUP_DIM), POOL_GROUP_DIM ** -0.5),
        "pool_scale": gain(ks[7], (N_A_LAYERS, D_MODEL)) + 0.05 * jax.random.normal(ks[8], (N_A_LAYERS, D_MODEL), f32),
        "kv_norm": gain(ks[9], (D_MODEL,)),
        "w_k": nrm(ks[10], (D_MODEL, KV_DIM), D_MODEL ** -0.5),
        "b_k": nrm(ks[11], (KV_DIM,), 0.02),
        "w_v": nrm(ks[12], (D_MODEL, KV_DIM), D_MODEL ** -0.5),
        "b_v": nrm(ks[13], (KV_DIM,), 0.02),
        "w_q": nrm(ks[14], (N_B_LAYERS, D_MODEL, ATTN_DIM), D_MODEL ** -0.5),
        "b_q": nrm(ks[15], (N_B_LAYERS, ATTN_DIM), 0.02),
        "w_o": nrm(ks[16], (N_B_LAYERS, ATTN_DIM, D_MODEL), ATTN_DIM ** -0.5),
        "b_o": nrm(ks[17], (N_B_LAYERS, D_MODEL), 0.02),
        "sinks": nrm(ks[18], (N_B_LAYERS, N_Q_HEADS), 0.5),
        "rel_bias": nrm(ks[19], (N_BUCKETS, N_Q_HEADS), 0.3),
        "w_gate_up": nrm(ks[20], (DEPTH, D_MODEL, 2 * D_FF), D_MODEL ** -0.5),
        "w_down": nrm(ks[21], (DEPTH, D_FF, D_MODEL), D_FF ** -0.5),
    }


def reference(x, meta_tokens, norm_mix_pre, norm_mix_post, norm_ffn_pre, norm_ffn_post,
              pool_w, pool_scale, kv_norm, w_k, b_k, w_v, b_v, w_q, b_q, w_o, b_o,
              sinks, rel_bias, w_gate_up, w_down):
    B = x.shape[0]
    meta = jnp.broadcast_to(meta_tokens[None].astype(x.dtype), (B, N_META, D_MODEL))
    hs = jnp.concatenate([meta, x], axis=1)
    L = hs.shape[1]
    n_blocks = (L + PAD_FRONT) // BLOCK
    bias, valid = band_bias_and_mask(rel_bias, n_blocks)
    k_blk = None
    v_blk = None
    for layer in range(DEPTH):
        h = rms_norm(hs, norm_mix_pre[layer])
        if layer < N_A_LAYERS:
            mix = pool_mixer(h, pool_w[layer], pool_scale[layer])
        else:
            j = layer - N_A_LAYERS
            mix = sliding_window_attention(h, k_blk, v_blk, w_q[j], b_q[j], w_o[j], b_o[j],
                                           sinks[j], bias, valid)
        hs = hs + rms_norm(mix, norm_mix_post[layer])
        h = rms_norm(hs, norm_ffn_pre[layer])
        hs = hs + rms_norm(swiglu(h, w_gate_up[layer], w_down[layer]), norm_ffn_post[layer])
        if layer == N_A_LAYERS - 1:
            kv_in = rms_norm(hs, kv_norm)
            k = (kv_in @ w_k + b_k).reshape(B, L, N_KV_HEADS, HEAD_DIM)
            v = (kv_in @ w_v + b_v).reshape(B, L, N_KV_HEADS, HEAD_DIM)
            k_blk = to_blocks_with_prev(k)
            v_blk = to_blocks_with_prev(v)
    return hs[:, N_META:]
```

```python
import os
import numpy as np
from contextlib import ExitStack
import concourse.bass as bass
import concourse.mybir as mybir
from concourse.bass_utils import run_bass_kernel_spmd

F32 = mybir.dt.float32
BF16 = mybir.dt.bfloat16
AF = mybir.ActivationFunctionType
ALU = mybir.AluOpType

D = 1024
DFF = 2816
NFC = DFF // 128
NXT = 18
NHS = 17
EPS = 1e-6
NSLOT = 4
NEG = -30000.0

CH_POOL = 0
CH_GU0 = 1
CH_WD0 = CH_GU0 + 22
CH_WK = CH_WD0 + 11
CH_WV = CH_WK + 1
CH_WQ = CH_WV + 1
CH_WO = CH_WQ + 4
CH_GU1 = CH_WO + 4
CH_WD1 = CH_GU1 + 22
NCHUNK = CH_WD1 + 11

(T_MIXPRE0, T_PSCALE, T_MIXPOST0, T_FFNPRE0, T_FFNPOST0, T_KV, T_MIXPRE1, T_MIXPOST1, T_BO,
 T_FFNPRE1, T_FFNPOST1, T_BV) = range(12)
NTAB = 12
NCST = 32


class Buf:
    def __init__(self, name):
        self.name = name
        self.w = None
        self.r = {}


class Sched:
    QS = ("pe", "act", "dve", "pool", "sp")

    def __init__(self, nc, es):
        self.nc = nc
        self.es = es
        self.ops = {k: [] for k in self.QS}
        self.sem = {}
        self.count = {k: 0 for k in self.QS}
        self.seen = {k: {} for k in self.QS}
        self.dcount = {}
        for q in self.QS:
            self.getsem(q)

    def getsem(self, name):
        if name not in self.sem:
            self.sem[name] = self.es.enter_context(self.nc.semaphore(name))
        return self.sem[name]

    def wait(self, q, tok):
        if tok is None:
            return
        sname, val = tok
        if self.seen[q].get(sname, 0) >= val:
            return
        self.seen[q][sname] = val
        sem = self.getsem(sname)
        self.ops[q].append(lambda e, sem=sem, val=val: e.wait_ge(sem, val))

    def _deps(self, q, rd, wr, extra):
        for t in extra:
            self.wait(q, t)
        for b in rd:
            self.wait(q, b.w)
        for b in wr:
            self.wait(q, b.w)
            for sn, v in b.r.items():
                self.wait(q, (sn, v))

    def _mark(self, tok, rd, wr):
        for b in rd:
            if b.r.get(tok[0], 0) < tok[1]:
                b.r[tok[0]] = tok[1]
        for b in wr:
            b.w = tok
            b.r = {}

    def op(self, q, fn, rd=(), wr=(), extra=()):
        self._deps(q, rd, wr, extra)
        fns = fn if isinstance(fn, (list, tuple)) else [fn]
        for f in fns[:-1]:
            self.ops[q].append(lambda e, f=f: f(e))
        self.count[q] += 1
        sem = self.sem[q]
        last = fns[-1]
        self.ops[q].append(lambda e, f=last, sem=sem: f(e).then_inc(sem, 1))
        tok = (q, self.count[q])
        self._mark(tok, rd, wr)
        return tok

    def dma(self, q, semname, pairs, rd=(), wr=(), extra=()):
        self._deps(q, rd, wr, extra)
        sem = self.getsem(semname)
        for out, in_ in pairs:
            self.ops[q].append(lambda e, out=out, in_=in_, sem=sem: e.dma_start(out=out, in_=in_).then_inc(sem, 16))
        self.dcount[semname] = self.dcount.get(semname, 0) + 16 * len(pairs)
        tok = (semname, self.dcount[semname])
        self._mark(tok, rd, wr)
        return tok


class _Stop(Exception):
    pass


def build_program(stop=99):
    nc = bass.Bass("TRN2", target_bir_lowering=False)
    xin = nc.dram_tensor("xin", [NXT * 128, D], F32, kind="ExternalInput").ap()
    wst = nc.dram_tensor("wst", [NCHUNK, 128, 2048], F32, kind="ExternalInput").ap()
    tabs = nc.dram_tensor("tabs", [NTAB, 128, D], F32, kind="ExternalInput").ap()
    cstd = nc.dram_tensor("cst", [128, NCST], F32, kind="ExternalInput").ap()
    bandd = nc.dram_tensor("bands", [128, 13 * 128], F32, kind="ExternalInput").ap()
    btabd = nc.dram_tensor("btab", [2, 128, 16 * 2 * 128], F32, kind="ExternalInput").ap()
    outd = nc.dram_tensor("out", [16 * 128, D], F32, kind="ExternalOutput").ap()
    dbgd = nc.dram_tensor("dbg", [NHS * 128, D], F32, kind="ExternalOutput").ap() if stop != 99 else None

    with ExitStack() as es:
        S = Sched(nc, es)

        def sb(name, shape, dt):
            return es.enter_context(nc.sbuf_tensor(name, shape, dt))

        hs = sb("hs", [128, NHS, D], F32)
        tab = sb("tab", [128, 3, D], F32)
        ring = sb("ring", [128, NSLOT, 2048], BF16)
        bands = sb("bandsb", [128, 13, 128], BF16)
        cst = sb("cstsb", [128, NCST], F32)
        st = sb("st", [128, 160], F32)
        junk = sb("junk", [128, D], BF16)
        PH_ELEMS = 52 * 1024
        phase = sb("phase", [128, PH_ELEMS], BF16)
        ps = es.enter_context(nc.psum_tensor("ps", [128, 8, 512], F32))

        ident = bands[:, 12, :]
        B_hs = [Buf("hs%d" % i) for i in range(NHS)]
        B_tab = [Buf("tab%d" % i) for i in range(3)]
        B_ring = [Buf("ring%d" % i) for i in range(NSLOT)]
        B_bands = Buf("bands")
        B_cst = Buf("cst")
        B_st = Buf("st")
        B_junk = Buf("junk")
        B_ps = [Buf("ps%d" % i) for i in range(8)]
        B_out = Buf("out")

        def psbf(b):
            return ps[:, b, :].bitcast(BF16).rearrange("p (k t) -> p k t", k=8)

        class Carver:
            def __init__(self):
                self.off = 0

            def take(self, dt, shape):
                n = int(np.prod(shape[1:]))
                nb = n if dt == BF16 else 2 * n
                nb = (nb + 15) // 16 * 16
                assert self.off + nb <= PH_ELEMS, ("phase overflow", self.off + nb)
                v = phase[:, self.off:self.off + nb]
                self.off += nb
                if dt == F32:
                    v = v.bitcast(F32)
                v = v[:, 0:n]
                if len(shape) == 3:
                    v = v.rearrange("p (a b) -> p a b", a=shape[1])
                elif len(shape) == 4:
                    v = v.rearrange("p (a b c) -> p a b c", a=shape[1], b=shape[2])
                return v

        def barrier():
            toks = [(q, S.count[q]) for q in ("pe", "act", "dve", "pool") if S.count[q] > 0]
            for q in ("pe", "act", "dve"):
                for t in toks:
                    if t[0] != q:
                        S.wait(q, t)
            return toks

        def load_tab(slot, idx, extra=()):
            return S.dma("sp", "d_tab%d" % slot, [(tab[:, slot, :], tabs[idx])], wr=[B_tab[slot]], extra=extra)

        ring_seq = []
        for l in range(2):
            if l == 1:
                for g_ in range(4):
                    for j_ in range(4):
                        ring_seq.append(CH_WQ + j_)
            for third in range(3):
                for c in range(NFC):
                    ring_seq.append((CH_GU0 if l == 0 else CH_GU1) + c)
        ring_state = {"issued": 0, "used": 0}

        def ring_issue():
            i = ring_state["issued"]
            if i >= len(ring_seq):
                return
            slot = i % NSLOT
            S.dma("pool", "d_ring%d" % slot, [(ring[:, slot, :], wst[ring_seq[i]])], wr=[B_ring[slot]])
            ring_state["issued"] += 1

        def ring_next():
            i = ring_state["used"]
            ring_state["used"] += 1
            return i % NSLOT

        def rstd_batch(n, col_ss, col_sd, col_r, bst=None):
            bst = bst or B_st
            S.op("act", lambda e: e.activation(out=st[:, col_sd:col_sd + n], in_=st[:, col_ss:col_ss + n],
                                                func=AF.Sqrt, bias=cst[:, 26:27], scale=1.0 / D),
                 rd=[B_cst], wr=[bst])
            S.op("dve", lambda e: e.reciprocal(out=st[:, col_r:col_r + n], in_=st[:, col_sd:col_sd + n]), wr=[bst])

        def sumsq(src, col, rd, bst=None):
            S.op("act", lambda e: e.activation(out=junk[:], in_=src, func=AF.Square, accum_out=st[:, col:col + 1]),
                 rd=rd, wr=[B_junk, bst or B_st])

        def checkpoint(n):
            if stop == n:
                bt_ = barrier()
                for i_ in range(NHS):
                    S.dma("sp", "d_dbg", [(dbgd[i_ * 128:(i_ + 1) * 128, :], hs[:, i_, :])], rd=[B_hs[i_]], wr=[B_out], extra=bt_)
                raise _Stop()

        try:
            B_pre = Buf("pre")
            B_ph = Buf("phase_generic")
            cv = Carver()
            hb_all = cv.take(BF16, [128, NXT, D])
            pre = cv.take(F32, [128, D])
            pTs = cv.take(BF16, [128, 2, 8, 128])
            ysb = cv.take(F32, [128, 2, D])
            t2b = cv.take(F32, [128, 2, D])
            poolw = cv.take(BF16, [128, 4, 2, 256])
            B_hb = [Buf("hb%d" % j) for j in range(NXT)]
            B_pTs = [Buf("pTs0"), Buf("pTs1")]
            B_ys = [Buf("ys0"), Buf("ys1")]
            B_t2 = Buf("t2")
            B_poolw = Buf("poolw")

            S.dma("sp", "d_cst", [(cst[:], cstd[:])], wr=[B_cst])
            load_tab(0, T_MIXPRE0)
            load_tab(1, T_PSCALE)
            load_tab(2, T_MIXPOST0)
            for j in range(NXT):
                dst = pre if j == 0 else hs[:, j - 1, :]
                S.dma("sp", "d_x%d" % j, [(dst, xin[j * 128:(j + 1) * 128, :])], wr=[B_pre if j == 0 else B_hs[j - 1]])
            S.dma("pool", "d_bands", [(bands[:].rearrange("p a b -> p (a b)"), bandd[:])], wr=[B_bands])
            S.dma("pool", "d_poolw", [(poolw.rearrange("p a b c -> p (a b c)"), wst[CH_POOL])], wr=[B_poolw])
            for _ in range(NSLOT):
                ring_issue()

            checkpoint(0)
            def xsrc(j):
                return (pre, B_pre) if j == 0 else (hs[:, j - 1, :], B_hs[j - 1])

            B_s1 = [Buf("s1_%d" % j) for j in range(NXT)]
            B_s3 = [Buf("s3_0"), Buf("s3_1")]
            B_t2p = [Buf("t2p0"), Buf("t2p1")]

            def stage1(j):
                src, bsrc = xsrc(j)
                sumsq(src, j, [bsrc], bst=B_s1[j])
                rstd_batch(1, j, 32 + j, 64 + j, bst=B_s1[j])
                S.op("dve", lambda e: e.scalar_tensor_tensor(
                    out=hb_all[:, j, :], in0=src, scalar=st[:, 64 + j:65 + j], in1=tab[:, 0, :],
                    op0=ALU.mult, op1=ALU.mult), rd=[bsrc, B_s1[j], B_tab[0]], wr=[B_hb[j]])

            def stage2(j):
                i = j - 1
                par = i % 2
                pb = 2 * par
                fns = []
                for k in range(8):
                    w = k // 2
                    own = (8 + w) if j == 1 else w
                    dst = ps[:, pb + k // 4, (k % 4) * 128:(k % 4 + 1) * 128]
                    fns.append(lambda e, dst=dst, k=k, own=own: e.matmul(
                        dst, lhsT=hb_all[:, j, k * 128:(k + 1) * 128], rhs=bands[:, own, :], start=True, stop=False))
                    fns.append(lambda e, dst=dst, k=k, w=w: e.matmul(
                        dst, lhsT=hb_all[:, j - 1, k * 128:(k + 1) * 128], rhs=bands[:, 4 + w, :], start=False, stop=True))
                S.op("pe", fns, rd=[B_hb[j], B_hb[j - 1], B_bands], wr=[B_ps[pb], B_ps[pb + 1]])
                S.op("act", lambda e: e.activation(
                    out=pTs[:, par].rearrange("p (a b) t -> p a (b t)", a=2), in_=ps[:, pb:pb + 2, :], func=AF.Copy),
                    rd=[B_ps[pb], B_ps[pb + 1]], wr=[B_pTs[par]])
                yb = 4 + 2 * par
                fns = []
                for g in range(4):
                    for cc in range(2):
                        dst = ps[:, yb + g // 2, (g % 2) * 256:(g % 2 + 1) * 256]
                        fns.append(lambda e, dst=dst, g=g, cc=cc: e.matmul(
                            dst, lhsT=pTs[:, par, 2 * g + cc, :], rhs=poolw[:, g, cc, :], start=(cc == 0), stop=(cc == 1)))
                S.op("pe", fns, rd=[B_pTs[par], B_poolw], wr=[B_ps[yb], B_ps[yb + 1]])

            def stage3(j):
                i = j - 1
                par = i % 2
                yb = 4 + 2 * par
                c0 = 20 + 4 * par
                ysv = ysb[:, par, :]
                t2v = t2b[:, par, :]
                S.op("dve", lambda e: e.tensor_tensor(
                    out=ysv.rearrange("p (a b) -> p a b", a=2), in0=ps[:, yb:yb + 2, :],
                    in1=tab[:, 1, :].rearrange("p (a b) -> p a b", a=2), op=ALU.mult),
                    rd=[B_ps[yb], B_ps[yb + 1], B_tab[1]], wr=[B_ys[par]])
                sumsq(ysv, c0, [B_ys[par]], bst=B_s3[par])
                rstd_batch(1, c0, c0 + 1, c0 + 2, bst=B_s3[par])
                S.op("dve", lambda e: e.scalar_tensor_tensor(
                    out=t2v, in0=ysv, scalar=st[:, c0 + 2:c0 + 3], in1=tab[:, 2, :], op0=ALU.mult, op1=ALU.mult),
                    rd=[B_ys[par], B_s3[par], B_tab[2]], wr=[B_t2p[par]])
                S.op("pool", lambda e: e.tensor_tensor(out=hs[:, i, :], in0=hs[:, i, :], in1=t2v, op=ALU.add),
                     rd=[B_t2p[par]], wr=[B_hs[i]])

            stage1(0)
            stage1(1)
            for j in range(1, NXT):
                if j + 1 < NXT:
                    stage1(j + 1)
                stage2(j)
                if j - 1 >= 1:
                    stage3(j - 1)
            stage3(NXT - 1)

            checkpoint(1)
            def ffn(layer, tiles, store):
                bt = barrier()
                cv = Carver()
                wd = cv.take(BF16, [128, NFC, D])
                hT = cv.take(BF16, [128, 8, 768])
                actT = cv.take(BF16, [128, NFC, 768])
                hb = cv.take(BF16, [128, 2, D])
                sg = cv.take(F32, [128, 2, 512])
                t2 = cv.take(F32, [128, D])
                B_wd, B_hT, B_actT, B_t2 = Buf("wd"), Buf("hT"), Buf("actT"), Buf("t2")
                B_hb = [Buf("hb0"), Buf("hb1")]
                B_sg = [Buf("sg0"), Buf("sg1")]
                chw = CH_WD0 if layer == 0 else CH_WD1
                S.dma("pool", "d_wd", [(wd[:, 2 * j:2 * j + 2, :].rearrange("p a b -> p (a b)"), wst[chw + j]) for j in range(11)],
                      wr=[B_wd], extra=bt)
                load_tab(0, T_FFNPRE0 if layer == 0 else T_FFNPRE1, extra=bt)
                load_tab(1, T_FFNPOST0 if layer == 0 else T_FFNPOST1, extra=bt)
                n = len(tiles)
                thirds = [tiles[0:6], tiles[6:n - 5], tiles[n - 5:]] if n == 17 else [tiles[0:6], tiles[6:11], tiles[11:]]
                it = 0
                ydb = 0
                B_pre = [Buf("fpre%d" % t_) for t_ in range(6)]

                def prenorm_tile(k3, ti):
                    i = thirds[k3][ti]
                    p2 = ti % 2
                    sumsq(hs[:, i, :], 40 + ti, [B_hs[i]], bst=B_pre[ti])
                    rstd_batch(1, 40 + ti, 48 + ti, 56 + ti, bst=B_pre[ti])
                    S.op("dve", lambda e: e.scalar_tensor_tensor(
                        out=hb[:, p2, :], in0=hs[:, i, :], scalar=st[:, 56 + ti:57 + ti], in1=tab[:, 0, :],
                        op0=ALU.mult, op1=ALU.mult), rd=[B_hs[i], B_pre[ti], B_tab[0]], wr=[B_hb[p2]])
                    tpv = psbf(p2)
                    S.op("pe", [lambda e, k=k: e.transpose(
                        out=tpv[:, k, :], in_=hb[:, p2, k * 128:(k + 1) * 128], identity=ident) for k in range(8)],
                        rd=[B_hb[p2], B_bands], wr=[B_ps[p2]])
                    S.op("act", lambda e: e.activation(
                        out=hT[:, :, ti * 128:(ti + 1) * 128], in_=tpv, func=AF.Copy), rd=[B_ps[p2]], wr=[B_hT])

                for ti in range(len(thirds[0])):
                    prenorm_tile(0, ti)
                for k3, tl in enumerate(thirds):
                    nt = len(tl)
                    ntok = nt * 128
                    groups = [(0, 384), (384, ntok - 384)]
                    for c in range(NFC):
                        slot = ring_next()
                        W = ring[:, slot, :].rearrange("p (k f) -> p k f", k=8)
                        for (g0, N) in groups:
                            gb = 2 + 2 * (it % 2)
                            ub = gb + 1
                            S.op("pe", [lambda e, k=k, W=W, g0=g0, N=N, gb=gb: e.matmul(
                                ps[:, gb, 0:N], lhsT=W[:, k, 0:128], rhs=hT[:, k, g0:g0 + N], start=(k == 0), stop=(k == 7))
                                for k in range(8)], rd=[B_ring[slot], B_hT], wr=[B_ps[gb]])
                            S.op("pe", [lambda e, k=k, W=W, g0=g0, N=N, ub=ub: e.matmul(
                                ps[:, ub, 0:N], lhsT=W[:, k, 128:256], rhs=hT[:, k, g0:g0 + N], start=(k == 0), stop=(k == 7))
                                for k in range(8)], rd=[B_ring[slot], B_hT], wr=[B_ps[ub]])
                            s2 = it % 2
                            S.op("act", lambda e, s2=s2, gb=gb, N=N: e.activation(
                                out=sg[:, s2, 0:N], in_=ps[:, gb, 0:N], func=AF.Silu), rd=[B_ps[gb]], wr=[B_sg[s2]])
                            S.op("dve", lambda e, s2=s2, ub=ub, N=N, c=c, g0=g0: e.tensor_tensor(
                                out=actT[:, c, g0:g0 + N], in0=ps[:, ub, 0:N], in1=sg[:, s2, 0:N], op=ALU.mult),
                                rd=[B_ps[ub], B_sg[s2]], wr=[B_actT])
                            it += 1
                        ring_issue()
                    for ti, i in enumerate(tl):
                        yb = 2 + 2 * (ydb % 3)
                        ydb += 1
                        fns = []
                        for c in range(NFC):
                            for half in range(2):
                                fns.append(lambda e, c=c, half=half, ti=ti, yb=yb: e.matmul(
                                    ps[:, yb + half, :], lhsT=actT[:, c, ti * 128:(ti + 1) * 128],
                                    rhs=wd[:, c, half * 512:(half + 1) * 512], start=(c == 0), stop=(c == NFC - 1)))
                        S.op("pe", fns, rd=[B_actT, B_wd], wr=[B_ps[yb], B_ps[yb + 1]])
                        if k3 + 1 < 3 and ti < len(thirds[k3 + 1]):
                            prenorm_tile(k3 + 1, ti)
                        yv = ps[:, yb:yb + 2, :]
                        S.op("act", lambda e, yv=yv: e.activation(
                            out=junk[:].rearrange("p (a b) -> p a b", a=2), in_=yv, func=AF.Square, accum_out=st[:, 20:21]),
                            rd=[B_ps[yb], B_ps[yb + 1]], wr=[B_junk, B_st])
                        rstd_batch(1, 20, 21, 22)
                        S.op("dve", lambda e, yv=yv: e.scalar_tensor_tensor(
                            out=t2.rearrange("p (a b) -> p a b", a=2), in0=yv, scalar=st[:, 22:23],
                            in1=tab[:, 1, :].rearrange("p (a b) -> p a b", a=2), op0=ALU.mult, op1=ALU.mult),
                            rd=[B_ps[yb], B_ps[yb + 1], B_st, B_tab[1]], wr=[B_t2])
                        S.op("dve", lambda e, i=i: e.tensor_tensor(out=hs[:, i, :], in0=hs[:, i, :], in1=t2, op=ALU.add),
                             rd=[B_t2], wr=[B_hs[i]])
                        if store:
                            S.dma("sp", "d_out", [(outd[(i - 1) * 128:i * 128, :], hs[:, i, :])], rd=[B_hs[i]], wr=[B_out])

            if not os.environ.get('SKIPFFN'):
                ffn(0, list(range(NHS)), False)
            checkpoint(2)

            bt = barrier()
            cv = Carver()
            KT = cv.take(BF16, [128, 2, 2, NHS * 128])
            VA = cv.take(BF16, [128, NHS, 2, 65])
            kv_end = cv.off
            wk = cv.take(BF16, [128, 8, 256])
            wv = cv.take(BF16, [128, 8, 128])
            hT = cv.take(BF16, [128, 8, 512])
            hb = cv.take(BF16, [128, 2, D])
            B_KT, B_VA, B_wkv, B_hT = Buf("KT"), Buf("VA"), Buf("wkv"), Buf("hT")
            B_hb = [Buf("hb0"), Buf("hb1")]
            S.dma("pool", "d_wkv", [(wk.rearrange("p a b -> p (a b)"), wst[CH_WK]),
                                    (wv.rearrange("p a b -> p (a b)"), wst[CH_WV][:, 0:1024])], wr=[B_wkv], extra=bt)
            load_tab(0, T_KV, extra=bt)
            load_tab(1, T_BV, extra=bt)
            KVD = int(os.environ.get('KVDBG', '9'))
            if KVD >= 1:
                S.op("dve", lambda e: e.memset(VA[:, :, :, 64:65], 1.0), wr=[B_VA], extra=bt)
            S.op("dve", lambda e: e.memset(KT.rearrange("p a b c -> p (a b c)"), 0.0), wr=[B_KT], extra=bt)
            B_kvs = [Buf("kvs%d" % i_) for i_ in range(NHS)]
            vcnt = 0
            for g0t in range(0, NHS, 4):
                tl = list(range(g0t, min(g0t + 4, NHS)))
                N = len(tl) * 128
                for ti, i in enumerate(tl):
                    p2 = ti % 2
                    sumsq(hs[:, i, :], i, [B_hs[i]], bst=B_kvs[i])
                    rstd_batch(1, i, 32 + i, 64 + i, bst=B_kvs[i])
                    S.op("dve", lambda e, i=i, p2=p2: e.scalar_tensor_tensor(
                        out=hb[:, p2, :], in0=hs[:, i, :], scalar=st[:, 64 + i:65 + i], in1=tab[:, 0, :],
                        op0=ALU.mult, op1=ALU.mult), rd=[B_hs[i], B_kvs[i], B_tab[0]], wr=[B_hb[p2]])
                    tpv = psbf(p2)
                    S.op("pe", [lambda e, k=k, p2=p2, tpv=tpv: e.transpose(
                        out=tpv[:, k, :], in_=hb[:, p2, k * 128:(k + 1) * 128], identity=ident) for k in range(8)],
                        rd=[B_hb[p2], B_bands], wr=[B_ps[p2]])
                    S.op("act", lambda e, ti=ti, tpv=tpv: e.activation(
                        out=hT[:, :, ti * 128:(ti + 1) * 128], in_=tpv, func=AF.Copy), rd=[B_ps[p2]], wr=[B_hT])
                for kvh in range(2 if KVD >= 3 else 0):
                    kb = 2 + kvh
                    S.op("pe", [lambda e, k=k, kvh=kvh, kb=kb, N=N: e.matmul(
                        ps[:, kb, 0:N], lhsT=wk[:, k, kvh * 128:(kvh + 1) * 128], rhs=hT[:, k, 0:N],
                        start=(k == 0), stop=(k == 7)) for k in range(8)], rd=[B_wkv, B_hT], wr=[B_ps[kb]])
                    for e_ in range(2):
                        S.op("act", lambda e, kvh=kvh, kb=kb, N=N, g0t=g0t, e_=e_: e.activation(
                            out=KT[64 * e_:64 * e_ + 64, kvh, e_, g0t * 128:g0t * 128 + N],
                            in_=ps[64 * e_:64 * e_ + 64, kb, 0:N], func=AF.Identity,
                            bias=cst[64 * e_:64 * e_ + 64, 8 + kvh:9 + kvh]), rd=[B_ps[kb], B_cst], wr=[B_KT])
                for ti, i in enumerate(tl if KVD >= 4 else []):
                    vb = 4 + (vcnt % 2)
                    vcnt += 1
                    S.op("pe", [lambda e, k=k, ti=ti, vb=vb: e.matmul(
                        ps[:, vb, 0:128], lhsT=hT[:, k, ti * 128:(ti + 1) * 128], rhs=wv[:, k, :],
                        start=(k == 0), stop=(k == 7)) for k in range(8)], rd=[B_wkv, B_hT], wr=[B_ps[vb]])
                    S.op("dve", lambda e, i=i, vb=vb: e.tensor_tensor(
                        out=VA[:, i, :, 0:64], in0=ps[:, vb, 0:128].rearrange("p (a b) -> p a b", a=2),
                        in1=tab[:, 1, 0:128].rearrange("p (a b) -> p a b", a=2), op=ALU.add),
                        rd=[B_ps[vb], B_tab[1]], wr=[B_VA])

            checkpoint(3)
            bt = barrier()
            cv = Carver()
            cv.off = kv_end
            wo = cv.take(BF16, [128, 8, D])
            bias = cv.take(F32, [128, 16, 2, 128])
            PT = cv.take(BF16, [128, 16, 2, 128])
            lg = cv.take(F32, [128, 2, 512])
            hT4 = cv.take(BF16, [128, 8, 512])
            QT4 = cv.take(BF16, [128, 8, 512])
            oT4 = cv.take(BF16, [128, 4, 8, 128])
            hb = cv.take(BF16, [128, 2, D])
            obf = cv.take(BF16, [128, D])
            t2 = cv.take(F32, [128, D])
            bohl = cv.take(BF16, [128, 2, D])
            ones = cv.take(BF16, [128, 128])
            B_wo, B_bias, B_PTx, B_hT4, B_QT4, B_obf, B_t2, B_bohl = (Buf(n) for n in
                                                                      ("wo", "bias", "PT", "hT4", "QT4", "obf", "t2", "bohl"))
            B_PT = [Buf("PT%d" % h_) for h_ in range(8)]
            B_oT4 = [Buf("oT4_%d" % t) for t in range(4)]
            B_lg = [Buf("lg0"), Buf("lg1")]
            B_hb = [Buf("hb0"), Buf("hb1")]
            B_esink, B_stb = Buf("esink"), Buf("stb")
            B_pn = [Buf("pn0"), Buf("pn1")]
            B_den = [Buf("den0"), Buf("den1")]
            S.dma("pool", "d_wqo", [(wo[:, 2 * j:2 * j + 2, :].rearrange("p a b -> p (a b)"), wst[CH_WO + j]) for j in range(4)],
                  wr=[B_wo], extra=bt)
            S.dma("sp", "d_bias", [(bias.rearrange("p a b c -> p (a b c)"), btabd[0])], wr=[B_bias], extra=bt)
            load_tab(0, T_MIXPRE1, extra=bt)
            load_tab(1, T_MIXPOST1, extra=bt)
            load_tab(2, T_BO, extra=bt)
            S.op("dve", lambda e: e.memset(ones, 1.0), wr=[B_bohl], extra=bt)
            S.op("dve", lambda e: e.tensor_copy(out=bohl[:, 0, :], in_=tab[:, 2, :]), rd=[B_tab[2]], wr=[B_bohl])
            S.op("dve", lambda e: e.tensor_tensor(out=t2, in0=tab[:, 2, :], in1=bohl[:, 0, :], op=ALU.subtract),
                 rd=[B_tab[2], B_bohl], wr=[B_t2])
            S.op("dve", lambda e: e.tensor_copy(out=bohl[:, 1, :], in_=t2), rd=[B_t2], wr=[B_bohl])
            S.op("act", lambda e: e.activation(out=st[:, 96:112], in_=cst[:, 10:26], func=AF.Exp),
                 rd=[B_cst], wr=[B_esink], extra=bt)
            hsplit = [(0, 7), (7, 14), (14, 16)]

            def q_stage(g):
                for ti in range(4):
                    i = 1 + 4 * g + ti
                    p2 = ti % 2
                    tb = 6 + p2
                    S.op("dve", lambda e, i=i, p2=p2: e.scalar_tensor_tensor(
                        out=hb[:, p2, :], in0=hs[:, i, :], scalar=st[:, 64 + i:65 + i], in1=tab[:, 0, :],
                        op0=ALU.mult, op1=ALU.mult), rd=[B_hs[i], B_kvs[i], B_tab[0]], wr=[B_hb[p2]])
                    tpv = psbf(tb)
                    S.op("pe", [lambda e, k=k, p2=p2, tpv=tpv: e.transpose(
                        out=tpv[:, k, :], in_=hb[:, p2, k * 128:(k + 1) * 128], identity=ident) for k in range(8)],
                        rd=[B_hb[p2], B_bands], wr=[B_ps[tb]])
                    S.op("act", lambda e, ti=ti, tpv=tpv: e.activation(
                        out=hT4[:, :, ti * 128:(ti + 1) * 128], in_=tpv, func=AF.Copy), rd=[B_ps[tb]], wr=[B_hT4])
                for j in range(4):
                    slot = ring_next()
                    W = ring[:, slot, :].rearrange("p (k f) -> p k f", k=8)
                    for e2 in range(2):
                        hp = 2 * j + e2
                        qb = 6 + hp % 2
                        S.op("pe", [lambda e, k=k, W=W, e2=e2, qb=qb: e.matmul(
                            ps[:, qb, :], lhsT=W[:, k, e2 * 128:(e2 + 1) * 128], rhs=hT4[:, k, :],
                            start=(k == 0), stop=(k == 7)) for k in range(8)], rd=[B_ring[slot], B_hT4], wr=[B_ps[qb]])
                        if hp % 2 == 0:
                            S.op("act", lambda e, hp=hp, qb=qb: e.activation(
                                out=QT4[:, hp, :], in_=ps[:, qb, :], func=AF.Identity, bias=cst[:, hp:hp + 1]),
                                rd=[B_ps[qb], B_cst], wr=[B_QT4])
                        else:
                            S.op("dve", lambda e, hp=hp, qb=qb: e.tensor_scalar(
                                out=QT4[:, hp, :], in0=ps[:, qb, :], scalar1=cst[:, hp:hp + 1], scalar2=None, op0=ALU.add),
                                rd=[B_ps[qb], B_cst], wr=[B_QT4])
                    ring_issue()

            def s_tile(g, ti, hook=None):
                i = 1 + 4 * g + ti
                p2 = i % 2
                den = st[:, 112 + 16 * p2:128 + 16 * p2]

                def scores(hp):
                    kvh = hp // 4
                    sbk = 1 + hp % 2
                    sv = ps[:, sbk, :].rearrange("p (e b q) -> p e b q", e=2, b=2)
                    fns = []
                    for e_ in range(2):
                        for blk in range(2):
                            kt = i - 1 + blk
                            fns.append(lambda e, e_=e_, blk=blk, kt=kt: e.matmul(
                                sv[:, e_, blk, :], lhsT=KT[:, kvh, e_, kt * 128:(kt + 1) * 128],
                                rhs=QT4[:, hp, ti * 128:(ti + 1) * 128], start=True, stop=True))
                    S.op("pe", fns, rd=[B_KT, B_QT4], wr=[B_ps[sbk]])

                def c1_bank(bi):
                    h0, h1 = hsplit[bi]
                    nh = h1 - h0
                    ov = ps[:, 3 + bi, 0:nh * 65].rearrange("p (h c) -> p h c", c=65)
                    S.op("dve", lambda e: e.tensor_tensor(
                        out=den[:, h0:h1], in0=ov[:, :, 64], in1=st[:, 96 + h0:96 + h1], op=ALU.add),
                        rd=[B_ps[3 + bi], B_esink], wr=[B_den[p2]])
                    S.op("dve", lambda e: e.reciprocal(out=den[:, h0:h1], in_=den[:, h0:h1]), rd=[], wr=[B_den[p2]])
                    S.op("dve", lambda e: e.tensor_tensor(
                        out=obf.rearrange("p (h c) -> p h c", c=64)[:, h0:h1, :], in0=ov[:, :, 0:64],
                        in1=den[:, h0:h1].unsqueeze(2).to_broadcast([128, nh, 64]), op=ALU.mult),
                        rd=[B_ps[3 + bi], B_den[p2]], wr=[B_obf])

                scores(0)
                scores(1)
                for hp in range(8):
                    kvh = hp // 4
                    sbk = 1 + hp % 2
                    l2 = hp % 2
                    S.op("dve", lambda e, hp=hp, sbk=sbk, l2=l2: e.scalar_tensor_tensor(
                        out=lg[:, l2, :], in0=ps[:, sbk, :], scalar=0.125,
                        in1=bias[:, 2 * hp:2 * hp + 2, :, :].rearrange("p a b c -> p (a b c)"), op0=ALU.mult, op1=ALU.add),
                        rd=[B_ps[sbk], B_bias], wr=[B_lg[l2]])
                    S.op("act", lambda e, hp=hp, l2=l2: e.activation(
                        out=PT[:, 2 * hp:2 * hp + 2, :, :].rearrange("p a b c -> p (a b c)"), in_=lg[:, l2, :], func=AF.Exp),
                        rd=[B_lg[l2]], wr=[B_PT[hp]])
                    if hp + 2 < 8:
                        scores(hp + 2)
                    fns = []
                    for e_ in range(2):
                        h = 2 * hp + e_
                        ob = 3 + h // 7
                        oc = (h % 7) * 65
                        for blk in range(2):
                            kt = i - 1 + blk
                            fns.append(lambda e, h=h, ob=ob, oc=oc, blk=blk, kt=kt, kvh=kvh: e.matmul(
                                ps[:, ob, oc:oc + 65], lhsT=PT[:, h, blk, :], rhs=VA[:, kt, kvh, :],
                                start=(blk == 0), stop=(blk == 1)))
                    S.op("pe", fns, rd=[B_PT[hp], B_VA],
                         wr=[B_ps[b_] for b_ in sorted({3 + (2 * hp) // 7, 3 + (2 * hp + 1) // 7})])
                    if hp == 1 and hook is not None:
                        hook()
                    if hp == 4:
                        c1_bank(0)
                    if hp == 7:
                        c1_bank(1)
                if i == 1:
                    S.dma("sp", "d_bias", [(bias.rearrange("p a b c -> p (a b c)"), btabd[1])], wr=[B_bias])

                def tail():
                    c1_bank(2)
                    tpv = psbf(0)
                    S.op("pe", [lambda e, k=k: e.transpose(
                        out=tpv[:, k, :], in_=obf[:, k * 128:(k + 1) * 128], identity=ident) for k in range(8)],
                        rd=[B_obf, B_bands], wr=[B_ps[0]])
                    S.op("act", lambda e: e.activation(out=oT4[:, ti], in_=tpv, func=AF.Copy), rd=[B_ps[0]], wr=[B_oT4[ti]])
                return tail

            def o_tile(g, ti, ybase):
                i = 1 + 4 * g + ti
                p2 = ti % 2
                c0 = 20 + 4 * p2
                for half in range(2):
                    fns = [lambda e, k=k, half=half: e.matmul(
                        ps[:, ybase + half, :], lhsT=oT4[:, ti, k, :], rhs=wo[:, k, half * 512:(half + 1) * 512],
                        start=(k == 0), stop=False) for k in range(8)]
                    for hl in range(2):
                        fns.append(lambda e, hl=hl, half=half: e.matmul(
                            ps[:, ybase + half, :], lhsT=ones[0:1, :], rhs=bohl[0:1, hl, half * 512:(half + 1) * 512],
                            start=False, stop=(hl == 1)))
                    S.op("pe", fns, rd=[B_oT4[ti], B_wo, B_bohl], wr=[B_ps[ybase + half]])
                yv = ps[:, ybase:ybase + 2, :]
                S.op("act", lambda e: e.activation(
                    out=junk[:].rearrange("p (a b) -> p a b", a=2), in_=yv, func=AF.Square,
                    accum_out=st[:, c0:c0 + 1]), rd=[B_ps[ybase], B_ps[ybase + 1]], wr=[B_junk, B_pn[p2]])
                rstd_batch(1, c0, c0 + 1, c0 + 2, bst=B_pn[p2])
                S.op("dve", lambda e: e.scalar_tensor_tensor(
                    out=t2.rearrange("p (a b) -> p a b", a=2), in0=yv, scalar=st[:, c0 + 2:c0 + 3],
                    in1=tab[:, 1, :].rearrange("p (a b) -> p a b", a=2), op0=ALU.mult, op1=ALU.mult),
                    rd=[B_ps[ybase], B_ps[ybase + 1], B_pn[p2], B_tab[1]], wr=[B_t2])
                S.op("pool", lambda e: e.tensor_tensor(out=hs[:, i, :], in0=hs[:, i, :], in1=t2, op=ALU.add),
                     rd=[B_t2], wr=[B_hs[i]])

            q_stage(0)
            for g in range(4):
                tail = None
                for ti in range(4):
                    tail = s_tile(g, ti, hook=tail)
                tail()
                if g + 1 < 4:
                    q_stage(g + 1)
                for t_ in range(4):
                    o_tile(g, t_, 6 if t_ % 2 == 0 else 1)

            checkpoint(4)
            ffn(1, list(range(1, NHS)), True)


        except _Stop:
            pass
        S.wait("sp", B_out.w)

        with nc.Block() as block:
            @block.tensor
            def _(e):
                for f in S.ops["pe"]:
                    f(e)

            @block.scalar
            def _(e):
                for f in S.ops["act"]:
                    f(e)

            @block.vector
            def _(e):
                for f in S.ops["dve"]:
                    f(e)

            @block.gpsimd
            def _(e):
                for f in S.ops["pool"]:
                    f(e)

            @block.sync
            def _(e):
                for f in S.ops["sp"]:
                    f(e)
    return nc


def _t5_bucket(d):
    max_exact = 16
    df = np.maximum(d, 1).astype(np.float32)
    large = max_exact + (np.log(df / max_exact) / np.log(128 / max_exact) * (32 - max_exact)).astype(np.int32)
    large = np.minimum(large, 31)
    return np.where(d < max_exact, d, large)


def _band_consts(first_core):
    out = np.zeros((128, 13, 128), np.float32)
    tp = np.arange(128)[:, None]
    t = np.arange(128)[None, :]
    for wi, w in enumerate((2, 4, 8, 16)):
        dlt = t - tp
        own = np.where((dlt >= 0) & (dlt < w), 1.0 / w, 0.0) - (dlt == 0)
        prev = np.where((t + 128 - tp) < w, 1.0 / w, 0.0)
        out[:, wi, :] = own
        out[:, 4 + wi, :] = prev
        if first_core:
            pos = np.maximum(t - 112 + 1, 1)
            cnt = np.minimum(pos, w).astype(np.float32)
            valid = (dlt >= 0) & (dlt < w) & (tp >= 112)
            out[:, 8 + wi, :] = np.where(valid, 1.0 / cnt, 0.0) - (dlt == 0)
        else:
            out[:, 8 + wi, :] = own
    out[:, 12, :] = np.eye(128, dtype=np.float32)
    return out.reshape(128, 13 * 128)


def _bias_tables(rel_bias, first_core):
    s = np.arange(128)[:, None]
    q = np.arange(128)[None, :]
    tabs = np.empty((2, 128, 16, 2, 128), np.float32)
    for blk in range(2):
        d = q - s + (128 if blk == 0 else 0)
        ok = (d >= 0) & (d < 128)
        bidx = _t5_bucket(np.maximum(d, 0))
        g = rel_bias[bidx]
        g = np.where(ok[:, :, None], g, np.float32(NEG))
        g = np.transpose(g, (0, 2, 1))
        tabs[1, :, :, blk, :] = g
        if blk == 0 and first_core:
            g = np.where((s >= 112)[:, :, None] & np.ones((1, 1, 128), bool), g, np.float32(NEG))
        tabs[0, :, :, blk, :] = g
    return tabs.reshape(2, 128, 16 * 2 * 128)


def _prep_common(inp):
    f = np.float32
    wst = np.zeros((NCHUNK, 128, 2048), f)
    pw = inp["pool_w"][0]
    wst[CH_POOL] = pw.reshape(4, 2, 128, 256).transpose(2, 0, 1, 3).reshape(128, 2048)
    for l in range(2):
        gu = inp["w_gate_up"][l]
        gate = gu[:, :DFF].reshape(8, 128, NFC, 128)
        up = gu[:, DFF:].reshape(8, 128, NFC, 128)
        blk = np.stack([gate, up], axis=3)
        blk = blk.transpose(2, 1, 0, 3, 4).reshape(NFC, 128, 2048)
        base = CH_GU0 if l == 0 else CH_GU1
        wst[base:base + NFC] = blk
        wdn = inp["w_down"][l].reshape(11, 2, 128, D).transpose(0, 2, 1, 3).reshape(11, 128, 2048)
        base = CH_WD0 if l == 0 else CH_WD1
        wst[base:base + 11] = wdn
    wk = inp["w_k"].reshape(8, 128, 2, 64)
    wkd = np.stack([wk, wk], axis=3)
    wst[CH_WK] = wkd.transpose(1, 0, 2, 3, 4).reshape(128, 2048)
    wst[CH_WV][:, :1024] = inp["w_v"].reshape(8, 128, 128).transpose(1, 0, 2).reshape(128, 1024)
    wq = inp["w_q"][0].reshape(8, 128, 4, 256)
    wst[CH_WQ:CH_WQ + 4] = wq.transpose(2, 1, 0, 3).reshape(4, 128, 2048)
    wo = inp["w_o"][0].reshape(4, 2, 128, D)
    wst[CH_WO:CH_WO + 4] = wo.transpose(0, 2, 1, 3).reshape(4, 128, 2048)

    rows = [inp["norm_mix_pre"][0], inp["pool_scale"][0], inp["norm_mix_post"][0], inp["norm_ffn_pre"][0],
            inp["norm_ffn_post"][0], inp["kv_norm"], inp["norm_mix_pre"][1], inp["norm_mix_post"][1],
            inp["b_o"][0], inp["norm_ffn_pre"][1], inp["norm_ffn_post"][1],
            np.concatenate([inp["b_v"], np.zeros(D - 128, f)])]
    tabs = np.ascontiguousarray(np.broadcast_to(np.stack(rows)[:, None, :], (NTAB, 128, D))).astype(f)

    cst = np.zeros((128, NCST), f)
    cst[:, 0:8] = inp["b_q"][0].reshape(8, 128).T
    bk = inp["b_k"].reshape(2, 64)
    cst[:, 8:10] = np.concatenate([bk, bk], axis=1).T
    cst[:, 10:26] = np.broadcast_to(inp["sinks"][0][None, :], (128, 16))
    cst[:, 26] = EPS
    return wst, tabs, cst


_NC_CACHE = {}


def kernel(**inputs):
    inp = {k: np.asarray(v, dtype=np.float32) for k, v in inputs.items()}
    x = inp["x"]
    meta = inp["meta_tokens"]
    wst, tabs, cst = _prep_common(inp)
    bands = [_band_consts(True), _band_consts(False)]
    btab = [_bias_tables(inp["rel_bias"], True), _bias_tables(inp["rel_bias"], False)]
    in_maps = []
    for c in range(8):
        b, r = c // 4, c % 4
        xin = np.zeros((NXT * 128, D), np.float32)
        if r == 0:
            xin[128 + 112:256] = meta
            xin[256:] = x[b, 0:2048]
        else:
            xin[:] = x[b, 2048 * r - 256:2048 * r + 2048]
        fc = (r == 0)
        in_maps.append({"xin": xin, "wst": wst, "tabs": tabs, "cst": cst,
                        "bands": bands[0 if fc else 1], "btab": btab[0 if fc else 1]})
    if "nc" not in _NC_CACHE:
        _NC_CACHE["nc"] = build_program()
    nc = _NC_CACHE["nc"]
    res = run_bass_kernel_spmd(nc, in_maps, core_ids=list(range(8)))
    out = np.empty((2, 8192, D), np.float32)
    for c in range(8):
        b, r = c // 4, c % 4
        out[b, 2048 * r:2048 * r + 2048] = res.results[c]["out"]
    return out
```

```python
import os
import numpy as np
from contextlib import ExitStack
import concourse.bass as bass
import concourse.mybir as mybir
from concourse.bass_utils import run_bass_kernel_spmd

F32 = mybir.dt.float32
BF16 = mybir.dt.bfloat16
AF = mybir.ActivationFunctionType
ALU = mybir.AluOpType

D = 1024
DFF = 2816
NFC = DFF // 128
NXT = 18
NHS = 17
EPS = 1e-6
NSLOT = 4
NEG = -30000.0

CH_POOL = 0
CH_GU0 = 1
CH_WD0 = CH_GU0 + 22
CH_WK = CH_WD0 + 11
CH_WV = CH_WK + 1
CH_WQ = CH_WV + 1
CH_WO = CH_WQ + 4
CH_GU1 = CH_WO + 4
CH_WD1 = CH_GU1 + 22
NCHUNK = CH_WD1 + 11

(T_MIXPRE0, T_PSCALE, T_MIXPOST0, T_FFNPRE0, T_FFNPOST0, T_KV, T_MIXPRE1, T_MIXPOST1, T_BO,
 T_FFNPRE1, T_FFNPOST1, T_BV) = range(12)
NTAB = 12
NCST = 32


class Buf:
    def __init__(self, name):
        self.name = name
        self.w = None
        self.r = {}


class Sched:
    QS = ("pe", "act", "dve", "pool", "sp")

    def __init__(self, nc, es):
        self.nc = nc
        self.es = es
        self.ops = {k: [] for k in self.QS}
        self.sem = {}
        self.count = {k: 0 for k in self.QS}
        self.seen = {k: {} for k in self.QS}
        self.dcount = {}
        for q in self.QS:
            self.getsem(q)

    def getsem(self, name):
        if name not in self.sem:
            self.sem[name] = self.es.enter_context(self.nc.semaphore(name))
        return self.sem[name]

    def wait(self, q, tok):
        if tok is None:
            return
        sname, val = tok
        if self.seen[q].get(sname, 0) >= val:
            return
        self.seen[q][sname] = val
        sem = self.getsem(sname)
        self.ops[q].append(lambda e, sem=sem, val=val: e.wait_ge(sem, val))

    def _deps(self, q, rd, wr, extra):
        for t in extra:
            self.wait(q, t)
        for b in rd:
            self.wait(q, b.w)
        for b in wr:
            self.wait(q, b.w)
            for sn, v in b.r.items():
                self.wait(q, (sn, v))

    def _mark(self, tok, rd, wr):
        for b in rd:
            if b.r.get(tok[0], 0) < tok[1]:
                b.r[tok[0]] = tok[1]
        for b in wr:
            b.w = tok
            b.r = {}

    def op(self, q, fn, rd=(), wr=(), extra=()):
        self._deps(q, rd, wr, extra)
        fns = fn if isinstance(fn, (list, tuple)) else [fn]
        for f in fns[:-1]:
            self.ops[q].append(lambda e, f=f: f(e))
        self.count[q] += 1
        sem = self.sem[q]
        last = fns[-1]
        self.ops[q].append(lambda e, f=last, sem=sem: f(e).then_inc(sem, 1))
        tok = (q, self.count[q])
        self._mark(tok, rd, wr)
        return tok

    def dma(self, q, semname, pairs, rd=(), wr=(), extra=()):
        self._deps(q, rd, wr, extra)
        sem = self.getsem(semname)
        for out, in_ in pairs:
            self.ops[q].append(lambda e, out=out, in_=in_, sem=sem: e.dma_start(out=out, in_=in_).then_inc(sem, 16))
        self.dcount[semname] = self.dcount.get(semname, 0) + 16 * len(pairs)
        tok = (semname, self.dcount[semname])
        self._mark(tok, rd, wr)
        return tok


class _Stop(Exception):
    pass


def build_program(stop=99):
    nc = bass.Bass("TRN2", target_bir_lowering=False)
    xin = nc.dram_tensor("xin", [NXT * 128, D], F32, kind="ExternalInput").ap()
    wst = nc.dram_tensor("wst", [NCHUNK, 128, 2048], F32, kind="ExternalInput").ap()
    tabs = nc.dram_tensor("tabs", [NTAB, 128, D], F32, kind="ExternalInput").ap()
    cstd = nc.dram_tensor("cst", [128, NCST], F32, kind="ExternalInput").ap()
    bandd = nc.dram_tensor("bands", [128, 13 * 128], F32, kind="ExternalInput").ap()
    btabd = nc.dram_tensor("btab", [2, 128, 16 * 2 * 128], F32, kind="ExternalInput").ap()
    outd = nc.dram_tensor("out", [16 * 128, D], F32, kind="ExternalOutput").ap()
    dbgd = nc.dram_tensor("dbg", [NHS * 128, D], F32, kind="ExternalOutput").ap() if stop != 99 else None

    with ExitStack() as es:
        S = Sched(nc, es)

        def sb(name, shape, dt):
            return es.enter_context(nc.sbuf_tensor(name, shape, dt))

        hs = sb("hs", [128, NHS, D], F32)
        tab = sb("tab", [128, 3, D], F32)
        ring = sb("ring", [128, NSLOT, 2048], BF16)
        bands = sb("bandsb", [128, 13, 128], BF16)
        cst = sb("cstsb", [128, NCST], F32)
        st = sb("st", [128, 160], F32)
        junk = sb("junk", [128, D], BF16)
        PH_ELEMS = 52 * 1024
        phase = sb("phase", [128, PH_ELEMS], BF16)
        ps = es.enter_context(nc.psum_tensor("ps", [128, 8, 512], F32))

        ident = bands[:, 12, :]
        B_hs = [Buf("hs%d" % i) for i in range(NHS)]
        B_tab = [Buf("tab%d" % i) for i in range(3)]
        B_ring = [Buf("ring%d" % i) for i in range(NSLOT)]
        B_bands = Buf("bands")
        B_cst = Buf("cst")
        B_st = Buf("st")
        B_junk = Buf("junk")
        B_ps = [Buf("ps%d" % i) for i in range(8)]
        B_out = Buf("out")

        def psbf(b):
            return ps[:, b, :].bitcast(BF16).rearrange("p (k t) -> p k t", k=8)

        class Carver:
            def __init__(self):
                self.off = 0

            def take(self, dt, shape):
                n = int(np.prod(shape[1:]))
                nb = n if dt == BF16 else 2 * n
                nb = (nb + 15) // 16 * 16
                assert self.off + nb <= PH_ELEMS, ("phase overflow", self.off + nb)
                v = phase[:, self.off:self.off + nb]
                self.off += nb
                if dt == F32:
                    v = v.bitcast(F32)
                v = v[:, 0:n]
                if len(shape) == 3:
                    v = v.rearrange("p (a b) -> p a b", a=shape[1])
                elif len(shape) == 4:
                    v = v.rearrange("p (a b c) -> p a b c", a=shape[1], b=shape[2])
                return v

        def barrier():
            toks = [(q, S.count[q]) for q in ("pe", "act", "dve", "pool") if S.count[q] > 0]
            for q in ("pe", "act", "dve"):
                for t in toks:
                    if t[0] != q:
                        S.wait(q, t)
            return toks

        def load_tab(slot, idx, extra=()):
            return S.dma("sp", "d_tab%d" % slot, [(tab[:, slot, :], tabs[idx])], wr=[B_tab[slot]], extra=extra)

        ring_seq = []
        for l in range(2):
            if l == 1:
                for g_ in range(4):
                    for j_ in range(4):
                        ring_seq.append(CH_WQ + j_)
            for third in range(3):
                for c in range(NFC):
                    ring_seq.append((CH_GU0 if l == 0 else CH_GU1) + c)
        ring_state = {"issued": 0, "used": 0}

        def ring_issue():
            i = ring_state["issued"]
            if i >= len(ring_seq):
                return
            slot = i % NSLOT
            S.dma("pool", "d_ring%d" % slot, [(ring[:, slot, :], wst[ring_seq[i]])], wr=[B_ring[slot]])
            ring_state["issued"] += 1

        def ring_next():
            i = ring_state["used"]
            ring_state["used"] += 1
            return i % NSLOT

        def rstd_batch(n, col_ss, col_sd, col_r, bst=None):
            bst = bst or B_st
            S.op("act", lambda e: e.activation(out=st[:, col_sd:col_sd + n], in_=st[:, col_ss:col_ss + n],
                                                func=AF.Sqrt, bias=cst[:, 26:27], scale=1.0 / D),
                 rd=[B_cst], wr=[bst])
            S.op("dve", lambda e: e.reciprocal(out=st[:, col_r:col_r + n], in_=st[:, col_sd:col_sd + n]), wr=[bst])

        def sumsq(src, col, rd, bst=None):
            S.op("act", lambda e: e.activation(out=junk[:], in_=src, func=AF.Square, accum_out=st[:, col:col + 1]),
                 rd=rd, wr=[B_junk, bst or B_st])

        def checkpoint(n):
            if stop == n:
                bt_ = barrier()
                for i_ in range(NHS):
                    S.dma("sp", "d_dbg", [(dbgd[i_ * 128:(i_ + 1) * 128, :], hs[:, i_, :])], rd=[B_hs[i_]], wr=[B_out], extra=bt_)
                raise _Stop()

        try:
            B_pre = Buf("pre")
            B_ph = Buf("phase_generic")
            cv = Carver()
            hb_all = cv.take(BF16, [128, NXT, D])
            pre = cv.take(F32, [128, D])
            pTs = cv.take(BF16, [128, 2, 8, 128])
            ysb = cv.take(F32, [128, 2, D])
            t2b = cv.take(F32, [128, 2, D])
            poolw = cv.take(BF16, [128, 4, 2, 256])
            B_hb = [Buf("hb%d" % j) for j in range(NXT)]
            B_pTs = [Buf("pTs0"), Buf("pTs1")]
            B_ys = [Buf("ys0"), Buf("ys1")]
            B_t2 = Buf("t2")
            B_poolw = Buf("poolw")

            S.dma("sp", "d_cst", [(cst[:], cstd[:])], wr=[B_cst])
            load_tab(0, T_MIXPRE0)
            load_tab(1, T_PSCALE)
            load_tab(2, T_MIXPOST0)
            for j in range(NXT):
                dst = pre if j == 0 else hs[:, j - 1, :]
                S.dma("sp", "d_x%d" % j, [(dst, xin[j * 128:(j + 1) * 128, :])], wr=[B_pre if j == 0 else B_hs[j - 1]])
            S.dma("pool", "d_bands", [(bands[:].rearrange("p a b -> p (a b)"), bandd[:])], wr=[B_bands])
            S.dma("pool", "d_poolw", [(poolw.rearrange("p a b c -> p (a b c)"), wst[CH_POOL])], wr=[B_poolw])
            for _ in range(NSLOT):
                ring_issue()

            checkpoint(0)
            def xsrc(j):
                return (pre, B_pre) if j == 0 else (hs[:, j - 1, :], B_hs[j - 1])

            B_s1 = [Buf("s1_%d" % j) for j in range(NXT)]
            B_s3 = [Buf("s3_0"), Buf("s3_1")]
            B_t2p = [Buf("t2p0"), Buf("t2p1")]

            def stage1(j):
                src, bsrc = xsrc(j)
                sumsq(src, j, [bsrc], bst=B_s1[j])
                rstd_batch(1, j, 32 + j, 64 + j, bst=B_s1[j])
                S.op("dve", lambda e: e.scalar_tensor_tensor(
                    out=hb_all[:, j, :], in0=src, scalar=st[:, 64 + j:65 + j], in1=tab[:, 0, :],
                    op0=ALU.mult, op1=ALU.mult), rd=[bsrc, B_s1[j], B_tab[0]], wr=[B_hb[j]])

            def stage2(j):
                i = j - 1
                par = i % 2
                pb = 2 * par
                fns = []
                for k in range(8):
                    w = k // 2
                    own = (8 + w) if j == 1 else w
                    dst = ps[:, pb + k // 4, (k % 4) * 128:(k % 4 + 1) * 128]
                    fns.append(lambda e, dst=dst, k=k, own=own: e.matmul(
                        dst, lhsT=hb_all[:, j, k * 128:(k + 1) * 128], rhs=bands[:, own, :], start=True, stop=False))
                    fns.append(lambda e, dst=dst, k=k, w=w: e.matmul(
                        dst, lhsT=hb_all[:, j - 1, k * 128:(k + 1) * 128], rhs=bands[:, 4 + w, :], start=False, stop=True))
                S.op("pe", fns, rd=[B_hb[j], B_hb[j - 1], B_bands], wr=[B_ps[pb], B_ps[pb + 1]])
                S.op("act", lambda e: e.activation(
                    out=pTs[:, par].rearrange("p (a b) t -> p a (b t)", a=2), in_=ps[:, pb:pb + 2, :], func=AF.Copy),
                    rd=[B_ps[pb], B_ps[pb + 1]], wr=[B_pTs[par]])
                yb = 4 + 2 * par
                fns = []
                for g in range(4):
                    for cc in range(2):
                        dst = ps[:, yb + g // 2, (g % 2) * 256:(g % 2 + 1) * 256]
                        fns.append(lambda e, dst=dst, g=g, cc=cc: e.matmul(
                            dst, lhsT=pTs[:, par, 2 * g + cc, :], rhs=poolw[:, g, cc, :], start=(cc == 0), stop=(cc == 1)))
                S.op("pe", fns, rd=[B_pTs[par], B_poolw], wr=[B_ps[yb], B_ps[yb + 1]])

            def stage3(j):
                i = j - 1
                par = i % 2
                yb = 4 + 2 * par
                c0 = 20 + 4 * par
                ysv = ysb[:, par, :]
                t2v = t2b[:, par, :]
                S.op("dve", lambda e: e.tensor_tensor(
                    out=ysv.rearrange("p (a b) -> p a b", a=2), in0=ps[:, yb:yb + 2, :],
                    in1=tab[:, 1, :].rearrange("p (a b) -> p a b", a=2), op=ALU.mult),
                    rd=[B_ps[yb], B_ps[yb + 1], B_tab[1]], wr=[B_ys[par]])
                sumsq(ysv, c0, [B_ys[par]], bst=B_s3[par])
                rstd_batch(1, c0, c0 + 1, c0 + 2, bst=B_s3[par])
                S.op("dve", lambda e: e.scalar_tensor_tensor(
                    out=t2v, in0=ysv, scalar=st[:, c0 + 2:c0 + 3], in1=tab[:, 2, :], op0=ALU.mult, op1=ALU.mult),
                    rd=[B_ys[par], B_s3[par], B_tab[2]], wr=[B_t2p[par]])
                S.op("pool", lambda e: e.tensor_tensor(out=hs[:, i, :], in0=hs[:, i, :], in1=t2v, op=ALU.add),
                     rd=[B_t2p[par]], wr=[B_hs[i]])

            stage1(0)
            stage1(1)
            for j in range(1, NXT):
                if j + 1 < NXT:
                    stage1(j + 1)
                stage2(j)
                if j - 1 >= 1:
                    stage3(j - 1)
            stage3(NXT - 1)

            checkpoint(1)
            def ffn(layer, tiles, store):
                bt = barrier()
                cv = Carver()
                wd = cv.take(BF16, [128, NFC, D])
                hT = cv.take(BF16, [128, 8, 768])
                actT = cv.take(BF16, [128, NFC, 768])
                hb = cv.take(BF16, [128, 2, D])
                sg = cv.take(F32, [128, 2, 512])
                t2 = cv.take(F32, [128, D])
                B_wd, B_hT, B_actT, B_t2 = Buf("wd"), Buf("hT"), Buf("actT"), Buf("t2")
                B_hb = [Buf("hb0"), Buf("hb1")]
                B_sg = [Buf("sg0"), Buf("sg1")]
                chw = CH_WD0 if layer == 0 else CH_WD1
                S.dma("pool", "d_wd", [(wd[:, 2 * j:2 * j + 2, :].rearrange("p a b -> p (a b)"), wst[chw + j]) for j in range(11)],
                      wr=[B_wd], extra=bt)
                load_tab(0, T_FFNPRE0 if layer == 0 else T_FFNPRE1, extra=bt)
                load_tab(1, T_FFNPOST0 if layer == 0 else T_FFNPOST1, extra=bt)
                n = len(tiles)
                thirds = [tiles[0:6], tiles[6:n - 5], tiles[n - 5:]] if n == 17 else [tiles[0:6], tiles[6:11], tiles[11:]]
                it = 0
                ydb = 0
                B_pre = [Buf("fpre%d" % t_) for t_ in range(6)]

                def prenorm_tile(k3, ti):
                    i = thirds[k3][ti]
                    p2 = ti % 2
                    sumsq(hs[:, i, :], 40 + ti, [B_hs[i]], bst=B_pre[ti])
                    rstd_batch(1, 40 + ti, 48 + ti, 56 + ti, bst=B_pre[ti])
                    S.op("dve", lambda e: e.scalar_tensor_tensor(
                        out=hb[:, p2, :], in0=hs[:, i, :], scalar=st[:, 56 + ti:57 + ti], in1=tab[:, 0, :],
                        op0=ALU.mult, op1=ALU.mult), rd=[B_hs[i], B_pre[ti], B_tab[0]], wr=[B_hb[p2]])
                    tpv = psbf(p2)
                    S.op("pe", [lambda e, k=k: e.transpose(
                        out=tpv[:, k, :], in_=hb[:, p2, k * 128:(k + 1) * 128], identity=ident) for k in range(8)],
                        rd=[B_hb[p2], B_bands], wr=[B_ps[p2]])
                    S.op("act", lambda e: e.activation(
                        out=hT[:, :, ti * 128:(ti + 1) * 128], in_=tpv, func=AF.Copy), rd=[B_ps[p2]], wr=[B_hT])

                for ti in range(len(thirds[0])):
                    prenorm_tile(0, ti)
                for k3, tl in enumerate(thirds):
                    nt = len(tl)
                    ntok = nt * 128
                    groups = [(0, 384), (384, ntok - 384)]
                    for c in range(NFC):
                        slot = ring_next()
                        W = ring[:, slot, :].rearrange("p (k f) -> p k f", k=8)
                        for (g0, N) in groups:
                            gb = 2 + 2 * (it % 2)
                            ub = gb + 1
                            S.op("pe", [lambda e, k=k, W=W, g0=g0, N=N, gb=gb: e.matmul(
                                ps[:, gb, 0:N], lhsT=W[:, k, 0:128], rhs=hT[:, k, g0:g0 + N], start=(k == 0), stop=(k == 7))
                                for k in range(8)], rd=[B_ring[slot], B_hT], wr=[B_ps[gb]])
                            S.op("pe", [lambda e, k=k, W=W, g0=g0, N=N, ub=ub: e.matmul(
                                ps[:, ub, 0:N], lhsT=W[:, k, 128:256], rhs=hT[:, k, g0:g0 + N], start=(k == 0), stop=(k == 7))
                                for k in range(8)], rd=[B_ring[slot], B_hT], wr=[B_ps[ub]])
                            s2 = it % 2
                            S.op("act", lambda e, s2=s2, gb=gb, N=N: e.activation(
                                out=sg[:, s2, 0:N], in_=ps[:, gb, 0:N], func=AF.Silu), rd=[B_ps[gb]], wr=[B_sg[s2]])
                            S.op("dve", lambda e, s2=s2, ub=ub, N=N, c=c, g0=g0: e.tensor_tensor(
                                out=actT[:, c, g0:g0 + N], in0=ps[:, ub, 0:N], in1=sg[:, s2, 0:N], op=ALU.mult),
                                rd=[B_ps[ub], B_sg[s2]], wr=[B_actT])
                            it += 1
                        ring_issue()
                    for ti, i in enumerate(tl):
                        yb = 2 + 2 * (ydb % 3)
                        ydb += 1
                        fns = []
                        for c in range(NFC):
                            for half in range(2):
                                fns.append(lambda e, c=c, half=half, ti=ti, yb=yb: e.matmul(
                                    ps[:, yb + half, :], lhsT=actT[:, c, ti * 128:(ti + 1) * 128],
                                    rhs=wd[:, c, half * 512:(half + 1) * 512], start=(c == 0), stop=(c == NFC - 1)))
                        S.op("pe", fns, rd=[B_actT, B_wd], wr=[B_ps[yb], B_ps[yb + 1]])
                        if k3 + 1 < 3 and ti < len(thirds[k3 + 1]):
                            prenorm_tile(k3 + 1, ti)
                        yv = ps[:, yb:yb + 2, :]
                        S.op("act", lambda e, yv=yv: e.activation(
                            out=junk[:].rearrange("p (a b) -> p a b", a=2), in_=yv, func=AF.Square, accum_out=st[:, 20:21]),
                            rd=[B_ps[yb], B_ps[yb + 1]], wr=[B_junk, B_st])
                        rstd_batch(1, 20, 21, 22)
                        S.op("dve", lambda e, yv=yv: e.scalar_tensor_tensor(
                            out=t2.rearrange("p (a b) -> p a b", a=2), in0=yv, scalar=st[:, 22:23],
                            in1=tab[:, 1, :].rearrange("p (a b) -> p a b", a=2), op0=ALU.mult, op1=ALU.mult),
                            rd=[B_ps[yb], B_ps[yb + 1], B_st, B_tab[1]], wr=[B_t2])
                        S.op("dve", lambda e, i=i: e.tensor_tensor(out=hs[:, i, :], in0=hs[:, i, :], in1=t2, op=ALU.add),
                             rd=[B_t2], wr=[B_hs[i]])
                        if store:
                            S.dma("sp", "d_out", [(outd[(i - 1) * 128:i * 128, :], hs[:, i, :])], rd=[B_hs[i]], wr=[B_out])

            if not os.environ.get('SKIPFFN'):
                ffn(0, list(range(NHS)), False)
            checkpoint(2)

            bt = barrier()
            cv = Carver()
            KT = cv.take(BF16, [128, 2, 2, NHS * 128])
            VA = cv.take(BF16, [128, NHS, 2, 65])
            kv_end = cv.off
            wk = cv.take(BF16, [128, 8, 256])
            wv = cv.take(BF16, [128, 8, 128])
            hT = cv.take(BF16, [128, 8, 512])
            hb = cv.take(BF16, [128, 2, D])
            B_KT, B_VA, B_wkv, B_hT = Buf("KT"), Buf("VA"), Buf("wkv"), Buf("hT")
            B_hb = [Buf("hb0"), Buf("hb1")]
            S.dma("pool", "d_wkv", [(wk.rearrange("p a b -> p (a b)"), wst[CH_WK]),
                                    (wv.rearrange("p a b -> p (a b)"), wst[CH_WV][:, 0:1024])], wr=[B_wkv], extra=bt)
            load_tab(0, T_KV, extra=bt)
            load_tab(1, T_BV, extra=bt)
            KVD = int(os.environ.get('KVDBG', '9'))
            if KVD >= 1:
                S.op("dve", lambda e: e.memset(VA[:, :, :, 64:65], 1.0), wr=[B_VA], extra=bt)
            S.op("dve", lambda e: e.memset(KT.rearrange("p a b c -> p (a b c)"), 0.0), wr=[B_KT], extra=bt)
            for i in range(NHS):
                sumsq(hs[:, i, :], i, [B_hs[i]])
            rstd_batch(NHS, 0, 32, 64)
            vcnt = 0
            for g0t in range(0, NHS, 4):
                tl = list(range(g0t, min(g0t + 4, NHS)))
                N = len(tl) * 128
                for ti, i in enumerate(tl):
                    p2 = ti % 2
                    S.op("dve", lambda e, i=i, p2=p2: e.scalar_tensor_tensor(
                        out=hb[:, p2, :], in0=hs[:, i, :], scalar=st[:, 64 + i:65 + i], in1=tab[:, 0, :],
                        op0=ALU.mult, op1=ALU.mult), rd=[B_hs[i], B_st, B_tab[0]], wr=[B_hb[p2]])
                    tpv = psbf(p2)
                    S.op("pe", [lambda e, k=k, p2=p2, tpv=tpv: e.transpose(
                        out=tpv[:, k, :], in_=hb[:, p2, k * 128:(k + 1) * 128], identity=ident) for k in range(8)],
                        rd=[B_hb[p2], B_bands], wr=[B_ps[p2]])
                    S.op("act", lambda e, ti=ti, tpv=tpv: e.activation(
                        out=hT[:, :, ti * 128:(ti + 1) * 128], in_=tpv, func=AF.Copy), rd=[B_ps[p2]], wr=[B_hT])
                for kvh in range(2 if KVD >= 3 else 0):
                    kb = 2 + kvh
                    S.op("pe", [lambda e, k=k, kvh=kvh, kb=kb, N=N: e.matmul(
                        ps[:, kb, 0:N], lhsT=wk[:, k, kvh * 128:(kvh + 1) * 128], rhs=hT[:, k, 0:N],
                        start=(k == 0), stop=(k == 7)) for k in range(8)], rd=[B_wkv, B_hT], wr=[B_ps[kb]])
                    for e_ in range(2):
                        S.op("act", lambda e, kvh=kvh, kb=kb, N=N, g0t=g0t, e_=e_: e.activation(
                            out=KT[64 * e_:64 * e_ + 64, kvh, e_, g0t * 128:g0t * 128 + N],
                            in_=ps[64 * e_:64 * e_ + 64, kb, 0:N], func=AF.Identity,
                            bias=cst[64 * e_:64 * e_ + 64, 8 + kvh:9 + kvh]), rd=[B_ps[kb], B_cst], wr=[B_KT])
                for ti, i in enumerate(tl if KVD >= 4 else []):
                    vb = 4 + (vcnt % 2)
                    vcnt += 1
                    S.op("pe", [lambda e, k=k, ti=ti, vb=vb: e.matmul(
                        ps[:, vb, 0:128], lhsT=hT[:, k, ti * 128:(ti + 1) * 128], rhs=wv[:, k, :],
                        start=(k == 0), stop=(k == 7)) for k in range(8)], rd=[B_wkv, B_hT], wr=[B_ps[vb]])
                    S.op("dve", lambda e, i=i, vb=vb: e.tensor_tensor(
                        out=VA[:, i, :, 0:64], in0=ps[:, vb, 0:128].rearrange("p (a b) -> p a b", a=2),
                        in1=tab[:, 1, 0:128].rearrange("p (a b) -> p a b", a=2), op=ALU.add),
                        rd=[B_ps[vb], B_tab[1]], wr=[B_VA])

            checkpoint(3)
            bt = barrier()
            cv = Carver()
            cv.off = kv_end
            wo = cv.take(BF16, [128, 8, D])
            bias = cv.take(F32, [128, 16, 2, 128])
            PT = cv.take(BF16, [128, 16, 2, 128])
            lg = cv.take(F32, [128, 2, 512])
            hT4 = cv.take(BF16, [128, 8, 512])
            QT4 = cv.take(BF16, [128, 8, 512])
            oT4 = cv.take(BF16, [128, 4, 8, 128])
            hb = cv.take(BF16, [128, 2, D])
            obf = cv.take(BF16, [128, D])
            t2 = cv.take(F32, [128, D])
            bohl = cv.take(BF16, [128, 2, D])
            ones = cv.take(BF16, [128, 128])
            B_wo, B_bias, B_PTx, B_hT4, B_QT4, B_obf, B_t2, B_bohl = (Buf(n) for n in
                                                                      ("wo", "bias", "PT", "hT4", "QT4", "obf", "t2", "bohl"))
            B_PT = [Buf("PT%d" % h_) for h_ in range(8)]
            B_oT4 = [Buf("oT4_%d" % t) for t in range(4)]
            B_lg = [Buf("lg0"), Buf("lg1")]
            B_hb = [Buf("hb0"), Buf("hb1")]
            B_esink, B_stb = Buf("esink"), Buf("stb")
            B_pn = [Buf("pn0"), Buf("pn1")]
            B_den = [Buf("den0"), Buf("den1")]
            S.dma("pool", "d_wqo", [(wo[:, 2 * j:2 * j + 2, :].rearrange("p a b -> p (a b)"), wst[CH_WO + j]) for j in range(4)],
                  wr=[B_wo], extra=bt)
            S.dma("sp", "d_bias", [(bias.rearrange("p a b c -> p (a b c)"), btabd[0])], wr=[B_bias], extra=bt)
            load_tab(0, T_MIXPRE1, extra=bt)
            load_tab(1, T_MIXPOST1, extra=bt)
            load_tab(2, T_BO, extra=bt)
            S.op("dve", lambda e: e.memset(ones, 1.0), wr=[B_bohl], extra=bt)
            S.op("dve", lambda e: e.tensor_copy(out=bohl[:, 0, :], in_=tab[:, 2, :]), rd=[B_tab[2]], wr=[B_bohl])
            S.op("dve", lambda e: e.tensor_tensor(out=t2, in0=tab[:, 2, :], in1=bohl[:, 0, :], op=ALU.subtract),
                 rd=[B_tab[2], B_bohl], wr=[B_t2])
            S.op("dve", lambda e: e.tensor_copy(out=bohl[:, 1, :], in_=t2), rd=[B_t2], wr=[B_bohl])
            S.op("act", lambda e: e.activation(out=st[:, 96:112], in_=cst[:, 10:26], func=AF.Exp),
                 rd=[B_cst], wr=[B_esink], extra=bt)
            for i in range(1, NHS):
                sumsq(hs[:, i, :], i, [B_hs[i]], bst=B_stb)
            rstd_batch(NHS, 0, 32, 64, bst=B_stb)
            hsplit = [(0, 7), (7, 14), (14, 16)]

            def q_stage(g):
                for ti in range(4):
                    i = 1 + 4 * g + ti
                    p2 = ti % 2
                    tb = 6 + p2
                    S.op("dve", lambda e, i=i, p2=p2: e.scalar_tensor_tensor(
                        out=hb[:, p2, :], in0=hs[:, i, :], scalar=st[:, 64 + i:65 + i], in1=tab[:, 0, :],
                        op0=ALU.mult, op1=ALU.mult), rd=[B_hs[i], B_stb, B_tab[0]], wr=[B_hb[p2]])
                    tpv = psbf(tb)
                    S.op("pe", [lambda e, k=k, p2=p2, tpv=tpv: e.transpose(
                        out=tpv[:, k, :], in_=hb[:, p2, k * 128:(k + 1) * 128], identity=ident) for k in range(8)],
                        rd=[B_hb[p2], B_bands], wr=[B_ps[tb]])
                    S.op("act", lambda e, ti=ti, tpv=tpv: e.activation(
                        out=hT4[:, :, ti * 128:(ti + 1) * 128], in_=tpv, func=AF.Copy), rd=[B_ps[tb]], wr=[B_hT4])
                for j in range(4):
                    slot = ring_next()
                    W = ring[:, slot, :].rearrange("p (k f) -> p k f", k=8)
                    for e2 in range(2):
                        hp = 2 * j + e2
                        qb = 6 + hp % 2
                        S.op("pe", [lambda e, k=k, W=W, e2=e2, qb=qb: e.matmul(
                            ps[:, qb, :], lhsT=W[:, k, e2 * 128:(e2 + 1) * 128], rhs=hT4[:, k, :],
                            start=(k == 0), stop=(k == 7)) for k in range(8)], rd=[B_ring[slot], B_hT4], wr=[B_ps[qb]])
                        if hp % 2 == 0:
                            S.op("act", lambda e, hp=hp, qb=qb: e.activation(
                                out=QT4[:, hp, :], in_=ps[:, qb, :], func=AF.Identity, bias=cst[:, hp:hp + 1]),
                                rd=[B_ps[qb], B_cst], wr=[B_QT4])
                        else:
                            S.op("dve", lambda e, hp=hp, qb=qb: e.tensor_scalar(
                                out=QT4[:, hp, :], in0=ps[:, qb, :], scalar1=cst[:, hp:hp + 1], scalar2=None, op0=ALU.add),
                                rd=[B_ps[qb], B_cst], wr=[B_QT4])
                    ring_issue()

            def s_tile(g, ti, hook=None, filler=None):
                i = 1 + 4 * g + ti
                p2 = i % 2
                den = st[:, 112 + 16 * p2:128 + 16 * p2]

                def scores(hp):
                    kvh = hp // 4
                    sbk = 1 + hp % 2
                    sv = ps[:, sbk, :].rearrange("p (e b q) -> p e b q", e=2, b=2)
                    fns = []
                    for e_ in range(2):
                        for blk in range(2):
                            kt = i - 1 + blk
                            fns.append(lambda e, e_=e_, blk=blk, kt=kt: e.matmul(
                                sv[:, e_, blk, :], lhsT=KT[:, kvh, e_, kt * 128:(kt + 1) * 128],
                                rhs=QT4[:, hp, ti * 128:(ti + 1) * 128], start=True, stop=True))
                    S.op("pe", fns, rd=[B_KT, B_QT4], wr=[B_ps[sbk]])

                def c1_bank(bi):
                    h0, h1 = hsplit[bi]
                    nh = h1 - h0
                    ov = ps[:, 3 + bi, 0:nh * 65].rearrange("p (h c) -> p h c", c=65)
                    S.op("dve", lambda e: e.tensor_tensor(
                        out=den[:, h0:h1], in0=ov[:, :, 64], in1=st[:, 96 + h0:96 + h1], op=ALU.add),
                        rd=[B_ps[3 + bi], B_esink], wr=[B_den[p2]])
                    S.op("dve", lambda e: e.reciprocal(out=den[:, h0:h1], in_=den[:, h0:h1]), rd=[], wr=[B_den[p2]])
                    S.op("dve", lambda e: e.tensor_tensor(
                        out=obf.rearrange("p (h c) -> p h c", c=64)[:, h0:h1, :], in0=ov[:, :, 0:64],
                        in1=den[:, h0:h1].unsqueeze(2).to_broadcast([128, nh, 64]), op=ALU.mult),
                        rd=[B_ps[3 + bi], B_den[p2]], wr=[B_obf])

                scores(0)
                scores(1)
                for hp in range(8):
                    kvh = hp // 4
                    sbk = 1 + hp % 2
                    l2 = hp % 2
                    S.op("dve", lambda e, hp=hp, sbk=sbk, l2=l2: e.scalar_tensor_tensor(
                        out=lg[:, l2, :], in0=ps[:, sbk, :], scalar=0.125,
                        in1=bias[:, 2 * hp:2 * hp + 2, :, :].rearrange("p a b c -> p (a b c)"), op0=ALU.mult, op1=ALU.add),
                        rd=[B_ps[sbk], B_bias], wr=[B_lg[l2]])
                    S.op("act", lambda e, hp=hp, l2=l2: e.activation(
                        out=PT[:, 2 * hp:2 * hp + 2, :, :].rearrange("p a b c -> p (a b c)"), in_=lg[:, l2, :], func=AF.Exp),
                        rd=[B_lg[l2]], wr=[B_PT[hp]])
                    if hp + 2 < 8:
                        scores(hp + 2)
                    fns = []
                    for e_ in range(2):
                        h = 2 * hp + e_
                        ob = 3 + h // 7
                        oc = (h % 7) * 65
                        for blk in range(2):
                            kt = i - 1 + blk
                            fns.append(lambda e, h=h, ob=ob, oc=oc, blk=blk, kt=kt, kvh=kvh: e.matmul(
                                ps[:, ob, oc:oc + 65], lhsT=PT[:, h, blk, :], rhs=VA[:, kt, kvh, :],
                                start=(blk == 0), stop=(blk == 1)))
                    S.op("pe", fns, rd=[B_PT[hp], B_VA],
                         wr=[B_ps[b_] for b_ in sorted({3 + (2 * hp) // 7, 3 + (2 * hp + 1) // 7})])
                    if filler is not None:
                        filler(hp)
                    if hp == 1 and hook is not None:
                        hook()
                    if hp == 4:
                        c1_bank(0)
                    if hp == 7:
                        c1_bank(1)
                if i == 1:
                    S.dma("sp", "d_bias", [(bias.rearrange("p a b c -> p (a b c)"), btabd[1])], wr=[B_bias])

                def tail():
                    c1_bank(2)
                    tpv = psbf(0)
                    S.op("pe", [lambda e, k=k: e.transpose(
                        out=tpv[:, k, :], in_=obf[:, k * 128:(k + 1) * 128], identity=ident) for k in range(8)],
                        rd=[B_obf, B_bands], wr=[B_ps[0]])
                    S.op("act", lambda e: e.activation(out=oT4[:, ti], in_=tpv, func=AF.Copy), rd=[B_ps[0]], wr=[B_oT4[ti]])
                return tail

            def o_fns(ti, ybase, half):
                fns = [lambda e, k=k: e.matmul(
                    ps[:, ybase + half, :], lhsT=oT4[:, ti, k, :], rhs=wo[:, k, half * 512:(half + 1) * 512],
                    start=(k == 0), stop=False) for k in range(8)]
                for hl in range(2):
                    fns.append(lambda e, hl=hl: e.matmul(
                        ps[:, ybase + half, :], lhsT=ones[0:1, :], rhs=bohl[0:1, hl, half * 512:(half + 1) * 512],
                        start=False, stop=(hl == 1)))
                return fns

            O_CUTS = [(0, 3), (3, 6), (6, 8), (8, 10)]

            def o_piece(g, ti, hp):
                half = hp // 4
                a_, b_ = O_CUTS[hp % 4]
                fns = o_fns(ti, 6, half)[a_:b_]
                S.op("pe", fns, rd=[B_oT4[ti], B_wo, B_bohl], wr=[B_ps[6 + half]])

            def o_tile(g, ti, ybase):
                for half in range(2):
                    S.op("pe", o_fns(ti, ybase, half), rd=[B_oT4[ti], B_wo, B_bohl], wr=[B_ps[ybase + half]])
                o_post(g, ti, ybase)

            def o_post(g, ti, ybase):
                i = 1 + 4 * g + ti
                p2 = ti % 2
                c0 = 20 + 4 * p2
                yv = ps[:, ybase:ybase + 2, :]
                S.op("act", lambda e: e.activation(
                    out=junk[:].rearrange("p (a b) -> p a b", a=2), in_=yv, func=AF.Square,
                    accum_out=st[:, c0:c0 + 1]), rd=[B_ps[ybase], B_ps[ybase + 1]], wr=[B_junk, B_pn[p2]])
                rstd_batch(1, c0, c0 + 1, c0 + 2, bst=B_pn[p2])
                S.op("dve", lambda e: e.scalar_tensor_tensor(
                    out=t2.rearrange("p (a b) -> p a b", a=2), in0=yv, scalar=st[:, c0 + 2:c0 + 3],
                    in1=tab[:, 1, :].rearrange("p (a b) -> p a b", a=2), op0=ALU.mult, op1=ALU.mult),
                    rd=[B_ps[ybase], B_ps[ybase + 1], B_pn[p2], B_tab[1]], wr=[B_t2])
                S.op("pool", lambda e: e.tensor_tensor(out=hs[:, i, :], in0=hs[:, i, :], in1=t2, op=ALU.add),
                     rd=[B_t2], wr=[B_hs[i]])

            q_stage(0)
            for g in range(4):
                tail = None
                for ti in range(4):
                    filler = (lambda hp, g=g, ti=ti: o_piece(g - 1, ti, hp)) if g > 0 else None
                    tail = s_tile(g, ti, hook=tail, filler=filler)
                    if g > 0:
                        o_post(g - 1, ti, 6)
                tail()
                if g + 1 < 4:
                    q_stage(g + 1)
            for t_ in range(4):
                o_tile(3, t_, 6 if t_ % 2 == 0 else 1)

            checkpoint(4)
            ffn(1, list(range(1, NHS)), True)


        except _Stop:
            pass
        S.wait("sp", B_out.w)

        with nc.Block() as block:
            @block.tensor
            def _(e):
                for f in S.ops["pe"]:
                    f(e)

            @block.scalar
            def _(e):
                for f in S.ops["act"]:
                    f(e)

            @block.vector
            def _(e):
                for f in S.ops["dve"]:
                    f(e)

            @block.gpsimd
            def _(e):
                for f in S.ops["pool"]:
                    f(e)

            @block.sync
            def _(e):
                for f in S.ops["sp"]:
                    f(e)
    return nc


def _t5_bucket(d):
    max_exact = 16
    df = np.maximum(d, 1).astype(np.float32)
    large = max_exact + (np.log(df / max_exact) / np.log(128 / max_exact) * (32 - max_exact)).astype(np.int32)
    large = np.minimum(large, 31)
    return np.where(d < max_exact, d, large)


def _band_consts(first_core):
    out = np.zeros((128, 13, 128), np.float32)
    tp = np.arange(128)[:, None]
    t = np.arange(128)[None, :]
    for wi, w in enumerate((2, 4, 8, 16)):
        dlt = t - tp
        own = np.where((dlt >= 0) & (dlt < w), 1.0 / w, 0.0) - (dlt == 0)
        prev = np.where((t + 128 - tp) < w, 1.0 / w, 0.0)
        out[:, wi, :] = own
        out[:, 4 + wi, :] = prev
        if first_core:
            pos = np.maximum(t - 112 + 1, 1)
            cnt = np.minimum(pos, w).astype(np.float32)
            valid = (dlt >= 0) & (dlt < w) & (tp >= 112)
            out[:, 8 + wi, :] = np.where(valid, 1.0 / cnt, 0.0) - (dlt == 0)
        else:
            out[:, 8 + wi, :] = own
    out[:, 12, :] = np.eye(128, dtype=np.float32)
    return out.reshape(128, 13 * 128)


def _bias_tables(rel_bias, first_core):
    s = np.arange(128)[:, None]
    q = np.arange(128)[None, :]
    tabs = np.empty((2, 128, 16, 2, 128), np.float32)
    for blk in range(2):
        d = q - s + (128 if blk == 0 else 0)
        ok = (d >= 0) & (d < 128)
        bidx = _t5_bucket(np.maximum(d, 0))
        g = rel_bias[bidx]
        g = np.where(ok[:, :, None], g, np.float32(NEG))
        g = np.transpose(g, (0, 2, 1))
        tabs[1, :, :, blk, :] = g
        if blk == 0 and first_core:
            g = np.where((s >= 112)[:, :, None] & np.ones((1, 1, 128), bool), g, np.float32(NEG))
        tabs[0, :, :, blk, :] = g
    return tabs.reshape(2, 128, 16 * 2 * 128)


def _prep_common(inp):
    f = np.float32
    wst = np.zeros((NCHUNK, 128, 2048), f)
    pw = inp["pool_w"][0]
    wst[CH_POOL] = pw.reshape(4, 2, 128, 256).transpose(2, 0, 1, 3).reshape(128, 2048)
    for l in range(2):
        gu = inp["w_gate_up"][l]
        gate = gu[:, :DFF].reshape(8, 128, NFC, 128)
        up = gu[:, DFF:].reshape(8, 128, NFC, 128)
        blk = np.stack([gate, up], axis=3)
        blk = blk.transpose(2, 1, 0, 3, 4).reshape(NFC, 128, 2048)
        base = CH_GU0 if l == 0 else CH_GU1
        wst[base:base + NFC] = blk
        wdn = inp["w_down"][l].reshape(11, 2, 128, D).transpose(0, 2, 1, 3).reshape(11, 128, 2048)
        base = CH_WD0 if l == 0 else CH_WD1
        wst[base:base + 11] = wdn
    wk = inp["w_k"].reshape(8, 128, 2, 64)
    wkd = np.stack([wk, wk], axis=3)
    wst[CH_WK] = wkd.transpose(1, 0, 2, 3, 4).reshape(128, 2048)
    wst[CH_WV][:, :1024] = inp["w_v"].reshape(8, 128, 128).transpose(1, 0, 2).reshape(128, 1024)
    wq = inp["w_q"][0].reshape(8, 128, 4, 256)
    wst[CH_WQ:CH_WQ + 4] = wq.transpose(2, 1, 0, 3).reshape(4, 128, 2048)
    wo = inp["w_o"][0].reshape(4, 2, 128, D)
    wst[CH_WO:CH_WO + 4] = wo.transpose(0, 2, 1, 3).reshape(4, 128, 2048)

    rows = [inp["norm_mix_pre"][0], inp["pool_scale"][0], inp["norm_mix_post"][0], inp["norm_ffn_pre"][0],
            inp["norm_ffn_post"][0], inp["kv_norm"], inp["norm_mix_pre"][1], inp["norm_mix_post"][1],
            inp["b_o"][0], inp["norm_ffn_pre"][1], inp["norm_ffn_post"][1],
            np.concatenate([inp["b_v"], np.zeros(D - 128, f)])]
    tabs = np.ascontiguousarray(np.broadcast_to(np.stack(rows)[:, None, :], (NTAB, 128, D))).astype(f)

    cst = np.zeros((128, NCST), f)
    cst[:, 0:8] = inp["b_q"][0].reshape(8, 128).T
    bk = inp["b_k"].reshape(2, 64)
    cst[:, 8:10] = np.concatenate([bk, bk], axis=1).T
    cst[:, 10:26] = np.broadcast_to(inp["sinks"][0][None, :], (128, 16))
    cst[:, 26] = EPS
    return wst, tabs, cst


_NC_CACHE = {}


def kernel(**inputs):
    inp = {k: np.asarray(v, dtype=np.float32) for k, v in inputs.items()}
    x = inp["x"]
    meta = inp["meta_tokens"]
    wst, tabs, cst = _prep_common(inp)
    bands = [_band_consts(True), _band_consts(False)]
    btab = [_bias_tables(inp["rel_bias"], True), _bias_tables(inp["rel_bias"], False)]
    in_maps = []
    for c in range(8):
        b, r = c // 4, c % 4
        xin = np.zeros((NXT * 128, D), np.float32)
        if r == 0:
            xin[128 + 112:256] = meta
            xin[256:] = x[b, 0:2048]
        else:
            xin[:] = x[b, 2048 * r - 256:2048 * r + 2048]
        fc = (r == 0)
        in_maps.append({"xin": xin, "wst": wst, "tabs": tabs, "cst": cst,
                        "bands": bands[0 if fc else 1], "btab": btab[0 if fc else 1]})
    if "nc" not in _NC_CACHE:
        _NC_CACHE["nc"] = build_program()
    nc = _NC_CACHE["nc"]
    res = run_bass_kernel_spmd(nc, in_maps, core_ids=list(range(8)))
    out = np.empty((2, 8192, D), np.float32)
    for c in range(8):
        b, r = c // 4, c % 4
        out[b, 2048 * r:2048 * r + 2048] = res.results[c]["out"]
    return out
```

```python
import os
import numpy as np
from contextlib import ExitStack
import concourse.bass as bass
import concourse.mybir as mybir
from concourse.bass_utils import run_bass_kernel_spmd

F32 = mybir.dt.float32
BF16 = mybir.dt.bfloat16
AF = mybir.ActivationFunctionType
ALU = mybir.AluOpType

D = 1024
DFF = 2816
NFC = DFF // 128
NXT = 18
NHS = 17
EPS = 1e-6
NSLOT = 4
NEG = -30000.0

CH_POOL = 0
CH_GU0 = 1
CH_WD0 = CH_GU0 + 22
CH_WK = CH_WD0 + 11
CH_WV = CH_WK + 1
CH_WQ = CH_WV + 1
CH_WO = CH_WQ + 4
CH_GU1 = CH_WO + 4
CH_WD1 = CH_GU1 + 22
NCHUNK = CH_WD1 + 11

(T_MIXPRE0, T_PSCALE, T_MIXPOST0, T_FFNPRE0, T_FFNPOST0, T_KV, T_MIXPRE1, T_MIXPOST1, T_BO,
 T_FFNPRE1, T_FFNPOST1, T_BV) = range(12)
NTAB = 12
NCST = 32


class Buf:
    def __init__(self, name):
        self.name = name
        self.w = None
        self.r = {}


class Sched:
    QS = ("pe", "act", "dve", "pool", "sp")

    def __init__(self, nc, es):
        self.nc = nc
        self.es = es
        self.ops = {k: [] for k in self.QS}
        self.sem = {}
        self.count = {k: 0 for k in self.QS}
        self.seen = {k: {} for k in self.QS}
        self.dcount = {}
        for q in self.QS:
            self.getsem(q)

    def getsem(self, name):
        if name not in self.sem:
            self.sem[name] = self.es.enter_context(self.nc.semaphore(name))
        return self.sem[name]

    def wait(self, q, tok):
        if tok is None:
            return
        sname, val = tok
        if self.seen[q].get(sname, 0) >= val:
            return
        self.seen[q][sname] = val
        sem = self.getsem(sname)
        self.ops[q].append(lambda e, sem=sem, val=val: e.wait_ge(sem, val))

    def _deps(self, q, rd, wr, extra):
        for t in extra:
            self.wait(q, t)
        for b in rd:
            self.wait(q, b.w)
        for b in wr:
            self.wait(q, b.w)
            for sn, v in b.r.items():
                self.wait(q, (sn, v))

    def _mark(self, tok, rd, wr):
        for b in rd:
            if b.r.get(tok[0], 0) < tok[1]:
                b.r[tok[0]] = tok[1]
        for b in wr:
            b.w = tok
            b.r = {}

    def op(self, q, fn, rd=(), wr=(), extra=()):
        self._deps(q, rd, wr, extra)
        fns = fn if isinstance(fn, (list, tuple)) else [fn]
        for f in fns[:-1]:
            self.ops[q].append(lambda e, f=f: f(e))
        self.count[q] += 1
        sem = self.sem[q]
        last = fns[-1]
        self.ops[q].append(lambda e, f=last, sem=sem: f(e).then_inc(sem, 1))
        tok = (q, self.count[q])
        self._mark(tok, rd, wr)
        return tok

    def dma(self, q, semname, pairs, rd=(), wr=(), extra=()):
        self._deps(q, rd, wr, extra)
        sem = self.getsem(semname)
        for out, in_ in pairs:
            self.ops[q].append(lambda e, out=out, in_=in_, sem=sem: e.dma_start(out=out, in_=in_).then_inc(sem, 16))
        self.dcount[semname] = self.dcount.get(semname, 0) + 16 * len(pairs)
        tok = (semname, self.dcount[semname])
        self._mark(tok, rd, wr)
        return tok


class _Stop(Exception):
    pass


def build_program(stop=99):
    nc = bass.Bass("TRN2", target_bir_lowering=False)
    xin = nc.dram_tensor("xin", [NXT * 128, D], F32, kind="ExternalInput").ap()
    wst = nc.dram_tensor("wst", [NCHUNK, 128, 2048], F32, kind="ExternalInput").ap()
    tabs = nc.dram_tensor("tabs", [NTAB, 128, D], F32, kind="ExternalInput").ap()
    cstd = nc.dram_tensor("cst", [128, NCST], F32, kind="ExternalInput").ap()
    bandd = nc.dram_tensor("bands", [128, 13 * 128], F32, kind="ExternalInput").ap()
    btabd = nc.dram_tensor("btab", [2, 128, 16 * 2 * 128], F32, kind="ExternalInput").ap()
    outd = nc.dram_tensor("out", [16 * 128, D], F32, kind="ExternalOutput").ap()
    dbgd = nc.dram_tensor("dbg", [NHS * 128, D], F32, kind="ExternalOutput").ap() if stop != 99 else None

    with ExitStack() as es:
        S = Sched(nc, es)

        def sb(name, shape, dt):
            return es.enter_context(nc.sbuf_tensor(name, shape, dt))

        hs = sb("hs", [128, NHS, D], F32)
        tab = sb("tab", [128, 3, D], F32)
        ring = sb("ring", [128, NSLOT, 2048], BF16)
        bands = sb("bandsb", [128, 13, 128], BF16)
        cst = sb("cstsb", [128, NCST], F32)
        st = sb("st", [128, 160], F32)
        junk = sb("junk", [128, D], BF16)
        PH_ELEMS = 52 * 1024
        phase = sb("phase", [128, PH_ELEMS], BF16)
        ps = es.enter_context(nc.psum_tensor("ps", [128, 8, 512], F32))

        ident = bands[:, 12, :]
        B_hs = [Buf("hs%d" % i) for i in range(NHS)]
        B_tab = [Buf("tab%d" % i) for i in range(3)]
        B_ring = [Buf("ring%d" % i) for i in range(NSLOT)]
        B_bands = Buf("bands")
        B_cst = Buf("cst")
        B_st = Buf("st")
        B_junk = Buf("junk")
        B_ps = [Buf("ps%d" % i) for i in range(8)]
        B_out = Buf("out")

        def psbf(b):
            return ps[:, b, :].bitcast(BF16).rearrange("p (k t) -> p k t", k=8)

        class Carver:
            def __init__(self):
                self.off = 0

            def take(self, dt, shape):
                n = int(np.prod(shape[1:]))
                nb = n if dt == BF16 else 2 * n
                nb = (nb + 15) // 16 * 16
                assert self.off + nb <= PH_ELEMS, ("phase overflow", self.off + nb)
                v = phase[:, self.off:self.off + nb]
                self.off += nb
                if dt == F32:
                    v = v.bitcast(F32)
                v = v[:, 0:n]
                if len(shape) == 3:
                    v = v.rearrange("p (a b) -> p a b", a=shape[1])
                elif len(shape) == 4:
                    v = v.rearrange("p (a b c) -> p a b c", a=shape[1], b=shape[2])
                return v

        def barrier():
            toks = [(q, S.count[q]) for q in ("pe", "act", "dve", "pool") if S.count[q] > 0]
            for q in ("pe", "act", "dve"):
                for t in toks:
                    if t[0] != q:
                        S.wait(q, t)
            return toks

        def load_tab(slot, idx, extra=()):
            return S.dma("sp", "d_tab%d" % slot, [(tab[:, slot, :], tabs[idx])], wr=[B_tab[slot]], extra=extra)

        ring_seq = []
        for l in range(2):
            if l == 1:
                for g_ in range(4):
                    for j_ in range(4):
                        ring_seq.append(CH_WQ + j_)
            for third in range(3):
                for c in range(NFC):
                    ring_seq.append((CH_GU0 if l == 0 else CH_GU1) + c)
        ring_state = {"issued": 0, "used": 0}

        def ring_issue():
            i = ring_state["issued"]
            if i >= len(ring_seq):
                return
            slot = i % NSLOT
            S.dma("pool", "d_ring%d" % slot, [(ring[:, slot, :], wst[ring_seq[i]])], wr=[B_ring[slot]])
            ring_state["issued"] += 1

        def ring_next():
            i = ring_state["used"]
            ring_state["used"] += 1
            return i % NSLOT

        def rstd_batch(n, col_ss, col_sd, col_r, bst=None):
            bst = bst or B_st
            S.op("act", lambda e: e.activation(out=st[:, col_sd:col_sd + n], in_=st[:, col_ss:col_ss + n],
                                                func=AF.Sqrt, bias=cst[:, 26:27], scale=1.0 / D),
                 rd=[B_cst], wr=[bst])
            S.op("dve", lambda e: e.reciprocal(out=st[:, col_r:col_r + n], in_=st[:, col_sd:col_sd + n]), wr=[bst])

        def sumsq(src, col, rd, bst=None):
            S.op("act", lambda e: e.activation(out=junk[:], in_=src, func=AF.Square, accum_out=st[:, col:col + 1]),
                 rd=rd, wr=[B_junk, bst or B_st])

        def checkpoint(n):
            if stop == n:
                bt_ = barrier()
                for i_ in range(NHS):
                    S.dma("sp", "d_dbg", [(dbgd[i_ * 128:(i_ + 1) * 128, :], hs[:, i_, :])], rd=[B_hs[i_]], wr=[B_out], extra=bt_)
                raise _Stop()

        try:
            B_pre = Buf("pre")
            B_ph = Buf("phase_generic")
            cv = Carver()
            hb_all = cv.take(BF16, [128, NXT, D])
            pre = cv.take(F32, [128, D])
            pTs = cv.take(BF16, [128, 2, 8, 128])
            ysb = cv.take(F32, [128, 2, D])
            t2b = cv.take(F32, [128, 2, D])
            poolw = cv.take(BF16, [128, 4, 2, 256])
            B_hb = [Buf("hb%d" % j) for j in range(NXT)]
            B_pTs = [Buf("pTs0"), Buf("pTs1")]
            B_ys = [Buf("ys0"), Buf("ys1")]
            B_t2 = Buf("t2")
            B_poolw = Buf("poolw")

            S.dma("sp", "d_cst", [(cst[:], cstd[:])], wr=[B_cst])
            load_tab(0, T_MIXPRE0)
            load_tab(1, T_PSCALE)
            load_tab(2, T_MIXPOST0)
            for j in range(NXT):
                dst = pre if j == 0 else hs[:, j - 1, :]
                S.dma("sp", "d_x%d" % j, [(dst, xin[j * 128:(j + 1) * 128, :])], wr=[B_pre if j == 0 else B_hs[j - 1]])
            S.dma("pool", "d_bands", [(bands[:].rearrange("p a b -> p (a b)"), bandd[:])], wr=[B_bands])
            S.dma("pool", "d_poolw", [(poolw.rearrange("p a b c -> p (a b c)"), wst[CH_POOL])], wr=[B_poolw])
            for _ in range(NSLOT):
                ring_issue()

            checkpoint(0)
            def xsrc(j):
                return (pre, B_pre) if j == 0 else (hs[:, j - 1, :], B_hs[j - 1])

            B_s1 = [Buf("s1_%d" % j) for j in range(NXT)]
            B_s3 = [Buf("s3_0"), Buf("s3_1")]
            B_t2p = [Buf("t2p0"), Buf("t2p1")]

            def stage1(j):
                src, bsrc = xsrc(j)
                sumsq(src, j, [bsrc], bst=B_s1[j])
                rstd_batch(1, j, 32 + j, 64 + j, bst=B_s1[j])
                S.op("dve", lambda e: e.scalar_tensor_tensor(
                    out=hb_all[:, j, :], in0=src, scalar=st[:, 64 + j:65 + j], in1=tab[:, 0, :],
                    op0=ALU.mult, op1=ALU.mult), rd=[bsrc, B_s1[j], B_tab[0]], wr=[B_hb[j]])

            def stage2(j):
                i = j - 1
                par = i % 2
                pb = 2 * par
                fns = []
                for k in range(8):
                    w = k // 2
                    own = (8 + w) if j == 1 else w
                    dst = ps[:, pb + k // 4, (k % 4) * 128:(k % 4 + 1) * 128]
                    fns.append(lambda e, dst=dst, k=k, own=own: e.matmul(
                        dst, lhsT=hb_all[:, j, k * 128:(k + 1) * 128], rhs=bands[:, own, :], start=True, stop=False))
                    fns.append(lambda e, dst=dst, k=k, w=w: e.matmul(
                        dst, lhsT=hb_all[:, j - 1, k * 128:(k + 1) * 128], rhs=bands[:, 4 + w, :], start=False, stop=True))
                S.op("pe", fns, rd=[B_hb[j], B_hb[j - 1], B_bands], wr=[B_ps[pb], B_ps[pb + 1]])
                S.op("act", lambda e: e.activation(
                    out=pTs[:, par].rearrange("p (a b) t -> p a (b t)", a=2), in_=ps[:, pb:pb + 2, :], func=AF.Copy),
                    rd=[B_ps[pb], B_ps[pb + 1]], wr=[B_pTs[par]])
                yb = 4 + 2 * par
                fns = []
                for g in range(4):
                    for cc in range(2):
                        dst = ps[:, yb + g // 2, (g % 2) * 256:(g % 2 + 1) * 256]
                        fns.append(lambda e, dst=dst, g=g, cc=cc: e.matmul(
                            dst, lhsT=pTs[:, par, 2 * g + cc, :], rhs=poolw[:, g, cc, :], start=(cc == 0), stop=(cc == 1)))
                S.op("pe", fns, rd=[B_pTs[par], B_poolw], wr=[B_ps[yb], B_ps[yb + 1]])

            def stage3(j):
                i = j - 1
                par = i % 2
                yb = 4 + 2 * par
                c0 = 20 + 4 * par
                ysv = ysb[:, par, :]
                t2v = t2b[:, par, :]
                S.op("dve", lambda e: e.tensor_tensor(
                    out=ysv.rearrange("p (a b) -> p a b", a=2), in0=ps[:, yb:yb + 2, :],
                    in1=tab[:, 1, :].rearrange("p (a b) -> p a b", a=2), op=ALU.mult),
                    rd=[B_ps[yb], B_ps[yb + 1], B_tab[1]], wr=[B_ys[par]])
                sumsq(ysv, c0, [B_ys[par]], bst=B_s3[par])
                rstd_batch(1, c0, c0 + 1, c0 + 2, bst=B_s3[par])
                S.op("dve", lambda e: e.scalar_tensor_tensor(
                    out=t2v, in0=ysv, scalar=st[:, c0 + 2:c0 + 3], in1=tab[:, 2, :], op0=ALU.mult, op1=ALU.mult),
                    rd=[B_ys[par], B_s3[par], B_tab[2]], wr=[B_t2p[par]])
                S.op("pool", lambda e: e.tensor_tensor(out=hs[:, i, :], in0=hs[:, i, :], in1=t2v, op=ALU.add),
                     rd=[B_t2p[par]], wr=[B_hs[i]])

            stage1(0)
            stage1(1)
            stage1(2)
            for j in range(1, NXT):
                if j + 2 < NXT:
                    stage1(j + 2)
                stage2(j)
                if j - 1 >= 1:
                    stage3(j - 1)
            stage3(NXT - 1)

            checkpoint(1)
            def ffn(layer, tiles, store):
                bt = barrier()
                cv = Carver()
                wd = cv.take(BF16, [128, NFC, D])
                hT = cv.take(BF16, [128, 8, 768])
                actT = cv.take(BF16, [128, NFC, 768])
                hb = cv.take(BF16, [128, 2, D])
                sg = cv.take(F32, [128, 2, 512])
                t2 = cv.take(F32, [128, D])
                B_wd, B_hT, B_actT, B_t2 = Buf("wd"), Buf("hT"), Buf("actT"), Buf("t2")
                B_hb = [Buf("hb0"), Buf("hb1")]
                B_sg = [Buf("sg0"), Buf("sg1")]
                chw = CH_WD0 if layer == 0 else CH_WD1
                S.dma("pool", "d_wd", [(wd[:, 2 * j:2 * j + 2, :].rearrange("p a b -> p (a b)"), wst[chw + j]) for j in range(11)],
                      wr=[B_wd], extra=bt)
                load_tab(0, T_FFNPRE0 if layer == 0 else T_FFNPRE1, extra=bt)
                load_tab(1, T_FFNPOST0 if layer == 0 else T_FFNPOST1, extra=bt)
                n = len(tiles)
                thirds = [tiles[0:6], tiles[6:n - 5], tiles[n - 5:]] if n == 17 else [tiles[0:6], tiles[6:11], tiles[11:]]
                it = 0
                ydb = 0
                B_pre = [Buf("fpre%d" % t_) for t_ in range(6)]

                def prenorm_tile(k3, ti):
                    i = thirds[k3][ti]
                    p2 = ti % 2
                    sumsq(hs[:, i, :], 40 + ti, [B_hs[i]], bst=B_pre[ti])
                    rstd_batch(1, 40 + ti, 48 + ti, 56 + ti, bst=B_pre[ti])
                    S.op("dve", lambda e: e.scalar_tensor_tensor(
                        out=hb[:, p2, :], in0=hs[:, i, :], scalar=st[:, 56 + ti:57 + ti], in1=tab[:, 0, :],
                        op0=ALU.mult, op1=ALU.mult), rd=[B_hs[i], B_pre[ti], B_tab[0]], wr=[B_hb[p2]])
                    tpv = psbf(p2)
                    S.op("pe", [lambda e, k=k: e.transpose(
                        out=tpv[:, k, :], in_=hb[:, p2, k * 128:(k + 1) * 128], identity=ident) for k in range(8)],
                        rd=[B_hb[p2], B_bands], wr=[B_ps[p2]])
                    S.op("act", lambda e: e.activation(
                        out=hT[:, :, ti * 128:(ti + 1) * 128], in_=tpv, func=AF.Copy), rd=[B_ps[p2]], wr=[B_hT])

                for ti in range(len(thirds[0])):
                    prenorm_tile(0, ti)
                for k3, tl in enumerate(thirds):
                    nt = len(tl)
                    ntok = nt * 128
                    groups = [(0, 384), (384, ntok - 384)]
                    for c in range(NFC):
                        slot = ring_next()
                        W = ring[:, slot, :].rearrange("p (k f) -> p k f", k=8)
                        for (g0, N) in groups:
                            gb = 2 + 2 * (it % 2)
                            ub = gb + 1
                            S.op("pe", [lambda e, k=k, W=W, g0=g0, N=N, gb=gb: e.matmul(
                                ps[:, gb, 0:N], lhsT=W[:, k, 0:128], rhs=hT[:, k, g0:g0 + N], start=(k == 0), stop=(k == 7))
                                for k in range(8)], rd=[B_ring[slot], B_hT], wr=[B_ps[gb]])
                            S.op("pe", [lambda e, k=k, W=W, g0=g0, N=N, ub=ub: e.matmul(
                                ps[:, ub, 0:N], lhsT=W[:, k, 128:256], rhs=hT[:, k, g0:g0 + N], start=(k == 0), stop=(k == 7))
                                for k in range(8)], rd=[B_ring[slot], B_hT], wr=[B_ps[ub]])
                            s2 = it % 2
                            S.op("act", lambda e, s2=s2, gb=gb, N=N: e.activation(
                                out=sg[:, s2, 0:N], in_=ps[:, gb, 0:N], func=AF.Silu), rd=[B_ps[gb]], wr=[B_sg[s2]])
                            S.op("dve", lambda e, s2=s2, ub=ub, N=N, c=c, g0=g0: e.tensor_tensor(
                                out=actT[:, c, g0:g0 + N], in0=ps[:, ub, 0:N], in1=sg[:, s2, 0:N], op=ALU.mult),
                                rd=[B_ps[ub], B_sg[s2]], wr=[B_actT])
                            it += 1
                        ring_issue()
                    for ti, i in enumerate(tl):
                        yb = 2 + 2 * (ydb % 3)
                        ydb += 1
                        fns = []
                        for c in range(NFC):
                            for half in range(2):
                                fns.append(lambda e, c=c, half=half, ti=ti, yb=yb: e.matmul(
                                    ps[:, yb + half, :], lhsT=actT[:, c, ti * 128:(ti + 1) * 128],
                                    rhs=wd[:, c, half * 512:(half + 1) * 512], start=(c == 0), stop=(c == NFC - 1)))
                        S.op("pe", fns, rd=[B_actT, B_wd], wr=[B_ps[yb], B_ps[yb + 1]])
                        if k3 + 1 < 3 and ti < len(thirds[k3 + 1]):
                            prenorm_tile(k3 + 1, ti)
                        yv = ps[:, yb:yb + 2, :]
                        S.op("act", lambda e, yv=yv: e.activation(
                            out=junk[:].rearrange("p (a b) -> p a b", a=2), in_=yv, func=AF.Square, accum_out=st[:, 20:21]),
                            rd=[B_ps[yb], B_ps[yb + 1]], wr=[B_junk, B_st])
                        rstd_batch(1, 20, 21, 22)
                        S.op("dve", lambda e, yv=yv: e.scalar_tensor_tensor(
                            out=t2.rearrange("p (a b) -> p a b", a=2), in0=yv, scalar=st[:, 22:23],
                            in1=tab[:, 1, :].rearrange("p (a b) -> p a b", a=2), op0=ALU.mult, op1=ALU.mult),
                            rd=[B_ps[yb], B_ps[yb + 1], B_st, B_tab[1]], wr=[B_t2])
                        S.op("dve", lambda e, i=i: e.tensor_tensor(out=hs[:, i, :], in0=hs[:, i, :], in1=t2, op=ALU.add),
                             rd=[B_t2], wr=[B_hs[i]])
                        if store:
                            S.dma("sp", "d_out", [(outd[(i - 1) * 128:i * 128, :], hs[:, i, :])], rd=[B_hs[i]], wr=[B_out])

            if not os.environ.get('SKIPFFN'):
                ffn(0, list(range(NHS)), False)
            checkpoint(2)

            bt = barrier()
            cv = Carver()
            KT = cv.take(BF16, [128, 2, 2, NHS * 128])
            VA = cv.take(BF16, [128, NHS, 2, 65])
            kv_end = cv.off
            wk = cv.take(BF16, [128, 8, 256])
            wv = cv.take(BF16, [128, 8, 128])
            hT = cv.take(BF16, [128, 8, 512])
            hb = cv.take(BF16, [128, 2, D])
            B_KT, B_VA, B_wkv, B_hT = Buf("KT"), Buf("VA"), Buf("wkv"), Buf("hT")
            B_hb = [Buf("hb0"), Buf("hb1")]
            S.dma("pool", "d_wkv", [(wk.rearrange("p a b -> p (a b)"), wst[CH_WK]),
                                    (wv.rearrange("p a b -> p (a b)"), wst[CH_WV][:, 0:1024])], wr=[B_wkv], extra=bt)
            load_tab(0, T_KV, extra=bt)
            load_tab(1, T_BV, extra=bt)
            KVD = int(os.environ.get('KVDBG', '9'))
            if KVD >= 1:
                S.op("dve", lambda e: e.memset(VA[:, :, :, 64:65], 1.0), wr=[B_VA], extra=bt)
            S.op("dve", lambda e: e.memset(KT.rearrange("p a b c -> p (a b c)"), 0.0), wr=[B_KT], extra=bt)
            for i in range(NHS):
                sumsq(hs[:, i, :], i, [B_hs[i]])
            rstd_batch(NHS, 0, 32, 64)
            vcnt = 0
            for g0t in range(0, NHS, 4):
                tl = list(range(g0t, min(g0t + 4, NHS)))
                N = len(tl) * 128
                for ti, i in enumerate(tl):
                    p2 = ti % 2
                    S.op("dve", lambda e, i=i, p2=p2: e.scalar_tensor_tensor(
                        out=hb[:, p2, :], in0=hs[:, i, :], scalar=st[:, 64 + i:65 + i], in1=tab[:, 0, :],
                        op0=ALU.mult, op1=ALU.mult), rd=[B_hs[i], B_st, B_tab[0]], wr=[B_hb[p2]])
                    tpv = psbf(p2)
                    S.op("pe", [lambda e, k=k, p2=p2, tpv=tpv: e.transpose(
                        out=tpv[:, k, :], in_=hb[:, p2, k * 128:(k + 1) * 128], identity=ident) for k in range(8)],
                        rd=[B_hb[p2], B_bands], wr=[B_ps[p2]])
                    S.op("act", lambda e, ti=ti, tpv=tpv: e.activation(
                        out=hT[:, :, ti * 128:(ti + 1) * 128], in_=tpv, func=AF.Copy), rd=[B_ps[p2]], wr=[B_hT])
                for kvh in range(2 if KVD >= 3 else 0):
                    kb = 2 + kvh
                    S.op("pe", [lambda e, k=k, kvh=kvh, kb=kb, N=N: e.matmul(
                        ps[:, kb, 0:N], lhsT=wk[:, k, kvh * 128:(kvh + 1) * 128], rhs=hT[:, k, 0:N],
                        start=(k == 0), stop=(k == 7)) for k in range(8)], rd=[B_wkv, B_hT], wr=[B_ps[kb]])
                    for e_ in range(2):
                        S.op("act", lambda e, kvh=kvh, kb=kb, N=N, g0t=g0t, e_=e_: e.activation(
                            out=KT[64 * e_:64 * e_ + 64, kvh, e_, g0t * 128:g0t * 128 + N],
                            in_=ps[64 * e_:64 * e_ + 64, kb, 0:N], func=AF.Identity,
                            bias=cst[64 * e_:64 * e_ + 64, 8 + kvh:9 + kvh]), rd=[B_ps[kb], B_cst], wr=[B_KT])
                for ti, i in enumerate(tl if KVD >= 4 else []):
                    vb = 4 + (vcnt % 2)
                    vcnt += 1
                    S.op("pe", [lambda e, k=k, ti=ti, vb=vb: e.matmul(
                        ps[:, vb, 0:128], lhsT=hT[:, k, ti * 128:(ti + 1) * 128], rhs=wv[:, k, :],
                        start=(k == 0), stop=(k == 7)) for k in range(8)], rd=[B_wkv, B_hT], wr=[B_ps[vb]])
                    S.op("dve", lambda e, i=i, vb=vb: e.tensor_tensor(
                        out=VA[:, i, :, 0:64], in0=ps[:, vb, 0:128].rearrange("p (a b) -> p a b", a=2),
                        in1=tab[:, 1, 0:128].rearrange("p (a b) -> p a b", a=2), op=ALU.add),
                        rd=[B_ps[vb], B_tab[1]], wr=[B_VA])

            checkpoint(3)
            bt = barrier()
            cv = Carver()
            cv.off = kv_end
            wo = cv.take(BF16, [128, 8, D])
            bias = cv.take(F32, [128, 16, 2, 128])
            PT = cv.take(BF16, [128, 16, 2, 128])
            lg = cv.take(F32, [128, 2, 512])
            hT4 = cv.take(BF16, [128, 8, 512])
            QT4 = cv.take(BF16, [128, 8, 512])
            oT4 = cv.take(BF16, [128, 4, 8, 128])
            hb = cv.take(BF16, [128, 2, D])
            obf = cv.take(BF16, [128, D])
            t2 = cv.take(F32, [128, D])
            bohl = cv.take(BF16, [128, 2, D])
            ones = cv.take(BF16, [128, 128])
            B_wo, B_bias, B_PTx, B_hT4, B_QT4, B_obf, B_t2, B_bohl = (Buf(n) for n in
                                                                      ("wo", "bias", "PT", "hT4", "QT4", "obf", "t2", "bohl"))
            B_PT = [Buf("PT%d" % h_) for h_ in range(8)]
            B_oT4 = [Buf("oT4_%d" % t) for t in range(4)]
            B_lg = [Buf("lg0"), Buf("lg1")]
            B_hb = [Buf("hb0"), Buf("hb1")]
            B_esink, B_stb = Buf("esink"), Buf("stb")
            B_pn = [Buf("pn0"), Buf("pn1")]
            B_den = [Buf("den0"), Buf("den1")]
            S.dma("pool", "d_wqo", [(wo[:, 2 * j:2 * j + 2, :].rearrange("p a b -> p (a b)"), wst[CH_WO + j]) for j in range(4)],
                  wr=[B_wo], extra=bt)
            S.dma("sp", "d_bias", [(bias.rearrange("p a b c -> p (a b c)"), btabd[0])], wr=[B_bias], extra=bt)
            load_tab(0, T_MIXPRE1, extra=bt)
            load_tab(1, T_MIXPOST1, extra=bt)
            load_tab(2, T_BO, extra=bt)
            S.op("dve", lambda e: e.memset(ones, 1.0), wr=[B_bohl], extra=bt)
            S.op("dve", lambda e: e.tensor_copy(out=bohl[:, 0, :], in_=tab[:, 2, :]), rd=[B_tab[2]], wr=[B_bohl])
            S.op("dve", lambda e: e.tensor_tensor(out=t2, in0=tab[:, 2, :], in1=bohl[:, 0, :], op=ALU.subtract),
                 rd=[B_tab[2], B_bohl], wr=[B_t2])
            S.op("dve", lambda e: e.tensor_copy(out=bohl[:, 1, :], in_=t2), rd=[B_t2], wr=[B_bohl])
            S.op("act", lambda e: e.activation(out=st[:, 96:112], in_=cst[:, 10:26], func=AF.Exp),
                 rd=[B_cst], wr=[B_esink], extra=bt)
            for i in range(1, NHS):
                sumsq(hs[:, i, :], i, [B_hs[i]], bst=B_stb)
            rstd_batch(NHS, 0, 32, 64, bst=B_stb)
            hsplit = [(0, 7), (7, 14), (14, 16)]

            def q_stage(g):
                for ti in range(4):
                    i = 1 + 4 * g + ti
                    p2 = ti % 2
                    tb = 6 + p2
                    S.op("dve", lambda e, i=i, p2=p2: e.scalar_tensor_tensor(
                        out=hb[:, p2, :], in0=hs[:, i, :], scalar=st[:, 64 + i:65 + i], in1=tab[:, 0, :],
                        op0=ALU.mult, op1=ALU.mult), rd=[B_hs[i], B_stb, B_tab[0]], wr=[B_hb[p2]])
                    tpv = psbf(tb)
                    S.op("pe", [lambda e, k=k, p2=p2, tpv=tpv: e.transpose(
                        out=tpv[:, k, :], in_=hb[:, p2, k * 128:(k + 1) * 128], identity=ident) for k in range(8)],
                        rd=[B_hb[p2], B_bands], wr=[B_ps[tb]])
                    S.op("act", lambda e, ti=ti, tpv=tpv: e.activation(
                        out=hT4[:, :, ti * 128:(ti + 1) * 128], in_=tpv, func=AF.Copy), rd=[B_ps[tb]], wr=[B_hT4])
                for j in range(4):
                    slot = ring_next()
                    W = ring[:, slot, :].rearrange("p (k f) -> p k f", k=8)
                    for e2 in range(2):
                        hp = 2 * j + e2
                        qb = 6 + hp % 2
                        S.op("pe", [lambda e, k=k, W=W, e2=e2, qb=qb: e.matmul(
                            ps[:, qb, :], lhsT=W[:, k, e2 * 128:(e2 + 1) * 128], rhs=hT4[:, k, :],
                            start=(k == 0), stop=(k == 7)) for k in range(8)], rd=[B_ring[slot], B_hT4], wr=[B_ps[qb]])
                        if hp % 2 == 0:
                            S.op("act", lambda e, hp=hp, qb=qb: e.activation(
                                out=QT4[:, hp, :], in_=ps[:, qb, :], func=AF.Identity, bias=cst[:, hp:hp + 1]),
                                rd=[B_ps[qb], B_cst], wr=[B_QT4])
                        else:
                            S.op("dve", lambda e, hp=hp, qb=qb: e.tensor_scalar(
                                out=QT4[:, hp, :], in0=ps[:, qb, :], scalar1=cst[:, hp:hp + 1], scalar2=None, op0=ALU.add),
                                rd=[B_ps[qb], B_cst], wr=[B_QT4])
                    ring_issue()

            def s_tile(g, ti, hook=None):
                i = 1 + 4 * g + ti
                p2 = i % 2
                den = st[:, 112 + 16 * p2:128 + 16 * p2]

                def scores(hp):
                    kvh = hp // 4
                    sbk = 1 + hp % 2
                    sv = ps[:, sbk, :].rearrange("p (e b q) -> p e b q", e=2, b=2)
                    fns = []
                    for e_ in range(2):
                        for blk in range(2):
                            kt = i - 1 + blk
                            fns.append(lambda e, e_=e_, blk=blk, kt=kt: e.matmul(
                                sv[:, e_, blk, :], lhsT=KT[:, kvh, e_, kt * 128:(kt + 1) * 128],
                                rhs=QT4[:, hp, ti * 128:(ti + 1) * 128], start=True, stop=True))
                    S.op("pe", fns, rd=[B_KT, B_QT4], wr=[B_ps[sbk]])

                def c1_bank(bi):
                    h0, h1 = hsplit[bi]
                    nh = h1 - h0
                    ov = ps[:, 3 + bi, 0:nh * 65].rearrange("p (h c) -> p h c", c=65)
                    S.op("dve", lambda e: e.tensor_tensor(
                        out=den[:, h0:h1], in0=ov[:, :, 64], in1=st[:, 96 + h0:96 + h1], op=ALU.add),
                        rd=[B_ps[3 + bi], B_esink], wr=[B_den[p2]])
                    S.op("dve", lambda e: e.reciprocal(out=den[:, h0:h1], in_=den[:, h0:h1]), rd=[], wr=[B_den[p2]])
                    S.op("dve", lambda e: e.tensor_tensor(
                        out=obf.rearrange("p (h c) -> p h c", c=64)[:, h0:h1, :], in0=ov[:, :, 0:64],
                        in1=den[:, h0:h1].unsqueeze(2).to_broadcast([128, nh, 64]), op=ALU.mult),
                        rd=[B_ps[3 + bi], B_den[p2]], wr=[B_obf])

                scores(0)
                scores(1)
                for hp in range(8):
                    kvh = hp // 4
                    sbk = 1 + hp % 2
                    l2 = hp % 2
                    S.op("dve", lambda e, hp=hp, sbk=sbk, l2=l2: e.scalar_tensor_tensor(
                        out=lg[:, l2, :], in0=ps[:, sbk, :], scalar=0.125,
                        in1=bias[:, 2 * hp:2 * hp + 2, :, :].rearrange("p a b c -> p (a b c)"), op0=ALU.mult, op1=ALU.add),
                        rd=[B_ps[sbk], B_bias], wr=[B_lg[l2]])
                    S.op("act", lambda e, hp=hp, l2=l2: e.activation(
                        out=PT[:, 2 * hp:2 * hp + 2, :, :].rearrange("p a b c -> p (a b c)"), in_=lg[:, l2, :], func=AF.Exp),
                        rd=[B_lg[l2]], wr=[B_PT[hp]])
                    if hp + 2 < 8:
                        scores(hp + 2)
                    fns = []
                    for e_ in range(2):
                        h = 2 * hp + e_
                        ob = 3 + h // 7
                        oc = (h % 7) * 65
                        for blk in range(2):
                            kt = i - 1 + blk
                            fns.append(lambda e, h=h, ob=ob, oc=oc, blk=blk, kt=kt, kvh=kvh: e.matmul(
                                ps[:, ob, oc:oc + 65], lhsT=PT[:, h, blk, :], rhs=VA[:, kt, kvh, :],
                                start=(blk == 0), stop=(blk == 1)))
                    S.op("pe", fns, rd=[B_PT[hp], B_VA],
                         wr=[B_ps[b_] for b_ in sorted({3 + (2 * hp) // 7, 3 + (2 * hp + 1) // 7})])
                    if hp == 1 and hook is not None:
                        hook()
                    if hp == 4:
                        c1_bank(0)
                    if hp == 7:
                        c1_bank(1)
                if i == 1:
                    S.dma("sp", "d_bias", [(bias.rearrange("p a b c -> p (a b c)"), btabd[1])], wr=[B_bias])

                def tail():
                    c1_bank(2)
                    tpv = psbf(0)
                    S.op("pe", [lambda e, k=k: e.transpose(
                        out=tpv[:, k, :], in_=obf[:, k * 128:(k + 1) * 128], identity=ident) for k in range(8)],
                        rd=[B_obf, B_bands], wr=[B_ps[0]])
                    S.op("act", lambda e: e.activation(out=oT4[:, ti], in_=tpv, func=AF.Copy), rd=[B_ps[0]], wr=[B_oT4[ti]])
                return tail

            def o_tile(g, ti, ybase):
                i = 1 + 4 * g + ti
                p2 = ti % 2
                c0 = 20 + 4 * p2
                for half in range(2):
                    fns = [lambda e, k=k, half=half: e.matmul(
                        ps[:, ybase + half, :], lhsT=oT4[:, ti, k, :], rhs=wo[:, k, half * 512:(half + 1) * 512],
                        start=(k == 0), stop=False) for k in range(8)]
                    for hl in range(2):
                        fns.append(lambda e, hl=hl, half=half: e.matmul(
                            ps[:, ybase + half, :], lhsT=ones[0:1, :], rhs=bohl[0:1, hl, half * 512:(half + 1) * 512],
                            start=False, stop=(hl == 1)))
                    S.op("pe", fns, rd=[B_oT4[ti], B_wo, B_bohl], wr=[B_ps[ybase + half]])
                yv = ps[:, ybase:ybase + 2, :]
                S.op("act", lambda e: e.activation(
                    out=junk[:].rearrange("p (a b) -> p a b", a=2), in_=yv, func=AF.Square,
                    accum_out=st[:, c0:c0 + 1]), rd=[B_ps[ybase], B_ps[ybase + 1]], wr=[B_junk, B_pn[p2]])
                rstd_batch(1, c0, c0 + 1, c0 + 2, bst=B_pn[p2])
                S.op("dve", lambda e: e.scalar_tensor_tensor(
                    out=t2.rearrange("p (a b) -> p a b", a=2), in0=yv, scalar=st[:, c0 + 2:c0 + 3],
                    in1=tab[:, 1, :].rearrange("p (a b) -> p a b", a=2), op0=ALU.mult, op1=ALU.mult),
                    rd=[B_ps[ybase], B_ps[ybase + 1], B_pn[p2], B_tab[1]], wr=[B_t2])
                S.op("pool", lambda e: e.tensor_tensor(out=hs[:, i, :], in0=hs[:, i, :], in1=t2, op=ALU.add),
                     rd=[B_t2], wr=[B_hs[i]])

            q_stage(0)
            for g in range(4):
                tail = None
                for ti in range(4):
                    tail = s_tile(g, ti, hook=tail)
                tail()
                if g + 1 < 4:
                    q_stage(g + 1)
                for t_ in range(4):
                    o_tile(g, t_, 6 if t_ % 2 == 0 else 1)

            checkpoint(4)
            ffn(1, list(range(1, NHS)), True)


        except _Stop:
            pass
        S.wait("sp", B_out.w)

        with nc.Block() as block:
            @block.tensor
            def _(e):
                for f in S.ops["pe"]:
                    f(e)

            @block.scalar
            def _(e):
                for f in S.ops["act"]:
                    f(e)

            @block.vector
            def _(e):
                for f in S.ops["dve"]:
                    f(e)

            @block.gpsimd
            def _(e):
                for f in S.ops["pool"]:
                    f(e)

            @block.sync
            def _(e):
                for f in S.ops["sp"]:
                    f(e)
    return nc


def _t5_bucket(d):
    max_exact = 16
    df = np.maximum(d, 1).astype(np.float32)
    large = max_exact + (np.log(df / max_exact) / np.log(128 / max_exact) * (32 - max_exact)).astype(np.int32)
    large = np.minimum(large, 31)
    return np.where(d < max_exact, d, large)


def _band_consts(first_core):
    out = np.zeros((128, 13, 128), np.float32)
    tp = np.arange(128)[:, None]
    t = np.arange(128)[None, :]
    for wi, w in enumerate((2, 4, 8, 16)):
        dlt = t - tp
        own = np.where((dlt >= 0) & (dlt < w), 1.0 / w, 0.0) - (dlt == 0)
        prev = np.where((t + 128 - tp) < w, 1.0 / w, 0.0)
        out[:, wi, :] = own
        out[:, 4 + wi, :] = prev
        if first_core:
            pos = np.maximum(t - 112 + 1, 1)
            cnt = np.minimum(pos, w).astype(np.float32)
            valid = (dlt >= 0) & (dlt < w) & (tp >= 112)
            out[:, 8 + wi, :] = np.where(valid, 1.0 / cnt, 0.0) - (dlt == 0)
        else:
            out[:, 8 + wi, :] = own
    out[:, 12, :] = np.eye(128, dtype=np.float32)
    return out.reshape(128, 13 * 128)


def _bias_tables(rel_bias, first_core):
    s = np.arange(128)[:, None]
    q = np.arange(128)[None, :]
    tabs = np.empty((2, 128, 16, 2, 128), np.float32)
    for blk in range(2):
        d = q - s + (128 if blk == 0 else 0)
        ok = (d >= 0) & (d < 128)
        bidx = _t5_bucket(np.maximum(d, 0))
        g = rel_bias[bidx]
        g = np.where(ok[:, :, None], g, np.float32(NEG))
        g = np.transpose(g, (0, 2, 1))
        tabs[1, :, :, blk, :] = g
        if blk == 0 and first_core:
            g = np.where((s >= 112)[:, :, None] & np.ones((1, 1, 128), bool), g, np.float32(NEG))
        tabs[0, :, :, blk, :] = g
    return tabs.reshape(2, 128, 16 * 2 * 128)


def _prep_common(inp):
    f = np.float32
    wst = np.zeros((NCHUNK, 128, 2048), f)
    pw = inp["pool_w"][0]
    wst[CH_POOL] = pw.reshape(4, 2, 128, 256).transpose(2, 0, 1, 3).reshape(128, 2048)
    for l in range(2):
        gu = inp["w_gate_up"][l]
        gate = gu[:, :DFF].reshape(8, 128, NFC, 128)
        up = gu[:, DFF:].reshape(8, 128, NFC, 128)
        blk = np.stack([gate, up], axis=3)
        blk = blk.transpose(2, 1, 0, 3, 4).reshape(NFC, 128, 2048)
        base = CH_GU0 if l == 0 else CH_GU1
        wst[base:base + NFC] = blk
        wdn = inp["w_down"][l].reshape(11, 2, 128, D).transpose(0, 2, 1, 3).reshape(11, 128, 2048)
        base = CH_WD0 if l == 0 else CH_WD1
        wst[base:base + 11] = wdn
    wk = inp["w_k"].reshape(8, 128, 2, 64)
    wkd = np.stack([wk, wk], axis=3)
    wst[CH_WK] = wkd.transpose(1, 0, 2, 3, 4).reshape(128, 2048)
    wst[CH_WV][:, :1024] = inp["w_v"].reshape(8, 128, 128).transpose(1, 0, 2).reshape(128, 1024)
    wq = inp["w_q"][0].reshape(8, 128, 4, 256)
    wst[CH_WQ:CH_WQ + 4] = wq.transpose(2, 1, 0, 3).reshape(4, 128, 2048)
    wo = inp["w_o"][0].reshape(4, 2, 128, D)
    wst[CH_WO:CH_WO + 4] = wo.transpose(0, 2, 1, 3).reshape(4, 128, 2048)

    rows = [inp["norm_mix_pre"][0], inp["pool_scale"][0], inp["norm_mix_post"][0], inp["norm_ffn_pre"][0],
            inp["norm_ffn_post"][0], inp["kv_norm"], inp["norm_mix_pre"][1], inp["norm_mix_post"][1],
            inp["b_o"][0], inp["norm_ffn_pre"][1], inp["norm_ffn_post"][1],
            np.concatenate([inp["b_v"], np.zeros(D - 128, f)])]
    tabs = np.ascontiguousarray(np.broadcast_to(np.stack(rows)[:, None, :], (NTAB, 128, D))).astype(f)

    cst = np.zeros((128, NCST), f)
    cst[:, 0:8] = inp["b_q"][0].reshape(8, 128).T
    bk = inp["b_k"].reshape(2, 64)
    cst[:, 8:10] = np.concatenate([bk, bk], axis=1).T
    cst[:, 10:26] = np.broadcast_to(inp["sinks"][0][None, :], (128, 16))
    cst[:, 26] = EPS
    return wst, tabs, cst


_NC_CACHE = {}


def kernel(**inputs):
    inp = {k: np.asarray(v, dtype=np.float32) for k, v in inputs.items()}
    x = inp["x"]
    meta = inp["meta_tokens"]
    wst, tabs, cst = _prep_common(inp)
    bands = [_band_consts(True), _band_consts(False)]
    btab = [_bias_tables(inp["rel_bias"], True), _bias_tables(inp["rel_bias"], False)]
    in_maps = []
    for c in range(8):
        b, r = c // 4, c % 4
        xin = np.zeros((NXT * 128, D), np.float32)
        if r == 0:
            xin[128 + 112:256] = meta
            xin[256:] = x[b, 0:2048]
        else:
            xin[:] = x[b, 2048 * r - 256:2048 * r + 2048]
        fc = (r == 0)
        in_maps.append({"xin": xin, "wst": wst, "tabs": tabs, "cst": cst,
                        "bands": bands[0 if fc else 1], "btab": btab[0 if fc else 1]})
    if "nc" not in _NC_CACHE:
        _NC_CACHE["nc"] = build_program()
    nc = _NC_CACHE["nc"]
    res = run_bass_kernel_spmd(nc, in_maps, core_ids=list(range(8)))
    out = np.empty((2, 8192, D), np.float32)
    for c in range(8):
        b, r = c // 4, c % 4
        out[b, 2048 * r:2048 * r + 2048] = res.results[c]["out"]
    return out
```

```python
import os
import numpy as np
from contextlib import ExitStack
import concourse.bass as bass
import concourse.mybir as mybir
from concourse.bass_utils import run_bass_kernel_spmd

F32 = mybir.dt.float32
BF16 = mybir.dt.bfloat16
AF = mybir.ActivationFunctionType
ALU = mybir.AluOpType

D = 1024
DFF = 2816
NFC = DFF // 128
NXT = 18
NHS = 17
EPS = 1e-6
NSLOT = 4
NEG = -30000.0

CH_POOL = 0
CH_GU0 = 1
CH_WD0 = CH_GU0 + 22
CH_WK = CH_WD0 + 11
CH_WV = CH_WK + 1
CH_WQ = CH_WV + 1
CH_WO = CH_WQ + 4
CH_GU1 = CH_WO + 4
CH_WD1 = CH_GU1 + 22
NCHUNK = CH_WD1 + 11

(T_MIXPRE0, T_PSCALE, T_MIXPOST0, T_FFNPRE0, T_FFNPOST0, T_KV, T_MIXPRE1, T_MIXPOST1, T_BO,
 T_FFNPRE1, T_FFNPOST1, T_BV) = range(12)
NTAB = 12
NCST = 32


class Buf:
    def __init__(self, name):
        self.name = name
        self.w = None
        self.r = {}


class Sched:
    QS = ("pe", "act", "dve", "pool", "sp")

    def __init__(self, nc, es):
        self.nc = nc
        self.es = es
        self.ops = {k: [] for k in self.QS}
        self.sem = {}
        self.count = {k: 0 for k in self.QS}
        self.seen = {k: {} for k in self.QS}
        self.dcount = {}
        for q in self.QS:
            self.getsem(q)

    def getsem(self, name):
        if name not in self.sem:
            self.sem[name] = self.es.enter_context(self.nc.semaphore(name))
        return self.sem[name]

    def wait(self, q, tok):
        if tok is None:
            return
        sname, val = tok
        if self.seen[q].get(sname, 0) >= val:
            return
        self.seen[q][sname] = val
        sem = self.getsem(sname)
        self.ops[q].append(lambda e, sem=sem, val=val: e.wait_ge(sem, val))

    def _deps(self, q, rd, wr, extra):
        for t in extra:
            self.wait(q, t)
        for b in rd:
            self.wait(q, b.w)
        for b in wr:
            self.wait(q, b.w)
            for sn, v in b.r.items():
                self.wait(q, (sn, v))

    def _mark(self, tok, rd, wr):
        for b in rd:
            if b.r.get(tok[0], 0) < tok[1]:
                b.r[tok[0]] = tok[1]
        for b in wr:
            b.w = tok
            b.r = {}

    def op(self, q, fn, rd=(), wr=(), extra=()):
        self._deps(q, rd, wr, extra)
        fns = fn if isinstance(fn, (list, tuple)) else [fn]
        for f in fns[:-1]:
            self.ops[q].append(lambda e, f=f: f(e))
        self.count[q] += 1
        sem = self.sem[q]
        last = fns[-1]
        self.ops[q].append(lambda e, f=last, sem=sem: f(e).then_inc(sem, 1))
        tok = (q, self.count[q])
        self._mark(tok, rd, wr)
        return tok

    def dma(self, q, semname, pairs, rd=(), wr=(), extra=()):
        self._deps(q, rd, wr, extra)
        sem = self.getsem(semname)
        for out, in_ in pairs:
            self.ops[q].append(lambda e, out=out, in_=in_, sem=sem: e.dma_start(out=out, in_=in_).then_inc(sem, 16))
        self.dcount[semname] = self.dcount.get(semname, 0) + 16 * len(pairs)
        tok = (semname, self.dcount[semname])
        self._mark(tok, rd, wr)
        return tok


class _Stop(Exception):
    pass


def build_program(stop=99):
    nc = bass.Bass("TRN2", target_bir_lowering=False)
    xin = nc.dram_tensor("xin", [NXT * 128, D], F32, kind="ExternalInput").ap()
    wst = nc.dram_tensor("wst", [NCHUNK, 128, 2048], F32, kind="ExternalInput").ap()
    tabs = nc.dram_tensor("tabs", [NTAB, 128, D], F32, kind="ExternalInput").ap()
    cstd = nc.dram_tensor("cst", [128, NCST], F32, kind="ExternalInput").ap()
    bandd = nc.dram_tensor("bands", [128, 13 * 128], F32, kind="ExternalInput").ap()
    btabd = nc.dram_tensor("btab", [2, 128, 16 * 2 * 128], F32, kind="ExternalInput").ap()
    outd = nc.dram_tensor("out", [16 * 128, D], F32, kind="ExternalOutput").ap()
    dbgd = nc.dram_tensor("dbg", [NHS * 128, D], F32, kind="ExternalOutput").ap() if stop != 99 else None

    with ExitStack() as es:
        S = Sched(nc, es)

        def sb(name, shape, dt):
            return es.enter_context(nc.sbuf_tensor(name, shape, dt))

        hs = sb("hs", [128, NHS, D], F32)
        tab = sb("tab", [128, 3, D], F32)
        ring = sb("ring", [128, NSLOT, 2048], BF16)
        bands = sb("bandsb", [128, 13, 128], BF16)
        cst = sb("cstsb", [128, NCST], F32)
        st = sb("st", [128, 160], F32)
        junk = sb("junk", [128, D], BF16)
        PH_ELEMS = 52 * 1024
        phase = sb("phase", [128, PH_ELEMS], BF16)
        ps = es.enter_context(nc.psum_tensor("ps", [128, 8, 512], F32))

        ident = bands[:, 12, :]
        B_hs = [Buf("hs%d" % i) for i in range(NHS)]
        B_tab = [Buf("tab%d" % i) for i in range(3)]
        B_ring = [Buf("ring%d" % i) for i in range(NSLOT)]
        B_bands = Buf("bands")
        B_cst = Buf("cst")
        B_st = Buf("st")
        B_junk = Buf("junk")
        B_ps = [Buf("ps%d" % i) for i in range(8)]
        B_out = Buf("out")

        def psbf(b):
            return ps[:, b, :].bitcast(BF16).rearrange("p (k t) -> p k t", k=8)

        class Carver:
            def __init__(self):
                self.off = 0

            def take(self, dt, shape):
                n = int(np.prod(shape[1:]))
                nb = n if dt == BF16 else 2 * n
                nb = (nb + 15) // 16 * 16
                assert self.off + nb <= PH_ELEMS, ("phase overflow", self.off + nb)
                v = phase[:, self.off:self.off + nb]
                self.off += nb
                if dt == F32:
                    v = v.bitcast(F32)
                v = v[:, 0:n]
                if len(shape) == 3:
                    v = v.rearrange("p (a b) -> p a b", a=shape[1])
                elif len(shape) == 4:
                    v = v.rearrange("p (a b c) -> p a b c", a=shape[1], b=shape[2])
                return v

        def barrier():
            toks = [(q, S.count[q]) for q in ("pe", "act", "dve", "pool") if S.count[q] > 0]
            for q in ("pe", "act", "dve"):
                for t in toks:
                    if t[0] != q:
                        S.wait(q, t)
            return toks

        def load_tab(slot, idx, extra=()):
            return S.dma("sp", "d_tab%d" % slot, [(tab[:, slot, :], tabs[idx])], wr=[B_tab[slot]], extra=extra)

        ring_seq = []
        for l in range(2):
            if l == 1:
                for g_ in range(4):
                    for j_ in range(4):
                        ring_seq.append(CH_WQ + j_)
            for third in range(3):
                for c in range(NFC):
                    ring_seq.append((CH_GU0 if l == 0 else CH_GU1) + c)
        ring_state = {"issued": 0, "used": 0}

        def ring_issue():
            i = ring_state["issued"]
            if i >= len(ring_seq):
                return
            slot = i % NSLOT
            S.dma("pool", "d_ring%d" % slot, [(ring[:, slot, :], wst[ring_seq[i]])], wr=[B_ring[slot]])
            ring_state["issued"] += 1

        def ring_next():
            i = ring_state["used"]
            ring_state["used"] += 1
            return i % NSLOT

        def rstd_batch(n, col_ss, col_sd, col_r, bst=None):
            bst = bst or B_st
            S.op("act", lambda e: e.activation(out=st[:, col_sd:col_sd + n], in_=st[:, col_ss:col_ss + n],
                                                func=AF.Sqrt, bias=cst[:, 26:27], scale=1.0 / D),
                 rd=[B_cst], wr=[bst])
            S.op("dve", lambda e: e.reciprocal(out=st[:, col_r:col_r + n], in_=st[:, col_sd:col_sd + n]), wr=[bst])

        def sumsq(src, col, rd, bst=None):
            S.op("act", lambda e: e.activation(out=junk[:], in_=src, func=AF.Square, accum_out=st[:, col:col + 1]),
                 rd=rd, wr=[B_junk, bst or B_st])

        def checkpoint(n):
            if stop == n:
                bt_ = barrier()
                for i_ in range(NHS):
                    S.dma("sp", "d_dbg", [(dbgd[i_ * 128:(i_ + 1) * 128, :], hs[:, i_, :])], rd=[B_hs[i_]], wr=[B_out], extra=bt_)
                raise _Stop()

        try:
            B_pre = Buf("pre")
            B_ph = Buf("phase_generic")
            cv = Carver()
            hb_all = cv.take(BF16, [128, NXT, D])
            pre = cv.take(F32, [128, D])
            pTs = cv.take(BF16, [128, 2, 8, 128])
            ysb = cv.take(F32, [128, 2, D])
            t2b = cv.take(F32, [128, 2, D])
            poolw = cv.take(BF16, [128, 4, 2, 256])
            B_hb = [Buf("hb%d" % j) for j in range(NXT)]
            B_pTs = [Buf("pTs0"), Buf("pTs1")]
            B_ys = [Buf("ys0"), Buf("ys1")]
            B_t2 = Buf("t2")
            B_poolw = Buf("poolw")

            S.dma("sp", "d_cst", [(cst[:], cstd[:])], wr=[B_cst])
            load_tab(0, T_MIXPRE0)
            load_tab(1, T_PSCALE)
            load_tab(2, T_MIXPOST0)
            for j in range(NXT):
                dst = pre if j == 0 else hs[:, j - 1, :]
                S.dma("sp", "d_x%d" % j, [(dst, xin[j * 128:(j + 1) * 128, :])], wr=[B_pre if j == 0 else B_hs[j - 1]])
            S.dma("pool", "d_bands", [(bands[:].rearrange("p a b -> p (a b)"), bandd[:])], wr=[B_bands])
            S.dma("pool", "d_poolw", [(poolw.rearrange("p a b c -> p (a b c)"), wst[CH_POOL])], wr=[B_poolw])
            for _ in range(NSLOT):
                ring_issue()

            checkpoint(0)
            def xsrc(j):
                return (pre, B_pre) if j == 0 else (hs[:, j - 1, :], B_hs[j - 1])

            B_s1 = [Buf("s1_%d" % j) for j in range(NXT)]
            B_s3 = [Buf("s3_0"), Buf("s3_1")]
            B_t2p = [Buf("t2p0"), Buf("t2p1")]

            def stage1(j):
                src, bsrc = xsrc(j)
                sumsq(src, j, [bsrc], bst=B_s1[j])
                rstd_batch(1, j, 32 + j, 64 + j, bst=B_s1[j])
                S.op("dve", lambda e: e.scalar_tensor_tensor(
                    out=hb_all[:, j, :], in0=src, scalar=st[:, 64 + j:65 + j], in1=tab[:, 0, :],
                    op0=ALU.mult, op1=ALU.mult), rd=[bsrc, B_s1[j], B_tab[0]], wr=[B_hb[j]])

            def stage2(j):
                i = j - 1
                par = i % 2
                pb = 2 * par
                fns = []
                for k in range(8):
                    w = k // 2
                    own = (8 + w) if j == 1 else w
                    dst = ps[:, pb + k // 4, (k % 4) * 128:(k % 4 + 1) * 128]
                    fns.append(lambda e, dst=dst, k=k, own=own: e.matmul(
                        dst, lhsT=hb_all[:, j, k * 128:(k + 1) * 128], rhs=bands[:, own, :], start=True, stop=False))
                    fns.append(lambda e, dst=dst, k=k, w=w: e.matmul(
                        dst, lhsT=hb_all[:, j - 1, k * 128:(k + 1) * 128], rhs=bands[:, 4 + w, :], start=False, stop=True))
                S.op("pe", fns, rd=[B_hb[j], B_hb[j - 1], B_bands], wr=[B_ps[pb], B_ps[pb + 1]])
                S.op("act", lambda e: e.activation(
                    out=pTs[:, par].rearrange("p (a b) t -> p a (b t)", a=2), in_=ps[:, pb:pb + 2, :], func=AF.Copy),
                    rd=[B_ps[pb], B_ps[pb + 1]], wr=[B_pTs[par]])
                yb = 4 + 2 * par
                fns = []
                for g in range(4):
                    for cc in range(2):
                        dst = ps[:, yb + g // 2, (g % 2) * 256:(g % 2 + 1) * 256]
                        fns.append(lambda e, dst=dst, g=g, cc=cc: e.matmul(
                            dst, lhsT=pTs[:, par, 2 * g + cc, :], rhs=poolw[:, g, cc, :], start=(cc == 0), stop=(cc == 1)))
                S.op("pe", fns, rd=[B_pTs[par], B_poolw], wr=[B_ps[yb], B_ps[yb + 1]])

            def stage3(j):
                i = j - 1
                par = i % 2
                yb = 4 + 2 * par
                c0 = 20 + 4 * par
                ysv = ysb[:, par, :]
                t2v = t2b[:, par, :]
                S.op("dve", lambda e: e.tensor_tensor(
                    out=ysv.rearrange("p (a b) -> p a b", a=2), in0=ps[:, yb:yb + 2, :],
                    in1=tab[:, 1, :].rearrange("p (a b) -> p a b", a=2), op=ALU.mult),
                    rd=[B_ps[yb], B_ps[yb + 1], B_tab[1]], wr=[B_ys[par]])
                sumsq(ysv, c0, [B_ys[par]], bst=B_s3[par])
                rstd_batch(1, c0, c0 + 1, c0 + 2, bst=B_s3[par])
                S.op("dve", lambda e: e.scalar_tensor_tensor(
                    out=t2v, in0=ysv, scalar=st[:, c0 + 2:c0 + 3], in1=tab[:, 2, :], op0=ALU.mult, op1=ALU.mult),
                    rd=[B_ys[par], B_s3[par], B_tab[2]], wr=[B_t2p[par]])
                S.op("pool", lambda e: e.tensor_tensor(out=hs[:, i, :], in0=hs[:, i, :], in1=t2v, op=ALU.add),
                     rd=[B_t2p[par]], wr=[B_hs[i]])

            stage1(0)
            stage1(1)
            stage1(2)
            for j in range(1, NXT):
                if j + 2 < NXT:
                    stage1(j + 2)
                stage2(j)
                if j - 1 >= 1:
                    stage3(j - 1)
            stage3(NXT - 1)

            checkpoint(1)
            def ffn(layer, tiles, store):
                bt = barrier()
                cv = Carver()
                wd = cv.take(BF16, [128, NFC, D])
                hT = cv.take(BF16, [128, 8, 768])
                actT = cv.take(BF16, [128, NFC, 768])
                hb = cv.take(BF16, [128, 2, D])
                sg = cv.take(F32, [128, 2, 512])
                t2 = cv.take(F32, [128, D])
                B_wd, B_hT, B_actT, B_t2 = Buf("wd"), Buf("hT"), Buf("actT"), Buf("t2")
                B_hb = [Buf("hb0"), Buf("hb1")]
                B_sg = [Buf("sg0"), Buf("sg1")]
                chw = CH_WD0 if layer == 0 else CH_WD1
                S.dma("pool", "d_wd", [(wd[:, 2 * j:2 * j + 2, :].rearrange("p a b -> p (a b)"), wst[chw + j]) for j in range(11)],
                      wr=[B_wd], extra=bt)
                load_tab(0, T_FFNPRE0 if layer == 0 else T_FFNPRE1, extra=bt)
                load_tab(1, T_FFNPOST0 if layer == 0 else T_FFNPOST1, extra=bt)
                n = len(tiles)
                thirds = [tiles[0:6], tiles[6:n - 5], tiles[n - 5:]] if n == 17 else [tiles[0:6], tiles[6:11], tiles[11:]]
                it = 0
                ydb = 0
                B_pre = [Buf("fpre%d" % t_) for t_ in range(6)]

                def pre_stats(k3, ti):
                    i = thirds[k3][ti]
                    sumsq(hs[:, i, :], 40 + ti, [B_hs[i]], bst=B_pre[ti])
                    rstd_batch(1, 40 + ti, 48 + ti, 56 + ti, bst=B_pre[ti])

                def pre_hb(k3, ti):
                    i = thirds[k3][ti]
                    p2 = ti % 2
                    S.op("dve", lambda e: e.scalar_tensor_tensor(
                        out=hb[:, p2, :], in0=hs[:, i, :], scalar=st[:, 56 + ti:57 + ti], in1=tab[:, 0, :],
                        op0=ALU.mult, op1=ALU.mult), rd=[B_hs[i], B_pre[ti], B_tab[0]], wr=[B_hb[p2]])

                def pre_T(k3, ti):
                    p2 = ti % 2
                    tpv = psbf(p2)
                    S.op("pe", [lambda e, k=k: e.transpose(
                        out=tpv[:, k, :], in_=hb[:, p2, k * 128:(k + 1) * 128], identity=ident) for k in range(8)],
                        rd=[B_hb[p2], B_bands], wr=[B_ps[p2]])
                    S.op("act", lambda e: e.activation(
                        out=hT[:, :, ti * 128:(ti + 1) * 128], in_=tpv, func=AF.Copy), rd=[B_ps[p2]], wr=[B_hT])

                def prenorm_tile(k3, ti):
                    pre_stats(k3, ti)
                    pre_hb(k3, ti)
                    pre_T(k3, ti)

                n0 = len(thirds[0])
                for ti in range(n0):
                    pre_stats(0, ti)
                pre_hb(0, 0)
                for ti in range(n0):
                    if ti + 1 < n0:
                        pre_hb(0, ti + 1)
                    pre_T(0, ti)
                for k3, tl in enumerate(thirds):
                    nt = len(tl)
                    ntok = nt * 128
                    groups = [(0, 384), (384, ntok - 384)]
                    for c in range(NFC):
                        slot = ring_next()
                        W = ring[:, slot, :].rearrange("p (k f) -> p k f", k=8)
                        for (g0, N) in groups:
                            gb = 2 + 2 * (it % 2)
                            ub = gb + 1
                            S.op("pe", [lambda e, k=k, W=W, g0=g0, N=N, gb=gb: e.matmul(
                                ps[:, gb, 0:N], lhsT=W[:, k, 0:128], rhs=hT[:, k, g0:g0 + N], start=(k == 0), stop=(k == 7))
                                for k in range(8)], rd=[B_ring[slot], B_hT], wr=[B_ps[gb]])
                            S.op("pe", [lambda e, k=k, W=W, g0=g0, N=N, ub=ub: e.matmul(
                                ps[:, ub, 0:N], lhsT=W[:, k, 128:256], rhs=hT[:, k, g0:g0 + N], start=(k == 0), stop=(k == 7))
                                for k in range(8)], rd=[B_ring[slot], B_hT], wr=[B_ps[ub]])
                            s2 = it % 2
                            S.op("act", lambda e, s2=s2, gb=gb, N=N: e.activation(
                                out=sg[:, s2, 0:N], in_=ps[:, gb, 0:N], func=AF.Silu), rd=[B_ps[gb]], wr=[B_sg[s2]])
                            S.op("dve", lambda e, s2=s2, ub=ub, N=N, c=c, g0=g0: e.tensor_tensor(
                                out=actT[:, c, g0:g0 + N], in0=ps[:, ub, 0:N], in1=sg[:, s2, 0:N], op=ALU.mult),
                                rd=[B_ps[ub], B_sg[s2]], wr=[B_actT])
                            it += 1
                        ring_issue()
                    for ti, i in enumerate(tl):
                        yb = 2 + 2 * (ydb % 3)
                        ydb += 1
                        fns = []
                        for c in range(NFC):
                            for half in range(2):
                                fns.append(lambda e, c=c, half=half, ti=ti, yb=yb: e.matmul(
                                    ps[:, yb + half, :], lhsT=actT[:, c, ti * 128:(ti + 1) * 128],
                                    rhs=wd[:, c, half * 512:(half + 1) * 512], start=(c == 0), stop=(c == NFC - 1)))
                        S.op("pe", fns, rd=[B_actT, B_wd], wr=[B_ps[yb], B_ps[yb + 1]])
                        if k3 + 1 < 3 and ti < len(thirds[k3 + 1]):
                            prenorm_tile(k3 + 1, ti)
                        yv = ps[:, yb:yb + 2, :]
                        S.op("act", lambda e, yv=yv: e.activation(
                            out=junk[:].rearrange("p (a b) -> p a b", a=2), in_=yv, func=AF.Square, accum_out=st[:, 20:21]),
                            rd=[B_ps[yb], B_ps[yb + 1]], wr=[B_junk, B_st])
                        rstd_batch(1, 20, 21, 22)
                        S.op("dve", lambda e, yv=yv: e.scalar_tensor_tensor(
                            out=t2.rearrange("p (a b) -> p a b", a=2), in0=yv, scalar=st[:, 22:23],
                            in1=tab[:, 1, :].rearrange("p (a b) -> p a b", a=2), op0=ALU.mult, op1=ALU.mult),
                            rd=[B_ps[yb], B_ps[yb + 1], B_st, B_tab[1]], wr=[B_t2])
                        S.op("dve", lambda e, i=i: e.tensor_tensor(out=hs[:, i, :], in0=hs[:, i, :], in1=t2, op=ALU.add),
                             rd=[B_t2], wr=[B_hs[i]])
                        if store:
                            S.dma("sp", "d_out", [(outd[(i - 1) * 128:i * 128, :], hs[:, i, :])], rd=[B_hs[i]], wr=[B_out])

            if not os.environ.get('SKIPFFN'):
                ffn(0, list(range(NHS)), False)
            checkpoint(2)

            bt = barrier()
            cv = Carver()
            KT = cv.take(BF16, [128, 2, 2, NHS * 128])
            VA = cv.take(BF16, [128, NHS, 2, 65])
            kv_end = cv.off
            wk = cv.take(BF16, [128, 8, 256])
            wv = cv.take(BF16, [128, 8, 128])
            hT = cv.take(BF16, [128, 8, 512])
            hb = cv.take(BF16, [128, 2, D])
            B_KT, B_VA, B_wkv, B_hT = Buf("KT"), Buf("VA"), Buf("wkv"), Buf("hT")
            B_hb = [Buf("hb0"), Buf("hb1")]
            S.dma("pool", "d_wkv", [(wk.rearrange("p a b -> p (a b)"), wst[CH_WK]),
                                    (wv.rearrange("p a b -> p (a b)"), wst[CH_WV][:, 0:1024])], wr=[B_wkv], extra=bt)
            load_tab(0, T_KV, extra=bt)
            load_tab(1, T_BV, extra=bt)
            KVD = int(os.environ.get('KVDBG', '9'))
            if KVD >= 1:
                S.op("dve", lambda e: e.memset(VA[:, :, :, 64:65], 1.0), wr=[B_VA], extra=bt)
            S.op("dve", lambda e: e.memset(KT.rearrange("p a b c -> p (a b c)"), 0.0), wr=[B_KT], extra=bt)
            for i in range(NHS):
                sumsq(hs[:, i, :], i, [B_hs[i]])
            rstd_batch(NHS, 0, 32, 64)
            vcnt = 0
            for g0t in range(0, NHS, 4):
                tl = list(range(g0t, min(g0t + 4, NHS)))
                N = len(tl) * 128
                for ti, i in enumerate(tl):
                    p2 = ti % 2
                    S.op("dve", lambda e, i=i, p2=p2: e.scalar_tensor_tensor(
                        out=hb[:, p2, :], in0=hs[:, i, :], scalar=st[:, 64 + i:65 + i], in1=tab[:, 0, :],
                        op0=ALU.mult, op1=ALU.mult), rd=[B_hs[i], B_st, B_tab[0]], wr=[B_hb[p2]])
                    tpv = psbf(p2)
                    S.op("pe", [lambda e, k=k, p2=p2, tpv=tpv: e.transpose(
                        out=tpv[:, k, :], in_=hb[:, p2, k * 128:(k + 1) * 128], identity=ident) for k in range(8)],
                        rd=[B_hb[p2], B_bands], wr=[B_ps[p2]])
                    S.op("act", lambda e, ti=ti, tpv=tpv: e.activation(
                        out=hT[:, :, ti * 128:(ti + 1) * 128], in_=tpv, func=AF.Copy), rd=[B_ps[p2]], wr=[B_hT])
                for kvh in range(2 if KVD >= 3 else 0):
                    kb = 2 + kvh
                    S.op("pe", [lambda e, k=k, kvh=kvh, kb=kb, N=N: e.matmul(
                        ps[:, kb, 0:N], lhsT=wk[:, k, kvh * 128:(kvh + 1) * 128], rhs=hT[:, k, 0:N],
                        start=(k == 0), stop=(k == 7)) for k in range(8)], rd=[B_wkv, B_hT], wr=[B_ps[kb]])
                    for e_ in range(2):
                        S.op("act", lambda e, kvh=kvh, kb=kb, N=N, g0t=g0t, e_=e_: e.activation(
                            out=KT[64 * e_:64 * e_ + 64, kvh, e_, g0t * 128:g0t * 128 + N],
                            in_=ps[64 * e_:64 * e_ + 64, kb, 0:N], func=AF.Identity,
                            bias=cst[64 * e_:64 * e_ + 64, 8 + kvh:9 + kvh]), rd=[B_ps[kb], B_cst], wr=[B_KT])
                for ti, i in enumerate(tl if KVD >= 4 else []):
                    vb = 4 + (vcnt % 2)
                    vcnt += 1
                    S.op("pe", [lambda e, k=k, ti=ti, vb=vb: e.matmul(
                        ps[:, vb, 0:128], lhsT=hT[:, k, ti * 128:(ti + 1) * 128], rhs=wv[:, k, :],
                        start=(k == 0), stop=(k == 7)) for k in range(8)], rd=[B_wkv, B_hT], wr=[B_ps[vb]])
                    S.op("dve", lambda e, i=i, vb=vb: e.tensor_tensor(
                        out=VA[:, i, :, 0:64], in0=ps[:, vb, 0:128].rearrange("p (a b) -> p a b", a=2),
                        in1=tab[:, 1, 0:128].rearrange("p (a b) -> p a b", a=2), op=ALU.add),
                        rd=[B_ps[vb], B_tab[1]], wr=[B_VA])

            checkpoint(3)
            bt = barrier()
            cv = Carver()
            cv.off = kv_end
            wo = cv.take(BF16, [128, 8, D])
            bias = cv.take(F32, [128, 16, 2, 128])
            PT = cv.take(BF16, [128, 16, 2, 128])
            lg = cv.take(F32, [128, 2, 512])
            hT4 = cv.take(BF16, [128, 8, 512])
            QT4 = cv.take(BF16, [128, 8, 512])
            oT4 = cv.take(BF16, [128, 4, 8, 128])
            hb = cv.take(BF16, [128, 2, D])
            obf = cv.take(BF16, [128, D])
            t2 = cv.take(F32, [128, D])
            bohl = cv.take(BF16, [128, 2, D])
            ones = cv.take(BF16, [128, 128])
            B_wo, B_bias, B_PTx, B_hT4, B_QT4, B_obf, B_t2, B_bohl = (Buf(n) for n in
                                                                      ("wo", "bias", "PT", "hT4", "QT4", "obf", "t2", "bohl"))
            B_PT = [Buf("PT%d" % h_) for h_ in range(8)]
            B_oT4 = [Buf("oT4_%d" % t) for t in range(4)]
            B_lg = [Buf("lg0"), Buf("lg1")]
            B_hb = [Buf("hb0"), Buf("hb1")]
            B_esink, B_stb = Buf("esink"), Buf("stb")
            B_pn = [Buf("pn0"), Buf("pn1")]
            B_den = [Buf("den0"), Buf("den1")]
            S.dma("pool", "d_wqo", [(wo[:, 2 * j:2 * j + 2, :].rearrange("p a b -> p (a b)"), wst[CH_WO + j]) for j in range(4)],
                  wr=[B_wo], extra=bt)
            S.dma("sp", "d_bias", [(bias.rearrange("p a b c -> p (a b c)"), btabd[0])], wr=[B_bias], extra=bt)
            load_tab(0, T_MIXPRE1, extra=bt)
            load_tab(1, T_MIXPOST1, extra=bt)
            load_tab(2, T_BO, extra=bt)
            S.op("dve", lambda e: e.memset(ones, 1.0), wr=[B_bohl], extra=bt)
            S.op("dve", lambda e: e.tensor_copy(out=bohl[:, 0, :], in_=tab[:, 2, :]), rd=[B_tab[2]], wr=[B_bohl])
            S.op("dve", lambda e: e.tensor_tensor(out=t2, in0=tab[:, 2, :], in1=bohl[:, 0, :], op=ALU.subtract),
                 rd=[B_tab[2], B_bohl], wr=[B_t2])
            S.op("dve", lambda e: e.tensor_copy(out=bohl[:, 1, :], in_=t2), rd=[B_t2], wr=[B_bohl])
            S.op("act", lambda e: e.activation(out=st[:, 96:112], in_=cst[:, 10:26], func=AF.Exp),
                 rd=[B_cst], wr=[B_esink], extra=bt)
            for i in range(1, NHS):
                sumsq(hs[:, i, :], i, [B_hs[i]], bst=B_stb)
            rstd_batch(NHS, 0, 32, 64, bst=B_stb)
            hsplit = [(0, 7), (7, 14), (14, 16)]

            def q_stage(g):
                for ti in range(4):
                    i = 1 + 4 * g + ti
                    p2 = ti % 2
                    tb = 6 + p2
                    S.op("dve", lambda e, i=i, p2=p2: e.scalar_tensor_tensor(
                        out=hb[:, p2, :], in0=hs[:, i, :], scalar=st[:, 64 + i:65 + i], in1=tab[:, 0, :],
                        op0=ALU.mult, op1=ALU.mult), rd=[B_hs[i], B_stb, B_tab[0]], wr=[B_hb[p2]])
                    tpv = psbf(tb)
                    S.op("pe", [lambda e, k=k, p2=p2, tpv=tpv: e.transpose(
                        out=tpv[:, k, :], in_=hb[:, p2, k * 128:(k + 1) * 128], identity=ident) for k in range(8)],
                        rd=[B_hb[p2], B_bands], wr=[B_ps[tb]])
                    S.op("act", lambda e, ti=ti, tpv=tpv: e.activation(
                        out=hT4[:, :, ti * 128:(ti + 1) * 128], in_=tpv, func=AF.Copy), rd=[B_ps[tb]], wr=[B_hT4])
                for j in range(4):
                    slot = ring_next()
                    W = ring[:, slot, :].rearrange("p (k f) -> p k f", k=8)
                    for e2 in range(2):
                        hp = 2 * j + e2
                        qb = 6 + hp % 2
                        S.op("pe", [lambda e, k=k, W=W, e2=e2, qb=qb: e.matmul(
                            ps[:, qb, :], lhsT=W[:, k, e2 * 128:(e2 + 1) * 128], rhs=hT4[:, k, :],
                            start=(k == 0), stop=(k == 7)) for k in range(8)], rd=[B_ring[slot], B_hT4], wr=[B_ps[qb]])
                        if hp % 2 == 0:
                            S.op("act", lambda e, hp=hp, qb=qb: e.activation(
                                out=QT4[:, hp, :], in_=ps[:, qb, :], func=AF.Identity, bias=cst[:, hp:hp + 1]),
                                rd=[B_ps[qb], B_cst], wr=[B_QT4])
                        else:
                            S.op("dve", lambda e, hp=hp, qb=qb: e.tensor_scalar(
                                out=QT4[:, hp, :], in0=ps[:, qb, :], scalar1=cst[:, hp:hp + 1], scalar2=None, op0=ALU.add),
                                rd=[B_ps[qb], B_cst], wr=[B_QT4])
                    ring_issue()

            def s_tile(g, ti, hook=None):
                i = 1 + 4 * g + ti
                p2 = i % 2
                den = st[:, 112 + 16 * p2:128 + 16 * p2]

                def scores(hp):
                    kvh = hp // 4
                    sbk = 1 + hp % 2
                    sv = ps[:, sbk, :].rearrange("p (e b q) -> p e b q", e=2, b=2)
                    fns = []
                    for e_ in range(2):
                        for blk in range(2):
                            kt = i - 1 + blk
                            fns.append(lambda e, e_=e_, blk=blk, kt=kt: e.matmul(
                                sv[:, e_, blk, :], lhsT=KT[:, kvh, e_, kt * 128:(kt + 1) * 128],
                                rhs=QT4[:, hp, ti * 128:(ti + 1) * 128], start=True, stop=True))
                    S.op("pe", fns, rd=[B_KT, B_QT4], wr=[B_ps[sbk]])

                def c1_bank(bi):
                    h0, h1 = hsplit[bi]
                    nh = h1 - h0
                    ov = ps[:, 3 + bi, 0:nh * 65].rearrange("p (h c) -> p h c", c=65)
                    S.op("dve", lambda e: e.tensor_tensor(
                        out=den[:, h0:h1], in0=ov[:, :, 64], in1=st[:, 96 + h0:96 + h1], op=ALU.add),
                        rd=[B_ps[3 + bi], B_esink], wr=[B_den[p2]])
                    S.op("dve", lambda e: e.reciprocal(out=den[:, h0:h1], in_=den[:, h0:h1]), rd=[], wr=[B_den[p2]])
                    S.op("dve", lambda e: e.tensor_tensor(
                        out=obf.rearrange("p (h c) -> p h c", c=64)[:, h0:h1, :], in0=ov[:, :, 0:64],
                        in1=den[:, h0:h1].unsqueeze(2).to_broadcast([128, nh, 64]), op=ALU.mult),
                        rd=[B_ps[3 + bi], B_den[p2]], wr=[B_obf])

                scores(0)
                scores(1)
                for hp in range(8):
                    kvh = hp // 4
                    sbk = 1 + hp % 2
                    l2 = hp % 2
                    S.op("dve", lambda e, hp=hp, sbk=sbk, l2=l2: e.scalar_tensor_tensor(
                        out=lg[:, l2, :], in0=ps[:, sbk, :], scalar=0.125,
                        in1=bias[:, 2 * hp:2 * hp + 2, :, :].rearrange("p a b c -> p (a b c)"), op0=ALU.mult, op1=ALU.add),
                        rd=[B_ps[sbk], B_bias], wr=[B_lg[l2]])
                    S.op("act", lambda e, hp=hp, l2=l2: e.activation(
                        out=PT[:, 2 * hp:2 * hp + 2, :, :].rearrange("p a b c -> p (a b c)"), in_=lg[:, l2, :], func=AF.Exp),
                        rd=[B_lg[l2]], wr=[B_PT[hp]])
                    if hp + 2 < 8:
                        scores(hp + 2)
                    fns = []
                    for e_ in range(2):
                        h = 2 * hp + e_
                        ob = 3 + h // 7
                        oc = (h % 7) * 65
                        for blk in range(2):
                            kt = i - 1 + blk
                            fns.append(lambda e, h=h, ob=ob, oc=oc, blk=blk, kt=kt, kvh=kvh: e.matmul(
                                ps[:, ob, oc:oc + 65], lhsT=PT[:, h, blk, :], rhs=VA[:, kt, kvh, :],
                                start=(blk == 0), stop=(blk == 1)))
                    S.op("pe", fns, rd=[B_PT[hp], B_VA],
                         wr=[B_ps[b_] for b_ in sorted({3 + (2 * hp) // 7, 3 + (2 * hp + 1) // 7})])
                    if hp == 1 and hook is not None:
                        hook()
                    if hp == 4:
                        c1_bank(0)
                    if hp == 7:
                        c1_bank(1)
                if i == 1:
                    S.dma("sp", "d_bias", [(bias.rearrange("p a b c -> p (a b c)"), btabd[1])], wr=[B_bias])

                def tail():
                    c1_bank(2)
                    tpv = psbf(0)
                    S.op("pe", [lambda e, k=k: e.transpose(
                        out=tpv[:, k, :], in_=obf[:, k * 128:(k + 1) * 128], identity=ident) for k in range(8)],
                        rd=[B_obf, B_bands], wr=[B_ps[0]])
                    S.op("act", lambda e: e.activation(out=oT4[:, ti], in_=tpv, func=AF.Copy), rd=[B_ps[0]], wr=[B_oT4[ti]])
                return tail

            def o_tile(g, ti, ybase):
                i = 1 + 4 * g + ti
                p2 = ti % 2
                c0 = 20 + 4 * p2
                for half in range(2):
                    fns = [lambda e, k=k, half=half: e.matmul(
                        ps[:, ybase + half, :], lhsT=oT4[:, ti, k, :], rhs=wo[:, k, half * 512:(half + 1) * 512],
                        start=(k == 0), stop=False) for k in range(8)]
                    for hl in range(2):
                        fns.append(lambda e, hl=hl, half=half: e.matmul(
                            ps[:, ybase + half, :], lhsT=ones[0:1, :], rhs=bohl[0:1, hl, half * 512:(half + 1) * 512],
                            start=False, stop=(hl == 1)))
                    S.op("pe", fns, rd=[B_oT4[ti], B_wo, B_bohl], wr=[B_ps[ybase + half]])
                yv = ps[:, ybase:ybase + 2, :]
                S.op("act", lambda e: e.activation(
                    out=junk[:].rearrange("p (a b) -> p a b", a=2), in_=yv, func=AF.Square,
                    accum_out=st[:, c0:c0 + 1]), rd=[B_ps[ybase], B_ps[ybase + 1]], wr=[B_junk, B_pn[p2]])
                rstd_batch(1, c0, c0 + 1, c0 + 2, bst=B_pn[p2])
                S.op("dve", lambda e: e.scalar_tensor_tensor(
                    out=t2.rearrange("p (a b) -> p a b", a=2), in0=yv, scalar=st[:, c0 + 2:c0 + 3],
                    in1=tab[:, 1, :].rearrange("p (a b) -> p a b", a=2), op0=ALU.mult, op1=ALU.mult),
                    rd=[B_ps[ybase], B_ps[ybase + 1], B_pn[p2], B_tab[1]], wr=[B_t2])
                S.op("pool", lambda e: e.tensor_tensor(out=hs[:, i, :], in0=hs[:, i, :], in1=t2, op=ALU.add),
                     rd=[B_t2], wr=[B_hs[i]])

            q_stage(0)
            for g in range(4):
                tail = None
                for ti in range(4):
                    tail = s_tile(g, ti, hook=tail)
                tail()
                if g + 1 < 4:
                    q_stage(g + 1)
                for t_ in range(4):
                    o_tile(g, t_, 6 if t_ % 2 == 0 else 1)

            checkpoint(4)
            ffn(1, list(range(1, NHS)), True)


        except _Stop:
            pass
        S.wait("sp", B_out.w)

        with nc.Block() as block:
            @block.tensor
            def _(e):
                for f in S.ops["pe"]:
                    f(e)

            @block.scalar
            def _(e):
                for f in S.ops["act"]:
                    f(e)

            @block.vector
            def _(e):
                for f in S.ops["dve"]:
                    f(e)

            @block.gpsimd
            def _(e):
                for f in S.ops["pool"]:
                    f(e)

            @block.sync
            def _(e):
                for f in S.ops["sp"]:
                    f(e)
    return nc


def _t5_bucket(d):
    max_exact = 16
    df = np.maximum(d, 1).astype(np.float32)
    large = max_exact + (np.log(df / max_exact) / np.log(128 / max_exact) * (32 - max_exact)).astype(np.int32)
    large = np.minimum(large, 31)
    return np.where(d < max_exact, d, large)


def _band_consts(first_core):
    out = np.zeros((128, 13, 128), np.float32)
    tp = np.arange(128)[:, None]
    t = np.arange(128)[None, :]
    for wi, w in enumerate((2, 4, 8, 16)):
        dlt = t - tp
        own = np.where((dlt >= 0) & (dlt < w), 1.0 / w, 0.0) - (dlt == 0)
        prev = np.where((t + 128 - tp) < w, 1.0 / w, 0.0)
        out[:, wi, :] = own
        out[:, 4 + wi, :] = prev
        if first_core:
            pos = np.maximum(t - 112 + 1, 1)
            cnt = np.minimum(pos, w).astype(np.float32)
            valid = (dlt >= 0) & (dlt < w) & (tp >= 112)
            out[:, 8 + wi, :] = np.where(valid, 1.0 / cnt, 0.0) - (dlt == 0)
        else:
            out[:, 8 + wi, :] = own
    out[:, 12, :] = np.eye(128, dtype=np.float32)
    return out.reshape(128, 13 * 128)


def _bias_tables(rel_bias, first_core):
    s = np.arange(128)[:, None]
    q = np.arange(128)[None, :]
    tabs = np.empty((2, 128, 16, 2, 128), np.float32)
    for blk in range(2):
        d = q - s + (128 if blk == 0 else 0)
        ok = (d >= 0) & (d < 128)
        bidx = _t5_bucket(np.maximum(d, 0))
        g = rel_bias[bidx]
        g = np.where(ok[:, :, None], g, np.float32(NEG))
        g = np.transpose(g, (0, 2, 1))
        tabs[1, :, :, blk, :] = g
        if blk == 0 and first_core:
            g = np.where((s >= 112)[:, :, None] & np.ones((1, 1, 128), bool), g, np.float32(NEG))
        tabs[0, :, :, blk, :] = g
    return tabs.reshape(2, 128, 16 * 2 * 128)


def _prep_common(inp):
    f = np.float32
    wst = np.zeros((NCHUNK, 128, 2048), f)
    pw = inp["pool_w"][0]
    wst[CH_POOL] = pw.reshape(4, 2, 128, 256).transpose(2, 0, 1, 3).reshape(128, 2048)
    for l in range(2):
        gu = inp["w_gate_up"][l]
        gate = gu[:, :DFF].reshape(8, 128, NFC, 128)
        up = gu[:, DFF:].reshape(8, 128, NFC, 128)
        blk = np.stack([gate, up], axis=3)
        blk = blk.transpose(2, 1, 0, 3, 4).reshape(NFC, 128, 2048)
        base = CH_GU0 if l == 0 else CH_GU1
        wst[base:base + NFC] = blk
        wdn = inp["w_down"][l].reshape(11, 2, 128, D).transpose(0, 2, 1, 3).reshape(11, 128, 2048)
        base = CH_WD0 if l == 0 else CH_WD1
        wst[base:base + 11] = wdn
    wk = inp["w_k"].reshape(8, 128, 2, 64)
    wkd = np.stack([wk, wk], axis=3)
    wst[CH_WK] = wkd.transpose(1, 0, 2, 3, 4).reshape(128, 2048)
    wst[CH_WV][:, :1024] = inp["w_v"].reshape(8, 128, 128).transpose(1, 0, 2).reshape(128, 1024)
    wq = inp["w_q"][0].reshape(8, 128, 4, 256)
    wst[CH_WQ:CH_WQ + 4] = wq.transpose(2, 1, 0, 3).reshape(4, 128, 2048)
    wo = inp["w_o"][0].reshape(4, 2, 128, D)
    wst[CH_WO:CH_WO + 4] = wo.transpose(0, 2, 1, 3).reshape(4, 128, 2048)

    rows = [inp["norm_mix_pre"][0], inp["pool_scale"][0], inp["norm_mix_post"][0], inp["norm_ffn_pre"][0],
            inp["norm_ffn_post"][0], inp["kv_norm"], inp["norm_mix_pre"][1], inp["norm_mix_post"][1],
            inp["b_o"][0], inp["norm_ffn_pre"][1], inp["norm_ffn_post"][1],
            np.concatenate([inp["b_v"], np.zeros(D - 128, f)])]
    tabs = np.ascontiguousarray(np.broadcast_to(np.stack(rows)[:, None, :], (NTAB, 128, D))).astype(f)

    cst = np.zeros((128, NCST), f)
    cst[:, 0:8] = inp["b_q"][0].reshape(8, 128).T
    bk = inp["b_k"].reshape(2, 64)
    cst[:, 8:10] = np.concatenate([bk, bk], axis=1).T
    cst[:, 10:26] = np.broadcast_to(inp["sinks"][0][None, :], (128, 16))
    cst[:, 26] = EPS
    return wst, tabs, cst


_NC_CACHE = {}


def kernel(**inputs):
    inp = {k: np.asarray(v, dtype=np.float32) for k, v in inputs.items()}
    x = inp["x"]
    meta = inp["meta_tokens"]
    wst, tabs, cst = _prep_common(inp)
    bands = [_band_consts(True), _band_consts(False)]
    btab = [_bias_tables(inp["rel_bias"], True), _bias_tables(inp["rel_bias"], False)]
    in_maps = []
    for c in range(8):
        b, r = c // 4, c % 4
        xin = np.zeros((NXT * 128, D), np.float32)
        if r == 0:
            xin[128 + 112:256] = meta
            xin[256:] = x[b, 0:2048]
        else:
            xin[:] = x[b, 2048 * r - 256:2048 * r + 2048]
        fc = (r == 0)
        in_maps.append({"xin": xin, "wst": wst, "tabs": tabs, "cst": cst,
                        "bands": bands[0 if fc else 1], "btab": btab[0 if fc else 1]})
    if "nc" not in _NC_CACHE:
        _NC_CACHE["nc"] = build_program()
    nc = _NC_CACHE["nc"]
    res = run_bass_kernel_spmd(nc, in_maps, core_ids=list(range(8)))
    out = np.empty((2, 8192, D), np.float32)
    for c in range(8):
        b, r = c // 4, c % 4
        out[b, 2048 * r:2048 * r + 2048] = res.results[c]["out"]
    return out
```

```python
import os
import numpy as np
from contextlib import ExitStack
import concourse.bass as bass
import concourse.mybir as mybir
from concourse.bass_utils import run_bass_kernel_spmd

F32 = mybir.dt.float32
BF16 = mybir.dt.bfloat16
AF = mybir.ActivationFunctionType
ALU = mybir.AluOpType

D = 1024
DFF = 2816
NFC = DFF // 128
NXT = 18
NHS = 17
EPS = 1e-6
NSLOT = 4
NEG = -30000.0

CH_POOL = 0
CH_GU0 = 1
CH_WD0 = CH_GU0 + 22
CH_WK = CH_WD0 + 11
CH_WV = CH_WK + 1
CH_WQ = CH_WV + 1
CH_WO = CH_WQ + 4
CH_GU1 = CH_WO + 4
CH_WD1 = CH_GU1 + 22
NCHUNK = CH_WD1 + 11

(T_MIXPRE0, T_PSCALE, T_MIXPOST0, T_FFNPRE0, T_FFNPOST0, T_KV, T_MIXPRE1, T_MIXPOST1, T_BO,
 T_FFNPRE1, T_FFNPOST1, T_BV) = range(12)
NTAB = 12
NCST = 32


class Buf:
    def __init__(self, name):
        self.name = name
        self.w = None
        self.r = {}


class Sched:
    QS = ("pe", "act", "dve", "pool", "sp")

    def __init__(self, nc, es):
        self.nc = nc
        self.es = es
        self.ops = {k: [] for k in self.QS}
        self.sem = {}
        self.count = {k: 0 for k in self.QS}
        self.seen = {k: {} for k in self.QS}
        self.dcount = {}
        for q in self.QS:
            self.getsem(q)

    def getsem(self, name):
        if name not in self.sem:
            self.sem[name] = self.es.enter_context(self.nc.semaphore(name))
        return self.sem[name]

    def wait(self, q, tok):
        if tok is None:
            return
        sname, val = tok
        if self.seen[q].get(sname, 0) >= val:
            return
        self.seen[q][sname] = val
        sem = self.getsem(sname)
        self.ops[q].append(lambda e, sem=sem, val=val: e.wait_ge(sem, val))

    def _deps(self, q, rd, wr, extra):
        for t in extra:
            self.wait(q, t)
        for b in rd:
            self.wait(q, b.w)
        for b in wr:
            self.wait(q, b.w)
            for sn, v in b.r.items():
                self.wait(q, (sn, v))

    def _mark(self, tok, rd, wr):
        for b in rd:
            if b.r.get(tok[0], 0) < tok[1]:
                b.r[tok[0]] = tok[1]
        for b in wr:
            b.w = tok
            b.r = {}

    def op(self, q, fn, rd=(), wr=(), extra=()):
        self._deps(q, rd, wr, extra)
        fns = fn if isinstance(fn, (list, tuple)) else [fn]
        for f in fns[:-1]:
            self.ops[q].append(lambda e, f=f: f(e))
        self.count[q] += 1
        sem = self.sem[q]
        last = fns[-1]
        self.ops[q].append(lambda e, f=last, sem=sem: f(e).then_inc(sem, 1))
        tok = (q, self.count[q])
        self._mark(tok, rd, wr)
        return tok

    def dma(self, q, semname, pairs, rd=(), wr=(), extra=()):
        self._deps(q, rd, wr, extra)
        sem = self.getsem(semname)
        for out, in_ in pairs:
            self.ops[q].append(lambda e, out=out, in_=in_, sem=sem: e.dma_start(out=out, in_=in_).then_inc(sem, 16))
        self.dcount[semname] = self.dcount.get(semname, 0) + 16 * len(pairs)
        tok = (semname, self.dcount[semname])
        self._mark(tok, rd, wr)
        return tok


class _Stop(Exception):
    pass


def build_program(stop=99):
    nc = bass.Bass("TRN2", target_bir_lowering=False)
    xin = nc.dram_tensor("xin", [NXT * 128, D], F32, kind="ExternalInput").ap()
    wst = nc.dram_tensor("wst", [NCHUNK, 128, 2048], F32, kind="ExternalInput").ap()
    tabs = nc.dram_tensor("tabs", [NTAB, 128, D], F32, kind="ExternalInput").ap()
    cstd = nc.dram_tensor("cst", [128, NCST], F32, kind="ExternalInput").ap()
    bandd = nc.dram_tensor("bands", [128, 13 * 128], F32, kind="ExternalInput").ap()
    btabd = nc.dram_tensor("btab", [2, 128, 16 * 2 * 128], F32, kind="ExternalInput").ap()
    outd = nc.dram_tensor("out", [16 * 128, D], F32, kind="ExternalOutput").ap()
    dbgd = nc.dram_tensor("dbg", [NHS * 128, D], F32, kind="ExternalOutput").ap() if stop != 99 else None

    with ExitStack() as es:
        S = Sched(nc, es)

        def sb(name, shape, dt):
            return es.enter_context(nc.sbuf_tensor(name, shape, dt))

        hs = sb("hs", [128, NHS, D], F32)
        tab = sb("tab", [128, 3, D], F32)
        ring = sb("ring", [128, NSLOT, 2048], BF16)
        bands = sb("bandsb", [128, 13, 128], BF16)
        cst = sb("cstsb", [128, NCST], F32)
        st = sb("st", [128, 160], F32)
        junk = sb("junk", [128, D], BF16)
        PH_ELEMS = 52 * 1024
        phase = sb("phase", [128, PH_ELEMS], BF16)
        ps = es.enter_context(nc.psum_tensor("ps", [128, 8, 512], F32))

        ident = bands[:, 12, :]
        B_hs = [Buf("hs%d" % i) for i in range(NHS)]
        B_tab = [Buf("tab%d" % i) for i in range(3)]
        B_ring = [Buf("ring%d" % i) for i in range(NSLOT)]
        B_bands = Buf("bands")
        B_cst = Buf("cst")
        B_st = Buf("st")
        B_junk = Buf("junk")
        B_ps = [Buf("ps%d" % i) for i in range(8)]
        B_out = Buf("out")

        def psbf(b):
            return ps[:, b, :].bitcast(BF16).rearrange("p (k t) -> p k t", k=8)

        class Carver:
            def __init__(self):
                self.off = 0

            def take(self, dt, shape):
                n = int(np.prod(shape[1:]))
                nb = n if dt == BF16 else 2 * n
                nb = (nb + 15) // 16 * 16
                assert self.off + nb <= PH_ELEMS, ("phase overflow", self.off + nb)
                v = phase[:, self.off:self.off + nb]
                self.off += nb
                if dt == F32:
                    v = v.bitcast(F32)
                v = v[:, 0:n]
                if len(shape) == 3:
                    v = v.rearrange("p (a b) -> p a b", a=shape[1])
                elif len(shape) == 4:
                    v = v.rearrange("p (a b c) -> p a b c", a=shape[1], b=shape[2])
                return v

        def barrier():
            toks = [(q, S.count[q]) for q in ("pe", "act", "dve", "pool") if S.count[q] > 0]
            for q in ("pe", "act", "dve"):
                for t in toks:
                    if t[0] != q:
                        S.wait(q, t)
            return toks

        def load_tab(slot, idx, extra=()):
            return S.dma("sp", "d_tab%d" % slot, [(tab[:, slot, :], tabs[idx])], wr=[B_tab[slot]], extra=extra)

        ring_seq = []
        for l in range(2):
            if l == 1:
                for g_ in range(4):
                    for j_ in range(4):
                        ring_seq.append(CH_WQ + j_)
            for third in range(3):
                for c in range(NFC):
                    ring_seq.append((CH_GU0 if l == 0 else CH_GU1) + c)
        ring_state = {"issued": 0, "used": 0}

        def ring_issue():
            i = ring_state["issued"]
            if i >= len(ring_seq):
                return
            slot = i % NSLOT
            S.dma("pool", "d_ring%d" % slot, [(ring[:, slot, :], wst[ring_seq[i]])], wr=[B_ring[slot]])
            ring_state["issued"] += 1

        def ring_next():
            i = ring_state["used"]
            ring_state["used"] += 1
            return i % NSLOT

        def rstd_batch(n, col_ss, col_sd, col_r, bst=None):
            bst = bst or B_st
            S.op("act", lambda e: e.activation(out=st[:, col_sd:col_sd + n], in_=st[:, col_ss:col_ss + n],
                                                func=AF.Sqrt, bias=cst[:, 26:27], scale=1.0 / D),
                 rd=[B_cst], wr=[bst])
            S.op("dve", lambda e: e.reciprocal(out=st[:, col_r:col_r + n], in_=st[:, col_sd:col_sd + n]), wr=[bst])

        def sumsq(src, col, rd, bst=None):
            S.op("act", lambda e: e.activation(out=junk[:], in_=src, func=AF.Square, accum_out=st[:, col:col + 1]),
                 rd=rd, wr=[B_junk, bst or B_st])

        def checkpoint(n):
            if stop == n:
                bt_ = barrier()
                for i_ in range(NHS):
                    S.dma("sp", "d_dbg", [(dbgd[i_ * 128:(i_ + 1) * 128, :], hs[:, i_, :])], rd=[B_hs[i_]], wr=[B_out], extra=bt_)
                raise _Stop()

        try:
            B_pre = Buf("pre")
            B_ph = Buf("phase_generic")
            cv = Carver()
            hb_all = cv.take(BF16, [128, NXT, D])
            pre = cv.take(F32, [128, D])
            pTs = cv.take(BF16, [128, 2, 8, 128])
            ysb = cv.take(F32, [128, 2, D])
            t2b = cv.take(F32, [128, 2, D])
            poolw = cv.take(BF16, [128, 4, 2, 256])
            B_hb = [Buf("hb%d" % j) for j in range(NXT)]
            B_pTs = [Buf("pTs0"), Buf("pTs1")]
            B_ys = [Buf("ys0"), Buf("ys1")]
            B_t2 = Buf("t2")
            B_poolw = Buf("poolw")

            S.dma("sp", "d_cst", [(cst[:], cstd[:])], wr=[B_cst])
            load_tab(0, T_MIXPRE0)
            load_tab(1, T_PSCALE)
            load_tab(2, T_MIXPOST0)
            for j in range(NXT):
                dst = pre if j == 0 else hs[:, j - 1, :]
                S.dma("sp", "d_x%d" % j, [(dst, xin[j * 128:(j + 1) * 128, :])], wr=[B_pre if j == 0 else B_hs[j - 1]])
            S.dma("pool", "d_bands", [(bands[:].rearrange("p a b -> p (a b)"), bandd[:])], wr=[B_bands])
            S.dma("pool", "d_poolw", [(poolw.rearrange("p a b c -> p (a b c)"), wst[CH_POOL])], wr=[B_poolw])
            for _ in range(NSLOT):
                ring_issue()

            checkpoint(0)
            def xsrc(j):
                return (pre, B_pre) if j == 0 else (hs[:, j - 1, :], B_hs[j - 1])

            B_s1 = [Buf("s1_%d" % j) for j in range(NXT)]
            B_s3 = [Buf("s3_0"), Buf("s3_1")]
            B_t2p = [Buf("t2p0"), Buf("t2p1")]

            def stage1(j):
                src, bsrc = xsrc(j)
                sumsq(src, j, [bsrc], bst=B_s1[j])
                rstd_batch(1, j, 32 + j, 64 + j, bst=B_s1[j])
                S.op("dve", lambda e: e.scalar_tensor_tensor(
                    out=hb_all[:, j, :], in0=src, scalar=st[:, 64 + j:65 + j], in1=tab[:, 0, :],
                    op0=ALU.mult, op1=ALU.mult), rd=[bsrc, B_s1[j], B_tab[0]], wr=[B_hb[j]])

            def stage2(j):
                i = j - 1
                par = i % 2
                pb = 2 * par
                fns = []
                for k in range(8):
                    w = k // 2
                    own = (8 + w) if j == 1 else w
                    dst = ps[:, pb + k // 4, (k % 4) * 128:(k % 4 + 1) * 128]
                    fns.append(lambda e, dst=dst, k=k, own=own: e.matmul(
                        dst, lhsT=hb_all[:, j, k * 128:(k + 1) * 128], rhs=bands[:, own, :], start=True, stop=False))
                    fns.append(lambda e, dst=dst, k=k, w=w: e.matmul(
                        dst, lhsT=hb_all[:, j - 1, k * 128:(k + 1) * 128], rhs=bands[:, 4 + w, :], start=False, stop=True))
                S.op("pe", fns, rd=[B_hb[j], B_hb[j - 1], B_bands], wr=[B_ps[pb], B_ps[pb + 1]])
                S.op("act", lambda e: e.activation(
                    out=pTs[:, par].rearrange("p (a b) t -> p a (b t)", a=2), in_=ps[:, pb:pb + 2, :], func=AF.Copy),
                    rd=[B_ps[pb], B_ps[pb + 1]], wr=[B_pTs[par]])
                yb = 4 + 2 * par
                fns = []
                for g in range(4):
                    for cc in range(2):
                        dst = ps[:, yb + g // 2, (g % 2) * 256:(g % 2 + 1) * 256]
                        fns.append(lambda e, dst=dst, g=g, cc=cc: e.matmul(
                            dst, lhsT=pTs[:, par, 2 * g + cc, :], rhs=poolw[:, g, cc, :], start=(cc == 0), stop=(cc == 1)))
                S.op("pe", fns, rd=[B_pTs[par], B_poolw], wr=[B_ps[yb], B_ps[yb + 1]])

            def stage3(j):
                i = j - 1
                par = i % 2
                yb = 4 + 2 * par
                c0 = 20 + 4 * par
                ysv = ysb[:, par, :]
                t2v = t2b[:, par, :]
                S.op("dve", lambda e: e.tensor_tensor(
                    out=ysv.rearrange("p (a b) -> p a b", a=2), in0=ps[:, yb:yb + 2, :],
                    in1=tab[:, 1, :].rearrange("p (a b) -> p a b", a=2), op=ALU.mult),
                    rd=[B_ps[yb], B_ps[yb + 1], B_tab[1]], wr=[B_ys[par]])
                sumsq(ysv, c0, [B_ys[par]], bst=B_s3[par])
                rstd_batch(1, c0, c0 + 1, c0 + 2, bst=B_s3[par])
                S.op("dve", lambda e: e.scalar_tensor_tensor(
                    out=t2v, in0=ysv, scalar=st[:, c0 + 2:c0 + 3], in1=tab[:, 2, :], op0=ALU.mult, op1=ALU.mult),
                    rd=[B_ys[par], B_s3[par], B_tab[2]], wr=[B_t2p[par]])
                S.op("pool", lambda e: e.tensor_tensor(out=hs[:, i, :], in0=hs[:, i, :], in1=t2v, op=ALU.add),
                     rd=[B_t2p[par]], wr=[B_hs[i]])

            stage1(0)
            stage1(1)
            stage1(2)
            for j in range(1, NXT):
                if j + 2 < NXT:
                    stage1(j + 2)
                stage2(j)
                if j - 1 >= 1:
                    stage3(j - 1)
            stage3(NXT - 1)

            checkpoint(1)
            def ffn(layer, tiles, store):
                bt = barrier()
                cv = Carver()
                wd = cv.take(BF16, [128, NFC, D])
                hT = cv.take(BF16, [128, 8, 768])
                actT = cv.take(BF16, [128, NFC, 768])
                hb = cv.take(BF16, [128, 2, D])
                sg = cv.take(F32, [128, 2, 512])
                t2 = cv.take(F32, [128, D])
                B_wd, B_hT, B_actT, B_t2 = Buf("wd"), Buf("hT"), Buf("actT"), Buf("t2")
                B_hb = [Buf("hb0"), Buf("hb1")]
                B_sg = [Buf("sg0"), Buf("sg1")]
                chw = CH_WD0 if layer == 0 else CH_WD1
                S.dma("pool", "d_wd", [(wd[:, 2 * j:2 * j + 2, :].rearrange("p a b -> p (a b)"), wst[chw + j]) for j in range(11)],
                      wr=[B_wd], extra=bt)
                load_tab(0, T_FFNPRE0 if layer == 0 else T_FFNPRE1, extra=bt)
                load_tab(1, T_FFNPOST0 if layer == 0 else T_FFNPOST1, extra=bt)
                n = len(tiles)
                thirds = [tiles[0:6], tiles[6:n - 5], tiles[n - 5:]] if n == 17 else [tiles[0:6], tiles[6:11], tiles[11:]]
                it = 0
                ydb = 0
                B_pre = [Buf("fpre%d" % t_) for t_ in range(6)]

                def pre_stats(k3, ti):
                    i = thirds[k3][ti]
                    sumsq(hs[:, i, :], 40 + ti, [B_hs[i]], bst=B_pre[ti])
                    rstd_batch(1, 40 + ti, 48 + ti, 56 + ti, bst=B_pre[ti])

                def pre_hb(k3, ti):
                    i = thirds[k3][ti]
                    p2 = ti % 2
                    S.op("dve", lambda e: e.scalar_tensor_tensor(
                        out=hb[:, p2, :], in0=hs[:, i, :], scalar=st[:, 56 + ti:57 + ti], in1=tab[:, 0, :],
                        op0=ALU.mult, op1=ALU.mult), rd=[B_hs[i], B_pre[ti], B_tab[0]], wr=[B_hb[p2]])

                def pre_T(k3, ti):
                    p2 = ti % 2
                    tpv = psbf(p2)
                    S.op("pe", [lambda e, k=k: e.transpose(
                        out=tpv[:, k, :], in_=hb[:, p2, k * 128:(k + 1) * 128], identity=ident) for k in range(8)],
                        rd=[B_hb[p2], B_bands], wr=[B_ps[p2]])
                    S.op("act", lambda e: e.activation(
                        out=hT[:, :, ti * 128:(ti + 1) * 128], in_=tpv, func=AF.Copy), rd=[B_ps[p2]], wr=[B_hT])

                def prenorm_tile(k3, ti):
                    pre_stats(k3, ti)
                    pre_hb(k3, ti)
                    pre_T(k3, ti)

                n0 = len(thirds[0])
                for ti in range(n0):
                    pre_stats(0, ti)
                pre_hb(0, 0)
                for ti in range(n0):
                    if ti + 1 < n0:
                        pre_hb(0, ti + 1)
                    pre_T(0, ti)
                for k3, tl in enumerate(thirds):
                    nt = len(tl)
                    ntok = nt * 128
                    groups = [(0, 384), (384, ntok - 384)]
                    for c in range(NFC):
                        slot = ring_next()
                        W = ring[:, slot, :].rearrange("p (k f) -> p k f", k=8)
                        for (g0, N) in groups:
                            gb = 2 + 2 * (it % 2)
                            ub = gb + 1
                            S.op("pe", [lambda e, k=k, W=W, g0=g0, N=N, gb=gb: e.matmul(
                                ps[:, gb, 0:N], lhsT=W[:, k, 0:128], rhs=hT[:, k, g0:g0 + N], start=(k == 0), stop=(k == 7))
                                for k in range(8)], rd=[B_ring[slot], B_hT], wr=[B_ps[gb]])
                            S.op("pe", [lambda e, k=k, W=W, g0=g0, N=N, ub=ub: e.matmul(
                                ps[:, ub, 0:N], lhsT=W[:, k, 128:256], rhs=hT[:, k, g0:g0 + N], start=(k == 0), stop=(k == 7))
                                for k in range(8)], rd=[B_ring[slot], B_hT], wr=[B_ps[ub]])
                            s2 = it % 2
                            S.op("act", lambda e, s2=s2, gb=gb, N=N: e.activation(
                                out=sg[:, s2, 0:N], in_=ps[:, gb, 0:N], func=AF.Silu), rd=[B_ps[gb]], wr=[B_sg[s2]])
                            S.op("dve", lambda e, s2=s2, ub=ub, N=N, c=c, g0=g0: e.tensor_tensor(
                                out=actT[:, c, g0:g0 + N], in0=ps[:, ub, 0:N], in1=sg[:, s2, 0:N], op=ALU.mult),
                                rd=[B_ps[ub], B_sg[s2]], wr=[B_actT])
                            it += 1
                        ring_issue()
                    for ti, i in enumerate(tl):
                        yb = 2 + 2 * (ydb % 3)
                        ydb += 1
                        fns = []
                        for c in range(NFC):
                            for half in range(2):
                                fns.append(lambda e, c=c, half=half, ti=ti, yb=yb: e.matmul(
                                    ps[:, yb + half, :], lhsT=actT[:, c, ti * 128:(ti + 1) * 128],
                                    rhs=wd[:, c, half * 512:(half + 1) * 512], start=(c == 0), stop=(c == NFC - 1)))
                        S.op("pe", fns, rd=[B_actT, B_wd], wr=[B_ps[yb], B_ps[yb + 1]])
                        if k3 + 1 < 3 and ti < len(thirds[k3 + 1]):
                            prenorm_tile(k3 + 1, ti)
                        yv = ps[:, yb:yb + 2, :]
                        S.op("act", lambda e, yv=yv: e.activation(
                            out=junk[:].rearrange("p (a b) -> p a b", a=2), in_=yv, func=AF.Square, accum_out=st[:, 20:21]),
                            rd=[B_ps[yb], B_ps[yb + 1]], wr=[B_junk, B_st])
                        rstd_batch(1, 20, 21, 22)
                        S.op("dve", lambda e, yv=yv: e.scalar_tensor_tensor(
                            out=t2.rearrange("p (a b) -> p a b", a=2), in0=yv, scalar=st[:, 22:23],
                            in1=tab[:, 1, :].rearrange("p (a b) -> p a b", a=2), op0=ALU.mult, op1=ALU.mult),
                            rd=[B_ps[yb], B_ps[yb + 1], B_st, B_tab[1]], wr=[B_t2])
                        S.op("dve", lambda e, i=i: e.tensor_tensor(out=hs[:, i, :], in0=hs[:, i, :], in1=t2, op=ALU.add),
                             rd=[B_t2], wr=[B_hs[i]])
                        if store:
                            S.dma("sp", "d_out", [(outd[(i - 1) * 128:i * 128, :], hs[:, i, :])], rd=[B_hs[i]], wr=[B_out])

            if not os.environ.get('SKIPFFN'):
                ffn(0, list(range(NHS)), False)
            checkpoint(2)

            bt = barrier()
            cv = Carver()
            KT = cv.take(BF16, [128, 2, 2, NHS * 128])
            VA = cv.take(BF16, [128, NHS, 2, 65])
            kv_end = cv.off
            wk = cv.take(BF16, [128, 8, 256])
            wv = cv.take(BF16, [128, 8, 128])
            hT = cv.take(BF16, [128, 8, 512])
            hb = cv.take(BF16, [128, 2, D])
            B_KT, B_VA, B_wkv, B_hT = Buf("KT"), Buf("VA"), Buf("wkv"), Buf("hT")
            B_hb = [Buf("hb0"), Buf("hb1")]
            S.dma("pool", "d_wkv", [(wk.rearrange("p a b -> p (a b)"), wst[CH_WK]),
                                    (wv.rearrange("p a b -> p (a b)"), wst[CH_WV][:, 0:1024])], wr=[B_wkv], extra=bt)
            load_tab(0, T_KV, extra=bt)
            load_tab(1, T_BV, extra=bt)
            KVD = int(os.environ.get('KVDBG', '9'))
            if KVD >= 1:
                S.op("dve", lambda e: e.memset(VA[:, :, :, 64:65], 1.0), wr=[B_VA], extra=bt)
            S.op("dve", lambda e: e.memset(KT.rearrange("p a b c -> p (a b c)"), 0.0), wr=[B_KT], extra=bt)
            B_kvs = [Buf("kvs%d" % g_) for g_ in range(5)]

            def kv_stats(g0t_):
                tl_ = list(range(g0t_, min(g0t_ + 4, NHS)))
                bs_ = B_kvs[g0t_ // 4]
                for i_ in tl_:
                    sumsq(hs[:, i_, :], i_, [B_hs[i_]], bst=bs_)
                rstd_batch(len(tl_), g0t_, 32 + g0t_, 64 + g0t_, bst=bs_)

            kv_stats(0)
            vcnt = 0
            for g0t in range(0, NHS, 4):
                tl = list(range(g0t, min(g0t + 4, NHS)))
                N = len(tl) * 128
                for ti, i in enumerate(tl):
                    p2 = ti % 2
                    S.op("dve", lambda e, i=i, p2=p2: e.scalar_tensor_tensor(
                        out=hb[:, p2, :], in0=hs[:, i, :], scalar=st[:, 64 + i:65 + i], in1=tab[:, 0, :],
                        op0=ALU.mult, op1=ALU.mult), rd=[B_hs[i], B_kvs[g0t // 4], B_tab[0]], wr=[B_hb[p2]])
                    tpv = psbf(p2)
                    S.op("pe", [lambda e, k=k, p2=p2, tpv=tpv: e.transpose(
                        out=tpv[:, k, :], in_=hb[:, p2, k * 128:(k + 1) * 128], identity=ident) for k in range(8)],
                        rd=[B_hb[p2], B_bands], wr=[B_ps[p2]])
                    S.op("act", lambda e, ti=ti, tpv=tpv: e.activation(
                        out=hT[:, :, ti * 128:(ti + 1) * 128], in_=tpv, func=AF.Copy), rd=[B_ps[p2]], wr=[B_hT])
                if g0t + 4 < NHS:
                    kv_stats(g0t + 4)
                for kvh in range(2 if KVD >= 3 else 0):
                    kb = 2 + kvh
                    S.op("pe", [lambda e, k=k, kvh=kvh, kb=kb, N=N: e.matmul(
                        ps[:, kb, 0:N], lhsT=wk[:, k, kvh * 128:(kvh + 1) * 128], rhs=hT[:, k, 0:N],
                        start=(k == 0), stop=(k == 7)) for k in range(8)], rd=[B_wkv, B_hT], wr=[B_ps[kb]])
                    for e_ in range(2):
                        S.op("act", lambda e, kvh=kvh, kb=kb, N=N, g0t=g0t, e_=e_: e.activation(
                            out=KT[64 * e_:64 * e_ + 64, kvh, e_, g0t * 128:g0t * 128 + N],
                            in_=ps[64 * e_:64 * e_ + 64, kb, 0:N], func=AF.Identity,
                            bias=cst[64 * e_:64 * e_ + 64, 8 + kvh:9 + kvh]), rd=[B_ps[kb], B_cst], wr=[B_KT])
                for ti, i in enumerate(tl if KVD >= 4 else []):
                    vb = 4 + (vcnt % 2)
                    vcnt += 1
                    S.op("pe", [lambda e, k=k, ti=ti, vb=vb: e.matmul(
                        ps[:, vb, 0:128], lhsT=hT[:, k, ti * 128:(ti + 1) * 128], rhs=wv[:, k, :],
                        start=(k == 0), stop=(k == 7)) for k in range(8)], rd=[B_wkv, B_hT], wr=[B_ps[vb]])
                    S.op("dve", lambda e, i=i, vb=vb: e.tensor_tensor(
                        out=VA[:, i, :, 0:64], in0=ps[:, vb, 0:128].rearrange("p (a b) -> p a b", a=2),
                        in1=tab[:, 1, 0:128].rearrange("p (a b) -> p a b", a=2), op=ALU.add),
                        rd=[B_ps[vb], B_tab[1]], wr=[B_VA])

            checkpoint(3)
            bt = barrier()
            cv = Carver()
            cv.off = kv_end
            wo = cv.take(BF16, [128, 8, D])
            bias = cv.take(F32, [128, 16, 2, 128])
            PT = cv.take(BF16, [128, 16, 2, 128])
            lg = cv.take(F32, [128, 2, 512])
            hT4 = cv.take(BF16, [128, 8, 512])
            QT4 = cv.take(BF16, [128, 8, 512])
            oT4 = cv.take(BF16, [128, 4, 8, 128])
            hb = cv.take(BF16, [128, 2, D])
            obf = cv.take(BF16, [128, D])
            t2 = cv.take(F32, [128, D])
            bohl = cv.take(BF16, [128, 2, D])
            ones = cv.take(BF16, [128, 128])
            B_wo, B_bias, B_PTx, B_hT4, B_QT4, B_obf, B_t2, B_bohl = (Buf(n) for n in
                                                                      ("wo", "bias", "PT", "hT4", "QT4", "obf", "t2", "bohl"))
            B_PT = [Buf("PT%d" % h_) for h_ in range(8)]
            B_oT4 = [Buf("oT4_%d" % t) for t in range(4)]
            B_lg = [Buf("lg0"), Buf("lg1")]
            B_hb = [Buf("hb0"), Buf("hb1")]
            B_esink, B_stb = Buf("esink"), Buf("stb")
            B_pn = [Buf("pn0"), Buf("pn1")]
            B_den = [Buf("den0"), Buf("den1")]
            S.dma("pool", "d_wqo", [(wo[:, 2 * j:2 * j + 2, :].rearrange("p a b -> p (a b)"), wst[CH_WO + j]) for j in range(4)],
                  wr=[B_wo], extra=bt)
            S.dma("sp", "d_bias", [(bias.rearrange("p a b c -> p (a b c)"), btabd[0])], wr=[B_bias], extra=bt)
            load_tab(0, T_MIXPRE1, extra=bt)
            load_tab(1, T_MIXPOST1, extra=bt)
            load_tab(2, T_BO, extra=bt)
            S.op("dve", lambda e: e.memset(ones, 1.0), wr=[B_bohl], extra=bt)
            S.op("dve", lambda e: e.tensor_copy(out=bohl[:, 0, :], in_=tab[:, 2, :]), rd=[B_tab[2]], wr=[B_bohl])
            S.op("dve", lambda e: e.tensor_tensor(out=t2, in0=tab[:, 2, :], in1=bohl[:, 0, :], op=ALU.subtract),
                 rd=[B_tab[2], B_bohl], wr=[B_t2])
            S.op("dve", lambda e: e.tensor_copy(out=bohl[:, 1, :], in_=t2), rd=[B_t2], wr=[B_bohl])
            S.op("act", lambda e: e.activation(out=st[:, 96:112], in_=cst[:, 10:26], func=AF.Exp),
                 rd=[B_cst], wr=[B_esink], extra=bt)
            hsplit = [(0, 7), (7, 14), (14, 16)]

            def q_stage(g):
                for ti in range(4):
                    i = 1 + 4 * g + ti
                    p2 = ti % 2
                    tb = 6 + p2
                    S.op("dve", lambda e, i=i, p2=p2: e.scalar_tensor_tensor(
                        out=hb[:, p2, :], in0=hs[:, i, :], scalar=st[:, 64 + i:65 + i], in1=tab[:, 0, :],
                        op0=ALU.mult, op1=ALU.mult), rd=[B_hs[i], B_kvs[i // 4], B_tab[0]], wr=[B_hb[p2]])
                    tpv = psbf(tb)
                    S.op("pe", [lambda e, k=k, p2=p2, tpv=tpv: e.transpose(
                        out=tpv[:, k, :], in_=hb[:, p2, k * 128:(k + 1) * 128], identity=ident) for k in range(8)],
                        rd=[B_hb[p2], B_bands], wr=[B_ps[tb]])
                    S.op("act", lambda e, ti=ti, tpv=tpv: e.activation(
                        out=hT4[:, :, ti * 128:(ti + 1) * 128], in_=tpv, func=AF.Copy), rd=[B_ps[tb]], wr=[B_hT4])
                for j in range(4):
                    slot = ring_next()
                    W = ring[:, slot, :].rearrange("p (k f) -> p k f", k=8)
                    for e2 in range(2):
                        hp = 2 * j + e2
                        qb = 6 + hp % 2
                        S.op("pe", [lambda e, k=k, W=W, e2=e2, qb=qb: e.matmul(
                            ps[:, qb, :], lhsT=W[:, k, e2 * 128:(e2 + 1) * 128], rhs=hT4[:, k, :],
                            start=(k == 0), stop=(k == 7)) for k in range(8)], rd=[B_ring[slot], B_hT4], wr=[B_ps[qb]])
                        if hp % 2 == 0:
                            S.op("act", lambda e, hp=hp, qb=qb: e.activation(
                                out=QT4[:, hp, :], in_=ps[:, qb, :], func=AF.Identity, bias=cst[:, hp:hp + 1]),
                                rd=[B_ps[qb], B_cst], wr=[B_QT4])
                        else:
                            S.op("dve", lambda e, hp=hp, qb=qb: e.tensor_scalar(
                                out=QT4[:, hp, :], in0=ps[:, qb, :], scalar1=cst[:, hp:hp + 1], scalar2=None, op0=ALU.add),
                                rd=[B_ps[qb], B_cst], wr=[B_QT4])
                    ring_issue()

            def s_tile(g, ti, hook=None):
                i = 1 + 4 * g + ti
                p2 = i % 2
                den = st[:, 112 + 16 * p2:128 + 16 * p2]

                def scores(hp):
                    kvh = hp // 4
                    sbk = 1 + hp % 2
                    sv = ps[:, sbk, :].rearrange("p (e b q) -> p e b q", e=2, b=2)
                    fns = []
                    for e_ in range(2):
                        for blk in range(2):
                            kt = i - 1 + blk
                            fns.append(lambda e, e_=e_, blk=blk, kt=kt: e.matmul(
                                sv[:, e_, blk, :], lhsT=KT[:, kvh, e_, kt * 128:(kt + 1) * 128],
                                rhs=QT4[:, hp, ti * 128:(ti + 1) * 128], start=True, stop=True))
                    S.op("pe", fns, rd=[B_KT, B_QT4], wr=[B_ps[sbk]])

                def c1_bank(bi):
                    h0, h1 = hsplit[bi]
                    nh = h1 - h0
                    ov = ps[:, 3 + bi, 0:nh * 65].rearrange("p (h c) -> p h c", c=65)
                    S.op("dve", lambda e: e.tensor_tensor(
                        out=den[:, h0:h1], in0=ov[:, :, 64], in1=st[:, 96 + h0:96 + h1], op=ALU.add),
                        rd=[B_ps[3 + bi], B_esink], wr=[B_den[p2]])
                    S.op("dve", lambda e: e.reciprocal(out=den[:, h0:h1], in_=den[:, h0:h1]), rd=[], wr=[B_den[p2]])
                    S.op("dve", lambda e: e.tensor_tensor(
                        out=obf.rearrange("p (h c) -> p h c", c=64)[:, h0:h1, :], in0=ov[:, :, 0:64],
                        in1=den[:, h0:h1].unsqueeze(2).to_broadcast([128, nh, 64]), op=ALU.mult),
                        rd=[B_ps[3 + bi], B_den[p2]], wr=[B_obf])

                scores(0)
                scores(1)
                for hp in range(8):
                    kvh = hp // 4
                    sbk = 1 + hp % 2
                    l2 = hp % 2
                    S.op("dve", lambda e, hp=hp, sbk=sbk, l2=l2: e.scalar_tensor_tensor(
                        out=lg[:, l2, :], in0=ps[:, sbk, :], scalar=0.125,
                        in1=bias[:, 2 * hp:2 * hp + 2, :, :].rearrange("p a b c -> p (a b c)"), op0=ALU.mult, op1=ALU.add),
                        rd=[B_ps[sbk], B_bias], wr=[B_lg[l2]])
                    S.op("act", lambda e, hp=hp, l2=l2: e.activation(
                        out=PT[:, 2 * hp:2 * hp + 2, :, :].rearrange("p a b c -> p (a b c)"), in_=lg[:, l2, :], func=AF.Exp),
                        rd=[B_lg[l2]], wr=[B_PT[hp]])
                    if hp + 2 < 8:
                        scores(hp + 2)
                    fns = []
                    for e_ in range(2):
                        h = 2 * hp + e_
                        ob = 3 + h // 7
                        oc = (h % 7) * 65
                        for blk in range(2):
                            kt = i - 1 + blk
                            fns.append(lambda e, h=h, ob=ob, oc=oc, blk=blk, kt=kt, kvh=kvh: e.matmul(
                                ps[:, ob, oc:oc + 65], lhsT=PT[:, h, blk, :], rhs=VA[:, kt, kvh, :],
                                start=(blk == 0), stop=(blk == 1)))
                    S.op("pe", fns, rd=[B_PT[hp], B_VA],
                         wr=[B_ps[b_] for b_ in sorted({3 + (2 * hp) // 7, 3 + (2 * hp + 1) // 7})])
                    if hp == 1 and hook is not None:
                        hook()
                    if hp == 4:
                        c1_bank(0)
                    if hp == 7:
                        c1_bank(1)
                if i == 1:
                    S.dma("sp", "d_bias", [(bias.rearrange("p a b c -> p (a b c)"), btabd[1])], wr=[B_bias])

                def tail():
                    c1_bank(2)
                    tpv = psbf(0)
                    S.op("pe", [lambda e, k=k: e.transpose(
                        out=tpv[:, k, :], in_=obf[:, k * 128:(k + 1) * 128], identity=ident) for k in range(8)],
                        rd=[B_obf, B_bands], wr=[B_ps[0]])
                    S.op("act", lambda e: e.activation(out=oT4[:, ti], in_=tpv, func=AF.Copy), rd=[B_ps[0]], wr=[B_oT4[ti]])
                return tail

            def o_tile(g, ti, ybase):
                i = 1 + 4 * g + ti
                p2 = ti % 2
                c0 = 20 + 4 * p2
                for half in range(2):
                    fns = [lambda e, k=k, half=half: e.matmul(
                        ps[:, ybase + half, :], lhsT=oT4[:, ti, k, :], rhs=wo[:, k, half * 512:(half + 1) * 512],
                        start=(k == 0), stop=False) for k in range(8)]
                    for hl in range(2):
                        fns.append(lambda e, hl=hl, half=half: e.matmul(
                            ps[:, ybase + half, :], lhsT=ones[0:1, :], rhs=bohl[0:1, hl, half * 512:(half + 1) * 512],
                            start=False, stop=(hl == 1)))
                    S.op("pe", fns, rd=[B_oT4[ti], B_wo, B_bohl], wr=[B_ps[ybase + half]])
                yv = ps[:, ybase:ybase + 2, :]
                S.op("act", lambda e: e.activation(
                    out=junk[:].rearrange("p (a b) -> p a b", a=2), in_=yv, func=AF.Square,
                    accum_out=st[:, c0:c0 + 1]), rd=[B_ps[ybase], B_ps[ybase + 1]], wr=[B_junk, B_pn[p2]])
                rstd_batch(1, c0, c0 + 1, c0 + 2, bst=B_pn[p2])
                S.op("dve", lambda e: e.scalar_tensor_tensor(
                    out=t2.rearrange("p (a b) -> p a b", a=2), in0=yv, scalar=st[:, c0 + 2:c0 + 3],
                    in1=tab[:, 1, :].rearrange("p (a b) -> p a b", a=2), op0=ALU.mult, op1=ALU.mult),
                    rd=[B_ps[ybase], B_ps[ybase + 1], B_pn[p2], B_tab[1]], wr=[B_t2])
                S.op("pool", lambda e: e.tensor_tensor(out=hs[:, i, :], in0=hs[:, i, :], in1=t2, op=ALU.add),
                     rd=[B_t2], wr=[B_hs[i]])

            q_stage(0)
            for g in range(4):
                tail = None
                for ti in range(4):
                    tail = s_tile(g, ti, hook=tail)
                tail()
                if g + 1 < 4:
                    q_stage(g + 1)
                for t_ in range(4):
                    o_tile(g, t_, 6 if t_ % 2 == 0 else 1)

            checkpoint(4)
            ffn(1, list(range(1, NHS)), True)


        except _Stop:
            pass
        S.wait("sp", B_out.w)

        with nc.Block() as block:
            @block.tensor
            def _(e):
                for f in S.ops["pe"]:
                    f(e)

            @block.scalar
            def _(e):
                for f in S.ops["act"]:
                    f(e)

            @block.vector
            def _(e):
                for f in S.ops["dve"]:
                    f(e)

            @block.gpsimd
            def _(e):
                for f in S.ops["pool"]:
                    f(e)

            @block.sync
            def _(e):
                for f in S.ops["sp"]:
                    f(e)
    return nc


def _t5_bucket(d):
    max_exact = 16
    df = np.maximum(d, 1).astype(np.float32)
    large = max_exact + (np.log(df / max_exact) / np.log(128 / max_exact) * (32 - max_exact)).astype(np.int32)
    large = np.minimum(large, 31)
    return np.where(d < max_exact, d, large)


def _band_consts(first_core):
    out = np.zeros((128, 13, 128), np.float32)
    tp = np.arange(128)[:, None]
    t = np.arange(128)[None, :]
    for wi, w in enumerate((2, 4, 8, 16)):
        dlt = t - tp
        own = np.where((dlt >= 0) & (dlt < w), 1.0 / w, 0.0) - (dlt == 0)
        prev = np.where((t + 128 - tp) < w, 1.0 / w, 0.0)
        out[:, wi, :] = own
        out[:, 4 + wi, :] = prev
        if first_core:
            pos = np.maximum(t - 112 + 1, 1)
            cnt = np.minimum(pos, w).astype(np.float32)
            valid = (dlt >= 0) & (dlt < w) & (tp >= 112)
            out[:, 8 + wi, :] = np.where(valid, 1.0 / cnt, 0.0) - (dlt == 0)
        else:
            out[:, 8 + wi, :] = own
    out[:, 12, :] = np.eye(128, dtype=np.float32)
    return out.reshape(128, 13 * 128)


def _bias_tables(rel_bias, first_core):
    s = np.arange(128)[:, None]
    q = np.arange(128)[None, :]
    tabs = np.empty((2, 128, 16, 2, 128), np.float32)
    for blk in range(2):
        d = q - s + (128 if blk == 0 else 0)
        ok = (d >= 0) & (d < 128)
        bidx = _t5_bucket(np.maximum(d, 0))
        g = rel_bias[bidx]
        g = np.where(ok[:, :, None], g, np.float32(NEG))
        g = np.transpose(g, (0, 2, 1))
        tabs[1, :, :, blk, :] = g
        if blk == 0 and first_core:
            g = np.where((s >= 112)[:, :, None] & np.ones((1, 1, 128), bool), g, np.float32(NEG))
        tabs[0, :, :, blk, :] = g
    return tabs.reshape(2, 128, 16 * 2 * 128)


def _prep_common(inp):
    f = np.float32
    wst = np.zeros((NCHUNK, 128, 2048), f)
    pw = inp["pool_w"][0]
    wst[CH_POOL] = pw.reshape(4, 2, 128, 256).transpose(2, 0, 1, 3).reshape(128, 2048)
    for l in range(2):
        gu = inp["w_gate_up"][l]
        gate = gu[:, :DFF].reshape(8, 128, NFC, 128)
        up = gu[:, DFF:].reshape(8, 128, NFC, 128)
        blk = np.stack([gate, up], axis=3)
        blk = blk.transpose(2, 1, 0, 3, 4).reshape(NFC, 128, 2048)
        base = CH_GU0 if l == 0 else CH_GU1
        wst[base:base + NFC] = blk
        wdn = inp["w_down"][l].reshape(11, 2, 128, D).transpose(0, 2, 1, 3).reshape(11, 128, 2048)
        base = CH_WD0 if l == 0 else CH_WD1
        wst[base:base + 11] = wdn
    wk = inp["w_k"].reshape(8, 128, 2, 64)
    wkd = np.stack([wk, wk], axis=3)
    wst[CH_WK] = wkd.transpose(1, 0, 2, 3, 4).reshape(128, 2048)
    wst[CH_WV][:, :1024] = inp["w_v"].reshape(8, 128, 128).transpose(1, 0, 2).reshape(128, 1024)
    wq = inp["w_q"][0].reshape(8, 128, 4, 256)
    wst[CH_WQ:CH_WQ + 4] = wq.transpose(2, 1, 0, 3).reshape(4, 128, 2048)
    wo = inp["w_o"][0].reshape(4, 2, 128, D)
    wst[CH_WO:CH_WO + 4] = wo.transpose(0, 2, 1, 3).reshape(4, 128, 2048)

    rows = [inp["norm_mix_pre"][0], inp["pool_scale"][0], inp["norm_mix_post"][0], inp["norm_ffn_pre"][0],
            inp["norm_ffn_post"][0], inp["kv_norm"], inp["norm_mix_pre"][1], inp["norm_mix_post"][1],
            inp["b_o"][0], inp["norm_ffn_pre"][1], inp["norm_ffn_post"][1],
            np.concatenate([inp["b_v"], np.zeros(D - 128, f)])]
    tabs = np.ascontiguousarray(np.broadcast_to(np.stack(rows)[:, None, :], (NTAB, 128, D))).astype(f)

    cst = np.zeros((128, NCST), f)
    cst[:, 0:8] = inp["b_q"][0].reshape(8, 128).T
    bk = inp["b_k"].reshape(2, 64)
    cst[:, 8:10] = np.concatenate([bk, bk], axis=1).T
    cst[:, 10:26] = np.broadcast_to(inp["sinks"][0][None, :], (128, 16))
    cst[:, 26] = EPS
    return wst, tabs, cst


_NC_CACHE = {}


def kernel(**inputs):
    inp = {k: np.asarray(v, dtype=np.float32) for k, v in inputs.items()}
    x = inp["x"]
    meta = inp["meta_tokens"]
    wst, tabs, cst = _prep_common(inp)
    bands = [_band_consts(True), _band_consts(False)]
    btab = [_bias_tables(inp["rel_bias"], True), _bias_tables(inp["rel_bias"], False)]
    in_maps = []
    for c in range(8):
        b, r = c // 4, c % 4
        xin = np.zeros((NXT * 128, D), np.float32)
        if r == 0:
            xin[128 + 112:256] = meta
            xin[256:] = x[b, 0:2048]
        else:
            xin[:] = x[b, 2048 * r - 256:2048 * r + 2048]
        fc = (r == 0)
        in_maps.append({"xin": xin, "wst": wst, "tabs": tabs, "cst": cst,
                        "bands": bands[0 if fc else 1], "btab": btab[0 if fc else 1]})
    if "nc" not in _NC_CACHE:
        _NC_CACHE["nc"] = build_program()
    nc = _NC_CACHE["nc"]
    res = run_bass_kernel_spmd(nc, in_maps, core_ids=list(range(8)))
    out = np.empty((2, 8192, D), np.float32)
    for c in range(8):
        b, r = c // 4, c % 4
        out[b, 2048 * r:2048 * r + 2048] = res.results[c]["out"]
    return out
```
